# Optimizing a Trainium2 kernel written in Bass

```python
import functools
import jax, jax.numpy as jnp
from jax import lax
import numpy as np

D_MODEL = 1024
BATCH = 8
SEQ = 2048
DEPTH = 2
DEC_BATCH = 128
DEC_SEQ = 1
PAST_LEN = 2048
PAGE_SIZE = 128

BRANCH_WIDTH = D_MODEL // 2
N_BRANCH = 3
GLA_HEADS = 4
GLA_DV = BRANCH_WIDTH // GLA_HEADS
GLA_DK = GLA_DV // 2
GLA_GATE_RANK = 16
GLA_GATE_NORMALIZER = 16.0
GLA_CHUNK = 64
GLA_NORM_EPS = 1e-5
MOBA_HEAD_DIM = 64
MOBA_HEADS = BRANCH_WIDTH // MOBA_HEAD_DIM
MOBA_BLOCK = 256
MOBA_TOPK = 3
MOBA_QBLOCK = 128
ROPE_THETA = 10000.0
RWKV_HEAD = 64
RWKV_HEADS = BRANCH_WIDTH // RWKV_HEAD
RWKV_DECAY_RANK = 64
RWKV_A_RANK = 64
RWKV_GATE_RANK = 128
RWKV_DECAY_SCALE = 0.606531
RWKV_GN_EPS = 64e-5
RWKV_PROJ = 3 * BRANCH_WIDTH + RWKV_DECAY_RANK + RWKV_A_RANK + RWKV_GATE_RANK
GLA_COLS = (GLA_HEADS * GLA_DK, GLA_HEADS * GLA_DK, BRANCH_WIDTH, GLA_GATE_RANK, BRANCH_WIDTH)
MOBA_COLS = (BRANCH_WIDTH, BRANCH_WIDTH, BRANCH_WIDTH)
IN_SPLITS = GLA_COLS + MOBA_COLS + (RWKV_PROJ, N_BRANCH * D_MODEL)
IN_PROJ = sum(IN_SPLITS)
RWKV_SPLITS = (BRANCH_WIDTH, BRANCH_WIDTH, BRANCH_WIDTH, RWKV_DECAY_RANK, RWKV_A_RANK, RWKV_GATE_RANK)
D_FF = 4 * D_MODEL
ALPHA = (2 * DEPTH) ** 0.25
BETA = (8 * DEPTH) ** -0.25
LN_EPS = 1e-5

kernel_name = 'hybrid_gla_moba_rwkv7_deepnorm_step'


def split_cols(t, sizes):
    out, o = [], 0
    for s in sizes:
        out.append(t[..., o:o + s])
        o += s
    return out


def layer_norm(x, g, b):
    xf = x.astype(jnp.float32)
    mu = xf.mean(-1, keepdims=True)
    var = jnp.square(xf - mu).mean(-1, keepdims=True)
    return ((xf - mu) * lax.rsqrt(var + LN_EPS) * g + b).astype(x.dtype)


def rope(x, pos):
    half = x.shape[-1] // 2
    inv = ROPE_THETA ** (-jnp.arange(half, dtype=jnp.float32) / half)
    ang = pos.astype(jnp.float32)[:, None] * inv[None, :]
    cos = jnp.cos(ang)[:, None, :]
    sin = jnp.sin(ang)[:, None, :]
    xf = x.astype(jnp.float32)
    x1, x2 = xf[..., :half], xf[..., half:]
    return jnp.concatenate([x1 * cos - x2 * sin, x1 * sin + x2 * cos], -1).astype(x.dtype)


def gla_chunked(q, k, v, log_a, S0):
    f32 = jnp.float32
    B, L, H, DK = q.shape
    DV = v.shape[-1]
    C = GLA_CHUNK
    n = -(-L // C)
    padw = ((0, 0), (0, n * C - L), (0, 0), (0, 0))

    def chunks(t):
        t = jnp.pad(t.astype(f32), padw)
        return t.reshape(B, n, C, H, t.shape[-1]).transpose(1, 0, 3, 2, 4)

    qc, kc, vc, gc = chunks(q), chunks(k), chunks(v), chunks(log_a)
    G = jnp.cumsum(gc, axis=3)
    G_last = G[:, :, :, -1:, :]
    q_dec = qc * jnp.exp(G)
    k_inv = kc * jnp.exp(-G)
    k_tail = kc * jnp.exp(G_last - G)
    causal = jnp.tril(jnp.ones((C, C), dtype=bool))
    A = jnp.where(causal, jnp.einsum('nbhid,nbhjd->nbhij', q_dec, k_inv), 0.0)
    o_intra = jnp.einsum('nbhij,nbhjv->nbhiv', A, vc)

    def step(S, inp):
        q_d, k_t, v_c, g_l = inp
        o_inter = jnp.einsum('bhid,bhdv->bhiv', q_d, S)
        S = S * jnp.exp(g_l)[:, :, 0, :, None] + jnp.einsum('bhjd,bhjv->bhdv', k_t, v_c)
        return S, o_inter

    S_fin, o_inter = lax.scan(step, S0.astype(f32), (q_dec, k_tail, vc, G_last))
    o = (o_intra + o_inter).transpose(1, 0, 3, 2, 4).reshape(B, n * C, H, DV)[:, :L]
    return o, S_fin.astype(S0.dtype)


def rwkv7_scan(r, w, k, v, kk, a, S0):
    f32 = jnp.float32
    xs = tuple(jnp.moveaxis(t.astype(f32), 1, 0) for t in (r, w, k, v, kk, a))

    def step(S, inp):
        r_t, w_t, k_t, v_t, kk_t, a_t = inp
        sa = jnp.einsum('bhij,bhj->bhi', S, -kk_t)
        S = (S * w_t[:, :, None, :] + sa[..., None] * (kk_t * a_t)[:, :, None, :]
             + v_t[..., None] * k_t[:, :, None, :])
        return S, jnp.einsum('bhij,bhj->bhi', S, r_t)

    S, y = lax.scan(step, S0.astype(f32), xs)
    return jnp.moveaxis(y, 0, 1), S.astype(S0.dtype)


def moba_attend(q, q_pos, k_blocks, v_blocks, k_mean):
    f32 = jnp.float32
    n_blocks, blk, H, dh = k_blocks.shape
    Lq = q.shape[0]
    n_sel = min(MOBA_TOPK, n_blocks)
    own = q_pos // blk
    qf = q.astype(f32)
    gate = jnp.einsum('qhd,nhd->qhn', qf, k_mean)
    fully_past = jnp.arange(n_blocks)[None, :] < own[:, None]
    gate = jnp.where(fully_past[:, None, :], gate, -jnp.inf)
    top_val, top_idx = lax.top_k(gate, n_sel)
    own_idx = jnp.broadcast_to(own[:, None, None], (Lq, H, 1)).astype(jnp.int32)
    blk_idx = jnp.concatenate([top_idx.astype(jnp.int32), own_idx], -1)
    blk_ok = jnp.concatenate([jnp.isfinite(top_val), jnp.ones((Lq, H, 1), dtype=bool)], -1)
    head = jnp.arange(H)[None, :, None]
    kg = jnp.moveaxis(k_blocks, 2, 0)[head, blk_idx]
    vg = jnp.moveaxis(v_blocks, 2, 0)[head, blk_idx]
    s = jnp.einsum('qhd,qhsbd->qhsb', qf, kg.astype(f32)) * (dh ** -0.5)
    key_pos = blk_idx[..., None] * blk + jnp.arange(blk)
    ok = blk_ok[..., None] & (key_pos <= q_pos[:, None, None, None])
    s = jnp.where(ok, s, -jnp.inf)
    p = jax.nn.softmax(s.reshape(Lq, H, -1), axis=-1).reshape(s.shape)
    out = jnp.einsum('qhsb,qhsbd->qhd', p, vg.astype(f32))
    return out.astype(q.dtype)


def moba_prompt(q, k, v, pos):
    B, L, H, dh = q.shape
    nb = -(-L // MOBA_BLOCK)
    padw = ((0, 0), (0, nb * MOBA_BLOCK - L), (0, 0), (0, 0))
    kb = jnp.pad(k, padw).reshape(B, nb, MOBA_BLOCK, H, dh)
    vb = jnp.pad(v, padw).reshape(B, nb, MOBA_BLOCK, H, dh)
    k_mean = kb.astype(jnp.float32).mean(axis=2)
    nq = L // MOBA_QBLOCK
    qb = q.reshape(B * nq, MOBA_QBLOCK, H, dh)
    pb = jnp.broadcast_to(pos.reshape(1, nq, MOBA_QBLOCK), (B, nq, MOBA_QBLOCK)).reshape(B * nq, MOBA_QBLOCK)
    bi = jnp.repeat(jnp.arange(B), nq)

    def one(args):
        qq, pp, b = args
        return moba_attend(qq, pp, kb[b], vb[b], k_mean[b])

    return lax.map(one, (qb, pb, bi)).reshape(B, L, H, dh)


def moba_sample(q, k, v, pos, cache_k_l, cache_v_l, page_table):
    Bd, Ld, H, dh = q.shape
    past = page_table.shape[1] * cache_k_l.shape[1]
    total = past + Ld
    nb = -(-total // MOBA_BLOCK)
    padw = ((0, nb * MOBA_BLOCK - total), (0, 0), (0, 0))

    def one(args):
        pt, qq, kn, vn = args
        kf = jnp.concatenate([cache_k_l[pt].reshape(past, H, dh), kn], 0)
        vf = jnp.concatenate([cache_v_l[pt].reshape(past, H, dh), vn], 0)
        kf = jnp.pad(kf, padw).reshape(nb, MOBA_BLOCK, H, dh)
        vf = jnp.pad(vf, padw).reshape(nb, MOBA_BLOCK, H, dh)
        return moba_attend(qq, pos, kf, vf, kf.astype(jnp.float32).mean(axis=1))

    return lax.map(one, (page_table, q, k, v))


def mixer(x, pos, p, gla_S0, wkv_S0, shift0, attend):
    f32 = jnp.float32
    B, L, _ = x.shape
    W = BRANCH_WIDTH
    gq, gk, gv, g_low, g_out, mq, mk, mv, pr, pg = split_cols(x @ p['w_in'], IN_SPLITS)

    q = (gq * GLA_DK ** -0.5).reshape(B, L, GLA_HEADS, GLA_DK)
    k = gk.reshape(B, L, GLA_HEADS, GLA_DK)
    v = gv.reshape(B, L, GLA_HEADS, GLA_DV)
    log_a = jax.nn.log_sigmoid((g_low @ p['gla_gk_up'] + p['gla_gk_bias']).astype(f32)) / GLA_GATE_NORMALIZER
    o, gla_S = gla_chunked(q, k, v, log_a.reshape(B, L, GLA_HEADS, GLA_DK), gla_S0)
    o = o * lax.rsqrt(jnp.square(o).mean(-1, keepdims=True) + GLA_NORM_EPS) * p['gla_norm_w']
    y_gla = (o * jax.nn.silu(g_out.astype(f32)).reshape(B, L, GLA_HEADS, GLA_DV)).reshape(B, L, W).astype(x.dtype)

    mq = rope(mq.reshape(B, L, MOBA_HEADS, MOBA_HEAD_DIM), pos)
    mk = rope(mk.reshape(B, L, MOBA_HEADS, MOBA_HEAD_DIM), pos)
    mv = mv.reshape(B, L, MOBA_HEADS, MOBA_HEAD_DIM)
    y_moba = attend(mq, mk, mv).reshape(B, L, W).astype(x.dtype)

    prev = jnp.concatenate([shift0[:, None, :].astype(pr.dtype), pr[:, :-1]], axis=1)
    ps = pr + p['rwkv_mu'] * (prev - pr)
    r, k7, v7, w_low, a_low, gr_low = split_cols(ps, RWKV_SPLITS)
    hs = (B, L, RWKV_HEADS, RWKV_HEAD)
    w = jnp.exp(-RWKV_DECAY_SCALE * jax.nn.sigmoid((p['rwkv_w0'] + jnp.tanh(w_low) @ p['rwkv_w_up']).astype(f32)))
    a = jax.nn.sigmoid((p['rwkv_a0'] + a_low @ p['rwkv_a_up']).astype(f32))
    g7 = jax.nn.sigmoid(gr_low) @ p['rwkv_g_up']
    kk = (k7 * p['rwkv_k_k']).astype(f32).reshape(hs)
    kk = kk * lax.rsqrt(jnp.sum(jnp.square(kk), -1, keepdims=True) + 1e-12)
    k7 = (k7.astype(f32) * (1.0 + (a - 1.0) * p['rwkv_k_a'])).reshape(hs)
    r = r.astype(f32).reshape(hs)
    v7 = v7.astype(f32).reshape(hs)
    yw, wkv_S = rwkv7_scan(r, w.reshape(hs), k7, v7, kk, a.reshape(hs), wkv_S0)
    mu_w = yw.mean(-1, keepdims=True)
    var_w = jnp.square(yw - mu_w).mean(-1, keepdims=True)
    yw = ((yw - mu_w) * lax.rsqrt(var_w + RWKV_GN_EPS)).reshape(B, L, W) * p['rwkv_ln_w'] + p['rwkv_ln_b']
    bonus = jnp.sum(r * k7 * p['rwkv_r_k'], -1, keepdims=True) * v7
    y_rwkv = ((yw + bonus.reshape(B, L, W)) * g7).astype(x.dtype)

    gates = jax.nn.sigmoid((pg + p['b_gate'].reshape(-1)).astype(f32)).reshape(B, L, N_BRANCH, D_MODEL)
    branches = jnp.stack([y_gla, y_moba, y_rwkv], axis=2)
    proj = jnp.einsum('blnw,nwd->blnd', branches, p['w_branch'])
    merged = jnp.einsum('blnd,blnd->bld', gates, proj.astype(f32)).astype(x.dtype)
    out = merged @ p['w_out']
    return out, (mk, mv, gla_S, wkv_S, pr[:, -1, :])


def layer(x, pos, p, gla_S0, wkv_S0, shift0, attend):
    mix, new_state = mixer(x, pos, p, gla_S0, wkv_S0, shift0, attend)
    x = layer_norm(ALPHA * x + mix, p['ln1_g'], p['ln1_b'])
    h = jnp.square(jax.nn.relu(x @ p['w_up'])) @ p['w_down']
    x = layer_norm(ALPHA * x + h, p['ln2_g'], p['ln2_b'])
    return x, new_state


def setup_inputs(seed: int = 0) -> dict:
    key = jax.random.key(seed)
    ks = iter(jax.random.split(key, 40))

    def nrm(shape, scale):
        return jax.random.normal(next(ks), shape, jnp.float32) * scale

    W = BRANCH_WIDTH
    n_pages = PAST_LEN // PAGE_SIZE
    n_used = DEC_BATCH * n_pages
    n_pool = n_used + max(1, n_used // 4)
    page_table = jax.random.permutation(next(ks), n_pool)[:n_used].reshape(DEC_BATCH, n_pages).astype(jnp.int32)
    kv_shape = (DEPTH, n_pool, PAGE_SIZE, MOBA_HEADS, MOBA_HEAD_DIM)
    return {
        'x_prompt': nrm((BATCH, SEQ, D_MODEL), 1.0),
        'x_sample': nrm((DEC_BATCH, DEC_SEQ, D_MODEL), 1.0),
        'cache_k': nrm(kv_shape, 1.0),
        'cache_v': nrm(kv_shape, 1.0),
        'page_table': page_table,
        'state_gla': nrm((DEPTH, DEC_BATCH, GLA_HEADS, GLA_DK, GLA_DV), 0.3),
        'state_wkv': nrm((DEPTH, DEC_BATCH, RWKV_HEADS, RWKV_HEAD, RWKV_HEAD), 0.3),
        'state_shift': nrm((DEPTH, DEC_BATCH, RWKV_PROJ), 1.0),
        'w_in': nrm((DEPTH, D_MODEL, IN_PROJ), D_MODEL ** -0.5),
        'b_gate': nrm((DEPTH, N_BRANCH, D_MODEL), 0.1),
        'gla_gk_up': nrm((DEPTH, GLA_GATE_RANK, GLA_HEADS * GLA_DK), GLA_GATE_RANK ** -0.5),
        'gla_gk_bias': nrm((DEPTH, GLA_HEADS * GLA_DK), 0.1),
        'gla_norm_w': 1.0 + nrm((DEPTH, GLA_DV), 0.02),
        'rwkv_mu': jax.random.uniform(next(ks), (DEPTH, RWKV_PROJ), jnp.float32),
        'rwkv_w0': nrm((DEPTH, W), 0.5) - 0.5,
        'rwkv_w_up': nrm((DEPTH, RWKV_DECAY_RANK, W), 0.1),
        'rwkv_a0': nrm((DEPTH, W), 0.1),
        'rwkv_a_up': nrm((DEPTH, RWKV_A_RANK, W), RWKV_A_RANK ** -0.5),
        'rwkv_g_up': nrm((DEPTH, RWKV_GATE_RANK, W), RWKV_GATE_RANK ** -0.5),
        'rwkv_k_k': 0.85 + nrm((DEPTH, W), 0.02),
        'rwkv_k_a': 1.0 + nrm((DEPTH, W), 0.02),
        'rwkv_r_k': nrm((DEPTH, RWKV_HEADS, RWKV_HEAD), 0.1),
        'rwkv_ln_w': 1.0 + nrm((DEPTH, W), 0.02),
        'rwkv_ln_b': nrm((DEPTH, W), 0.02),
        'w_branch': nrm((DEPTH, N_BRANCH, W, D_MODEL), BETA * W ** -0.5),
        'w_out': nrm((DEPTH, D_MODEL, D_MODEL), BETA * D_MODEL ** -0.5),
        'ln1_g': 1.0 + nrm((DEPTH, D_MODEL), 0.02),
        'ln1_b': nrm((DEPTH, D_MODEL), 0.02),
        'w_up': nrm((DEPTH, D_MODEL, D_FF), BETA * D_MODEL ** -0.5),
        'w_down': nrm((DEPTH, D_FF, D_MODEL), BETA * D_FF ** -0.5),
        'ln2_g': 1.0 + nrm((DEPTH, D_MODEL), 0.02),
        'ln2_b': nrm((DEPTH, D_MODEL), 0.02),
    }


def reference(x_prompt, x_sample, cache_k, cache_v, page_table, state_gla, state_wkv, state_shift,
              w_in, b_gate, gla_gk_up, gla_gk_bias, gla_norm_w, rwkv_mu, rwkv_w0, rwkv_w_up,
              rwkv_a0, rwkv_a_up, rwkv_g_up, rwkv_k_k, rwkv_k_a, rwkv_r_k, rwkv_ln_w, rwkv_ln_b,
              w_branch, w_out, ln1_g, ln1_b, w_up, w_down, ln2_g, ln2_b):
    B, L, _ = x_prompt.shape
    Bd, Ld, _ = x_sample.shape
    past = page_table.shape[1] * cache_k.shape[2]
    pos_p = jnp.arange(L, dtype=jnp.int32)
    pos_s = past + jnp.arange(Ld, dtype=jnp.int32)
    attend_p = functools.partial(moba_prompt, pos=pos_p)
    yp, ys = x_prompt, x_sample
    outs_p, outs_s = [], []
    for l in range(DEPTH):
        p = {
            'w_in': w_in[l], 'b_gate': b_gate[l], 'gla_gk_up': gla_gk_up[l], 'gla_gk_bias': gla_gk_bias[l],
            'gla_norm_w': gla_norm_w[l], 'rwkv_mu': rwkv_mu[l], 'rwkv_w0': rwkv_w0[l],
            'rwkv_w_up': rwkv_w_up[l], 'rwkv_a0': rwkv_a0[l], 'rwkv_a_up': rwkv_a_up[l],
            'rwkv_g_up': rwkv_g_up[l], 'rwkv_k_k': rwkv_k_k[l], 'rwkv_k_a': rwkv_k_a[l],
            'rwkv_r_k': rwkv_r_k[l], 'rwkv_ln_w': rwkv_ln_w[l], 'rwkv_ln_b': rwkv_ln_b[l],
            'w_branch': w_branch[l], 'w_out': w_out[l], 'ln1_g': ln1_g[l], 'ln1_b': ln1_b[l],
            'w_up': w_up[l], 'w_down': w_down[l], 'ln2_g': ln2_g[l], 'ln2_b': ln2_b[l],
        }
        yp, st_p = layer(yp, pos_p, p,
                         jnp.zeros((B, GLA_HEADS, GLA_DK, GLA_DV), state_gla.dtype),
                         jnp.zeros((B, RWKV_HEADS, RWKV_HEAD, RWKV_HEAD), state_wkv.dtype),
                         jnp.zeros((B, RWKV_PROJ), state_shift.dtype),
                         attend_p)
        attend_s = functools.partial(moba_sample, pos=pos_s, cache_k_l=cache_k[l],
                                     cache_v_l=cache_v[l], page_table=page_table)
        ys, st_s = layer(ys, pos_s, p, state_gla[l], state_wkv[l], state_shift[l], attend_s)
        outs_p.append(st_p)
        outs_s.append(st_s)
    k_p, v_p, gla_p, wkv_p, shift_p = [jnp.stack(t) for t in zip(*outs_p)]
    k_s, v_s, gla_s, wkv_s, shift_s = [jnp.stack(t) for t in zip(*outs_s)]
    return (yp, ys, k_p, v_p, k_s, v_s, gla_p, gla_s, wkv_p, wkv_s, shift_p, shift_s)
```

```python
import numpy as np
from contextlib import ExitStack
import concourse.bass as bass
import concourse.mybir as mybir
from concourse.bass_utils import run_bass_kernel_spmd

F32 = mybir.dt.float32
BF16 = mybir.dt.bfloat16
I32 = mybir.dt.int32
ALU = mybir.AluOpType
AF = mybir.ActivationFunctionType
AX = mybir.AxisListType

ENGS = ['pe', 'act', 'dve', 'pool', 'sp']
import os as _os
SKIP = _os.environ.get('SKIP', '').split(',')

D = 1024
NT = 17
NTOK = NT * 128
DEPTH = 2
W = 512
IN_PROJ = 7952
C_GQ, C_GK, C_GV, C_GLOW, C_GOUT = 0, 256, 512, 1024, 1040
C_MQ, C_MK, C_MV = 1552, 2064, 2576
C_PR = 3088
RWKV_PROJ = 1792
C_PG = 4880
ALPHA = (2 * DEPTH) ** 0.25
NEG = -30000.0


class Buf:
    __slots__ = ('name', 'w', 'r', 'excl')

    def __init__(self, name='', excl=False):
        self.name = name
        self.w = None
        self.r = {}
        self.excl = excl


class Op:
    __slots__ = ('stream', 'idx', 'fn', 'waits', 'marked', 'val', 'issuer')

    def __init__(self, stream, idx, fn, issuer):
        self.stream = stream
        self.idx = idx
        self.fn = fn
        self.waits = []
        self.marked = False
        self.val = 0
        self.issuer = issuer


class Tile:
    __slots__ = ('ap', 'bufs')

    def __init__(self, ap, bufs):
        self.ap = ap
        self.bufs = bufs if isinstance(bufs, list) else [bufs]

    def __getitem__(self, key):
        return self.ap[key]


def _bufs(items):
    out = []
    for it in items:
        if it is None:
            continue
        if isinstance(it, Buf):
            out.append(it)
        elif isinstance(it, Tile):
            out.extend(it.bufs)
        else:
            out.extend(_bufs(it))
    return out


class Prog:
    def __init__(self, nc, nslots=18):
        self.nc = nc
        self.es = ExitStack()
        self.nslots = nslots
        self.slotpool = {'sp': list(range(0, 7)), 'pool': list(range(7, 14)), 'act': list(range(14, 18))}
        self.slotrr = {'sp': 0, 'pool': 0, 'act': 0}
        self.streams = {e: [] for e in ENGS}
        for k in range(nslots):
            self.streams[('slot', k)] = []
        self.issue = {e: [] for e in ENGS}
        self.seen = {e: {} for e in ENGS}
        self.pending = {e: [] for e in ENGS}
        self.nextslot = 0

    def _deps(self, issuer, reads, writes):
        deps = {}

        def add(o):
            if o is None:
                return
            cur = deps.get(o.stream)
            if cur is None or cur.idx < o.idx:
                deps[o.stream] = o

        for b in reads:
            add(b.w)
        for b in writes:
            add(b.w)
            for o in b.r.values():
                add(o)
        if self.pending[issuer]:
            for o in self.pending[issuer]:
                add(o)
            self.pending[issuer] = []
        waits = []
        seen = self.seen[issuer]
        for st, o in deps.items():
            if st == 'pe' and issuer == 'pe':
                continue
            if seen.get(st, 0) >= o.idx:
                continue
            seen[st] = o.idx
            waits.append(o)
        return waits

    def _record(self, op, reads, writes):
        for b in reads:
            b.r[op.stream] = op
        for b in writes:
            b.w = op
            b.r = {}

    def _cut(self):
        import os
        cut = int(os.environ.get('CUT', '0'))
        self.nops = getattr(self, 'nops', 0) + 1
        return bool(cut) and self.nops > cut and not getattr(self, 'nocut', False)

    def op(self, eng, fn, reads=(), writes=()):
        if self._cut():
            return None
        reads = _bufs(reads)
        writes = _bufs(writes)
        writes = writes + [b for b in reads if b.excl]
        reads = [b for b in reads if not b.excl]
        waits = self._deps(eng, reads, writes)
        st = self.streams[eng]
        o = Op(eng, len(st) + 1, fn, eng)
        o.waits = waits
        st.append(o)
        self.issue[eng].append(o)
        self._record(o, reads, writes)
        return o

    def dma(self, issuer, out, in_, reads=(), writes=(), **kw):
        if self._cut():
            return None
        reads = _bufs(reads)
        writes = _bufs(writes)
        pool_ = self.slotpool[issuer]
        k = pool_[self.slotrr[issuer] % len(pool_)]
        self.slotrr[issuer] += 1
        stn = ('slot', k)
        st = self.streams[stn]
        waits = self._deps(issuer, reads, writes)
        if st:
            prev = st[-1]
            if self.seen[issuer].get(stn, 0) < prev.idx:
                self.seen[issuer][stn] = prev.idx
                waits.append(prev)
        fn_ = kw.pop('_fn', None)
        o = Op(stn, len(st) + 1, fn_ if fn_ is not None else (lambda e: e.dma_start(out=out, in_=in_, **kw)), issuer)
        o.waits = waits
        st.append(o)
        self.issue[issuer].append(o)
        self._record(o, reads, writes)
        return o

    def barrier(self):
        lasts = [st[-1] for st in self.streams.values() if st]
        for e in ENGS:
            self.pending[e] = list(lasts)

    def emit(self):
        nc = self.nc
        for e in ENGS:
            for o in self.issue[e]:
                for w in o.waits:
                    w.marked = True
        finals = []
        for stn, st in self.streams.items():
            if st:
                st[-1].marked = True
                finals.append(st[-1])
        for stn, st in self.streams.items():
            c = 0
            isslot = not isinstance(stn, str)
            for o in st:
                if isslot:
                    o.marked = True
                if o.marked:
                    c += 1
                o.val = c * (16 if isslot else 1)
        sems = {}
        for stn in self.streams:
            nm = stn if isinstance(stn, str) else f'slot{stn[1]}'
            sems[stn] = self.es.enter_context(nc.semaphore('sem_' + nm))
        prog = self
        self.stats = {str(k): len(v) for k, v in self.streams.items()}

        def run(ename, e):
            for o in prog.issue[ename]:
                for w in o.waits:
                    e.wait_ge(sems[w.stream], w.val)
                ins = o.fn(e)
                if o.marked:
                    ins.then_inc(sems[o.stream], 16 if not isinstance(o.stream, str) else 1)
            if ename == 'sp':
                for f in finals:
                    e.wait_ge(sems[f.stream], f.val)

        with nc.Block() as block:
            @block.tensor
            def _(e):
                run('pe', e)

            @block.scalar
            def _(e):
                run('act', e)

            @block.vector
            def _(e):
                run('dve', e)

            @block.gpsimd
            def _(e):
                run('pool', e)

            @block.sync
            def _(e):
                run('sp', e)
        self.es.close()


def make_consts():
    c = {}
    c['ident'] = np.eye(128, dtype=np.float32)
    j = np.arange(128)[:, None]
    i = np.arange(128)[None, :]
    same = (j // 64) == (i // 64)
    c['tri'] = (same & (j <= i)).astype(np.float32)
    c['tris'] = (same & (j < i)).astype(np.float32)
    c['trist'] = (same & (j > i)).astype(np.float32)
    c['bo'] = same.astype(np.float32)
    bs = np.zeros((128, 2), np.float32)
    bs[:64, 0] = 1
    bs[64:, 1] = 1
    c['bsel'] = bs
    pos = np.concatenate([np.arange(2048), np.full(128, 2048)]).astype(np.float32)
    inv = (10000.0 ** (-np.arange(32, dtype=np.float32) / 32)).astype(np.float32)
    ang = pos[:, None] * inv[None, :]
    c['rope'] = np.concatenate([np.cos(ang), np.sin(ang)], 1).astype(np.float32)
    c['cmask'] = np.where(j > i, NEG, 0.0).astype(np.float32)
    E = np.zeros((8, 8, 128), np.float32)
    for n in range(8):
        E[n, n, :] = 1
    c['eblk'] = E.reshape(8, 1024)
    pm = np.where(np.arange(8)[None, :] < np.arange(8)[:, None], 0.0, -1e30).astype(np.float32)
    c['pastneg'] = np.ascontiguousarray(np.broadcast_to(pm.reshape(1, 64), (128, 64)))
    i16 = np.eye(16, dtype=np.float32)
    c['i16bc'] = np.ascontiguousarray(np.broadcast_to(i16.reshape(1, 256), (128, 256)))
    c['iota'] = np.ascontiguousarray(np.broadcast_to(np.arange(128, dtype=np.float32)[:, None], (128, 16)))
    cmk = np.where(np.eye(16, dtype=bool), 0.0, NEG).astype(np.float32)
    c['cmk'] = np.ascontiguousarray(np.broadcast_to(cmk.reshape(1, 256), (8, 256)))
    c['dmask'] = np.where(np.eye(128, dtype=bool), 0.0, NEG).astype(np.float32)
    return c


CONST_SHAPES = {k: (v.shape, v.dtype) for k, v in make_consts().items()}

WEIGHTS = [('w_in', [DEPTH, D, IN_PROJ]), ('b_gate', [DEPTH, 3, D]), ('gla_gk_up', [DEPTH, 16, 256]),
           ('gla_gk_bias', [DEPTH, 256]), ('gla_norm_w', [DEPTH, 128]), ('rwkv_mu', [DEPTH, RWKV_PROJ]),
           ('rwkv_w0', [DEPTH, W]), ('rwkv_w_up', [DEPTH, 64, W]), ('rwkv_a0', [DEPTH, W]),
           ('rwkv_a_up', [DEPTH, 64, W]), ('rwkv_g_up', [DEPTH, 128, W]), ('rwkv_k_k', [DEPTH, W]),
           ('rwkv_k_a', [DEPTH, W]), ('rwkv_r_k', [DEPTH, 8, 64]), ('rwkv_ln_w', [DEPTH, W]),
           ('rwkv_ln_b', [DEPTH, W]), ('w_branch', [DEPTH, 3, W, D]), ('w_out', [DEPTH, D, D]),
           ('ln1_g', [DEPTH, D]), ('ln1_b', [DEPTH, D]), ('w_up', [DEPTH, D, 4 * D]),
           ('w_down', [DEPTH, 4 * D, D]), ('ln2_g', [DEPTH, D]), ('ln2_b', [DEPTH, D])]


class Builder:
    def __init__(self, stages=('all',), n_pool=2560):
        self.stages = stages
        nc = bass.Bass("TRN2", target_bir_lowering=False)
        self.nc = nc
        self.P = Prog(nc)
        P = self.P
        din = lambda n, s, dt=F32: nc.dram_tensor(n, list(s), dt, kind="ExternalInput").ap()
        dout = lambda n, s, dt=F32: nc.dram_tensor(n, list(s), dt, kind="ExternalOutput").ap()
        dint = lambda n, s, dt=F32: nc.dram_tensor(n, list(s), dt, kind="Internal").ap()
        self.I = {}
        self.I['xin'] = din('xin', [NTOK, D])
        for n, s in WEIGHTS:
            self.I[n] = din(n, s)
        self.I['state_gla'] = din('state_gla', [DEPTH, 16, 4, 64, 128])
        self.I['state_wkv'] = din('state_wkv', [DEPTH, 16, 8, 64, 64])
        self.I['state_shift'] = din('state_shift', [DEPTH, 16, RWKV_PROJ])
        self.I['cache_k'] = din('cache_k', [DEPTH, n_pool, 128, 8, 64])
        self.I['cache_v'] = din('cache_v', [DEPTH, n_pool, 128, 8, 64])
        self.I['page_table'] = din('page_table', [16, 16], I32)
        for n, (s, dt_) in CONST_SHAPES.items():
            self.I['c_' + n] = din('c_' + n, s, I32 if dt_ == np.int32 else F32)
        self.O = {}
        self.O['y'] = dout('y', [NTOK, D])
        self.O['k_out'] = dout('k_out', [DEPTH, NTOK, W])
        self.O['v_out'] = dout('v_out', [DEPTH, NTOK, W])
        self.O['gla_p'] = dout('gla_p', [DEPTH, 4, 64, 128])
        self.O['gla_s'] = dout('gla_s', [DEPTH, 16, 4, 64, 128])
        self.O['wkv_p'] = dout('wkv_p', [DEPTH, 8, 64, 64])
        self.O['wkv_s'] = dout('wkv_s', [DEPTH, 16, 8, 64, 64])
        self.O['shift'] = dout('shift', [DEPTH, 17, RWKV_PROJ])
        self.Pd = dint('Pscr', [1 + NTOK, IN_PROJ])
        self.Pb = [[Buf(f'P{t}_{g}') for g in range(16)] for t in range(NT)]
        self.Pzero = Buf('Pzero')
        self.X1d = dint('X1scr', [NTOK, D])
        self.X1b = [Buf(f'X1_{t}') for t in range(NT)]
        self.WUd = dint('WUscr', [8, 128, 4096], BF16)
        self.WUb = [Buf(f'WU{i}') for i in range(8)]
        self.WDd = dint('WDscr', [8, 128, 4096], BF16)
        self.WDb = [Buf(f'WD{i}') for i in range(8)]
        self.X2d = dint('X2scr', [NTOK, D])
        self.X2b = [Buf(f'X2_{t}') for t in range(NT)]
        if 'dbg' in stages:
            self.O['dbg_yT'] = dout('dbg_yT', [128, 12 * NTOK], BF16)
        self.ARENA_WORDS = 48000
        self.arena = P.es.enter_context(nc.sbuf_tensor('arena', [128, self.ARENA_WORDS], F32))
        self.psum = P.es.enter_context(nc.psum_tensor('psum', [128, 4096], F32))
        self.aptr = 0
        self.rr = 0
        self.pbufs = [Buf(f'bank{i}', excl=True) for i in range(8)]

    def alloc(self, shape, dtype=F32, name='', nbufs=1):
        parts = shape[0]
        free = int(np.prod(shape[1:]))
        words = free if dtype in (F32, I32) else (free + 1) // 2
        a = self.aptr
        self.aptr += words
        assert self.aptr <= self.ARENA_WORDS, f'arena overflow {self.aptr} ({name})'
        ap = self.arena[0:parts, a:a + words]
        if dtype == BF16:
            ap = ap.bitcast(BF16)
            if free % 2:
                ap = ap[:, 0:free]
        elif dtype == I32:
            ap = ap.bitcast(I32)
        ap = self._shape(ap, shape)
        return Tile(ap, [Buf(name) for _ in range(nbufs)])

    @staticmethod
    def _shape(ap, shape):
        if len(shape) == 2:
            return ap
        if len(shape) == 3:
            return ap.rearrange('p (a b) -> p a b', a=shape[1], b=shape[2])
        if len(shape) == 4:
            return ap.rearrange('p (a b c) -> p a b c', a=shape[1], b=shape[2], c=shape[3])
        raise ValueError

    def pbank(self, bank, nbanks=1, dtype=F32, shape=None, parts=128, name=''):
        ap = self.psum[0:parts, bank * 512:(bank + nbanks) * 512]
        if dtype == BF16:
            ap = ap.bitcast(BF16)
        if shape is not None:
            free = int(np.prod(shape[1:]))
            ap = ap[:, 0:free]
            ap = self._shape(ap, shape)
        return Tile(ap, [self.pbufs[bank + i] for i in range(nbanks)])

    def op(self, eng, method, reads, writes, *a, **kw):
        import os
        if eng == 'pool' and os.environ.get('NOPOOL'):
            eng = 'dve'
        return self.P.op(eng, lambda e: getattr(e, method)(*a, **kw), reads, writes)

    def opn(self, *a, **kw):
        import os
        self.midn += 1
        if self.midn > int(os.environ.get('MIDN', '1000')):
            return None
        return self.op(*a, **kw)

    def load(self, dst, src, reads=(), eng='sp'):
        return self.P.dma(eng, dst.ap if isinstance(dst, Tile) else dst, src, reads=reads, writes=[dst])

    def store(self, dst, src_tile, src_ap=None, writes=(), eng='pool'):
        return self.P.dma(eng, dst, src_ap if src_ap is not None else src_tile.ap, reads=[src_tile], writes=writes)

    def pcols(self, t, c0, c1):
        return [self.Pb[t][g] for g in range(c0 // 512, (c1 - 1) // 512 + 1)]

    def bc_row(self, dram_row_ap, n):
        return dram_row_ap.partition_broadcast(128)

    def rsqrt(self, dst, src, scale, eps, src_ap=None, dst_ap=None):
        da = dst_ap if dst_ap is not None else dst.ap
        sa = src_ap if src_ap is not None else src.ap
        self.op('act', 'activation', [src, self.epsb], [dst], out=da, in_=sa, func=AF.Ln, bias=self.epsb[:, self.epsidx[eps]:self.epsidx[eps] + 1], scale=scale)
        self.op('act', 'activation', [dst], [dst], out=da, in_=da, func=AF.Exp, scale=-0.5)

    def setup(self):
        I = self.I
        self.ident = self.alloc([128, 128], F32, 'ident')
        self.identb = self.alloc([128, 128], BF16, 'identb')
        self.tri = self.alloc([128, 128], F32, 'tri')
        self.tris = self.alloc([128, 128], F32, 'tris')
        self.trist = self.alloc([128, 128], F32, 'trist')
        self.bo = self.alloc([128, 128], F32, 'bo')
        self.bsel = self.alloc([128, 2], F32, 'bsel')
        for nm in ['ident', 'tri', 'tris', 'trist', 'bo', 'bsel']:
            self.load(getattr(self, nm), I['c_' + nm])
        self.op('dve', 'tensor_copy', [self.ident], [self.identb], out=self.identb.ap, in_=self.ident.ap)
        self.epsb = self.alloc([128, 4], F32, 'epsb')
        self.epsidx = {1e-5: 0, 64e-5: 1, 1e-12: 2, 1.0: 3}
        for e_, i_ in self.epsidx.items():
            self.op('dve', 'memset', [], [self.epsb], self.epsb[:, i_:i_ + 1], float(e_))
        self.yT = self.alloc([128, 12, NTOK], BF16, 'yT')
        self.yT.bufs = [Buf(f'yT{t}') for t in range(NT)]
        self.xT_start = self.aptr
        self.xT = self.alloc([128, 8, NTOK], BF16, 'xT')
        self.xT.bufs = [Buf(f'xT{t}') for t in range(NT)]
        if 'dbg' in self.stages:
            self.op('pool', 'memset', [], [self.yT], self.yT.ap, 0.0)
        else:
            self.op('pool', 'memset', [], [self.yT.bufs[16]], self.yT[:, :, 2048:NTOK], 0.0)
        self.pers_end = self.aptr
        z = self.alloc([1, IN_PROJ], F32, 'zrow')
        self.op('dve', 'memset', [], [z], z.ap, 0.0)
        self.P.dma('pool', self.Pd[0:1, :], z.ap, reads=[z], writes=[self.Pzero])
        self.aptr = self.pers_end

    def stage_begin(self, over_xT=False):
        self.P.barrier()
        self.aptr = self.xT_start if over_xT else self.pers_end

    def stage_xT(self, xsrc, xbufs):
        self.stage_begin()
        xs = [self.alloc([128, D], F32, f'xs{i}') for i in range(2)]
        xb = [self.alloc([128, D], BF16, f'xb{i}') for i in range(2)]
        pts = [self.pbank(i, 1, BF16, [128, 8, 128], name=f'ptx{i}') for i in range(2)]
        for t in range(NT):
            s, b, pt = xs[t % 2], xb[t % 2], pts[t % 2]
            self.load(s, xsrc[t * 128:(t + 1) * 128, :], reads=[xbufs[t]] if xbufs else [])
            self.op('dve' if t % 2 else 'pool', 'tensor_copy', [s], [b], out=b.ap, in_=s.ap)
            for k in range(8):
                self.op('pe', 'transpose', [b, self.identb], [pt], out=pt[:, k, :], in_=b[:, k * 128:(k + 1) * 128],
                        identity=self.identb.ap)
            self.op('act', 'copy', [pt], [self.xT.bufs[t]], out=self.xT[:, :, t * 128:(t + 1) * 128], in_=pt.ap)

    def stage_proj(self, l):
        self.stage_begin()
        w_in = self.I['w_in']
        wst = [self.alloc([128, 8, 512], F32, f'wst{i}', nbufs=2) for i in range(2)]
        wbf = [self.alloc([128, 8, 512], BF16, f'wbf{i}', nbufs=2) for i in range(2)]
        ost = [self.alloc([128, 512], F32, f'ost{i}') for i in range(4)]
        pss = [self.pbank(i, 1, F32, name=f'psA{i}') for i in range(4)]
        n = 0
        for g in range(16):
            c0 = g * 512
            cw = min(512, IN_PROJ - c0)
            ws, wb = wst[g % 2], wbf[g % 2]
            src = w_in[l, :, c0:c0 + cw].rearrange('(k p) n -> p k n', p=128)
            self.P.dma('sp', ws[:, 0:4, 0:cw], src[:, 0:4, :], writes=[ws.bufs[0]])
            self.P.dma('sp', ws[:, 4:8, 0:cw], src[:, 4:8, :], writes=[ws.bufs[1]])
            self.op('pool', 'tensor_copy', [ws.bufs[0]], [wb.bufs[0]], out=wb[:, 0:4, 0:cw], in_=ws[:, 0:4, 0:cw])
            self.op('dve', 'tensor_copy', [ws.bufs[1]], [wb.bufs[1]], out=wb[:, 4:8, 0:cw], in_=ws[:, 4:8, 0:cw])
            for t in range(NT):
                ps, os_ = pss[n % 4], ost[n % 4]
                for k in range(8):
                    self.op('pe', 'matmul', [self.xT.bufs[t], wb], [ps], ps[:, 0:cw],
                            lhsT=self.xT[:, k, t * 128:(t + 1) * 128], rhs=wb[:, k, 0:cw], start=(k == 0), stop=(k == 7))
                if n % 2:
                    self.op('act', 'copy', [ps], [os_], out=os_[:, 0:cw], in_=ps[:, 0:cw])
                else:
                    self.op('dve', 'tensor_copy', [ps], [os_], out=os_[:, 0:cw], in_=ps[:, 0:cw])
                self.P.dma('pool', self.Pd[1 + t * 128:1 + (t + 1) * 128, c0:c0 + cw], os_[:, 0:cw],
                           reads=_bufs([os_]), writes=[self.Pb[t][g]])
                n += 1

    def stage_gla(self, l):
        self.stage_begin(over_xT=True)
        I = self.I
        A = self.alloc
        gkup = A([16, 256], F32, 'gkup')
        self.load(gkup, I['gla_gk_up'][l])
        gbias = A([128, 256], F32, 'gbias')
        self.load(gbias, I['gla_gk_bias'][l].partition_broadcast(128))
        nw = A([128, 128], F32, 'gnw')
        self.load(nw, I['gla_norm_w'][l].partition_broadcast(128))
        S = [A([64, 512], F32, f'glaS{i}') for i in range(2)]
        self.op('dve', 'memset', [], [S[0]], S[0].ap, 0.0)
        gins = [A([128, 1552], F32, f'gin{i}') for i in range(2)]
        glT = A([16, 128], F32, 'glT')
        z = A([128, 256], F32, 'z')
        la = A([128, 256], F32, 'la')
        eg = A([128, 256], F32, 'eg')
        egn = A([128, 256], F32, 'egn')
        et = A([128, 256], F32, 'et')
        gs = A([128, 256], F32, 'gs')
        eglT = A([64, 8], F32, 'eglT')
        qd = A([128, 256], F32, 'qd')
        ki = A([128, 256], F32, 'ki')
        kt = A([128, 256], F32, 'kt')
        qdT = A([64, 4, 128], F32, 'qdT')
        kiT = A([64, 4, 128], F32, 'kiT')
        AT = A([128, 4, 128], F32, 'AT')
        o = A([128, 512], F32, 'o')
        sq = A([128, 128], F32, 'sq')
        ms = A([128, 4], F32, 'ms')
        rstd = A([128, 4], F32, 'rstd')
        sg = A([128, 512], F32, 'sg')
        yb = A([128, 512], BF16, 'yb')
        ps_t = self.pbank(0, 1, F32, name='ps_t')
        ps_g = self.pbank(1, 1, F32, name='ps_g')
        ps_e = self.pbank(2, 1, F32, parts=64, name='ps_e')
        ps_qT = self.pbank(3, 1, F32, parts=64, name='ps_qT')
        ps_kT = self.pbank(4, 1, F32, parts=64, name='ps_kT')
        ps_A = self.pbank(5, 1, F32, name='ps_A')
        ps_o = self.pbank(6, 1, F32, name='ps_o')
        ps_s = self.pbank(7, 1, F32, parts=64, name='ps_s')
        ps_y = self.pbank(0, 1, BF16, [128, 4, 128], name='ps_t')
        ps_y.bufs = ps_t.bufs
        idf = self.ident
        cur = 0
        rowm = self.ident[:, 0:1]
        for t, sb in [(t_, None) for t_ in range(16)] + [(16, b_) for b_ in range(16)]:
            gin = gins[(t + (sb or 0)) % 2]
            if sb is None:
                self.load(gin, self.Pd[1 + t * 128:1 + (t + 1) * 128, 0:1552], reads=self.pcols(t, 0, 1552))
            else:
                if sb == 0:
                    self.P.dma('pool', self.O['gla_p'][l].rearrange('h d v -> d h v'), S[cur].ap.rearrange('p (h v) -> p h v', h=4), reads=_bufs([S[cur]]))
                self.op('pool', 'memset', [], [gin], gin.ap, 0.0)
                self.P.dma('sp', gin[0:1, :], self.Pd[1 + 2048 + sb:2 + 2048 + sb, 0:1552], reads=self.pcols(16, 0, 1552), writes=_bufs([gin]))
                self.P.dma('sp', S[cur].ap.rearrange('p (h v) -> p h v', h=4), I['state_gla'][l, sb].rearrange('h d v -> d h v'), writes=_bufs([S[cur]]))
            q, k, v = gin[:, 0:256], gin[:, 256:512], gin[:, 512:1024]
            glow, gout = gin[:, 1024:1040], gin[:, 1040:1552]
            self.op('pe', 'transpose', [gin, idf], [ps_t], out=ps_t[0:16, 0:128], in_=glow, identity=idf.ap)
            self.op('act', 'copy', [ps_t], [glT], out=glT.ap, in_=ps_t[0:16, 0:128])
            self.op('pe', 'matmul', [glT, gkup], [ps_t], ps_t[:, 128:384], lhsT=glT.ap, rhs=gkup.ap, start=True, stop=True)
            self.op('dve', 'tensor_tensor', [ps_t, gbias], [z], out=z.ap, in0=ps_t[:, 128:384], in1=gbias.ap, op=ALU.add)
            self.op('act', 'activation', [z], [z], out=z.ap, in_=z.ap, func=AF.Exp, scale=-1.0)
            self.op('act', 'activation', [z, self.epsb], [la], out=la.ap, in_=z.ap, func=AF.Ln, bias=self.epsb[:, 3:4])
            self.op('dve', 'tensor_scalar', [la], [la], out=la.ap, in0=la.ap, scalar1=-1.0 / 16.0, scalar2=None, op0=ALU.mult)
            if sb is not None:
                self.op('dve', 'tensor_scalar', [la, self.ident], [la], out=la.ap, in0=la.ap, scalar1=rowm, scalar2=None, op0=ALU.mult)
            self.op('pe', 'matmul', [self.tri, la], [ps_g], ps_g[:, 0:256], lhsT=self.tri.ap, rhs=la.ap, start=True, stop=True)
            self.op('pe', 'matmul', [self.bo, la], [ps_g], ps_g[:, 256:512], lhsT=self.bo.ap, rhs=la.ap, start=True, stop=True)
            for h in range(4):
                self.op('pe', 'matmul', [la, self.bsel], [ps_e], ps_e[:, 2 * h:2 * h + 2], lhsT=la[:, h * 64:(h + 1) * 64],
                        rhs=self.bsel.ap, start=True, stop=True)
            self.op('act', 'activation', [ps_e], [eglT], out=eglT.ap, in_=ps_e[:, 0:8], func=AF.Exp)
            self.op('act', 'activation', [ps_g], [eg], out=eg.ap, in_=ps_g[:, 0:256], func=AF.Exp)
            self.op('act', 'activation', [ps_g], [egn], out=egn.ap, in_=ps_g[:, 0:256], func=AF.Exp, scale=-1.0)
            self.op('dve', 'tensor_copy', [ps_g], [gs], out=gs.ap, in_=ps_g[:, 0:256])
            self.op('dve', 'tensor_tensor', [ps_g, gs], [gs], out=gs.ap, in0=ps_g[:, 256:512], in1=gs.ap, op=ALU.subtract)
            self.op('act', 'activation', [gs], [et], out=et.ap, in_=gs.ap, func=AF.Exp)
            self.op('dve', 'scalar_tensor_tensor', [gin, eg], [qd], out=qd.ap, in0=q, scalar=0.125, in1=eg.ap, op0=ALU.mult, op1=ALU.mult)
            self.op('dve', 'tensor_tensor', [gin, egn], [ki], out=ki.ap, in0=k, in1=egn.ap, op=ALU.mult)
            self.op('pool', 'tensor_tensor', [gin, et], [kt], out=kt.ap, in0=k, in1=et.ap, op=ALU.mult)
            for h in range(4):
                self.op('pe', 'transpose', [qd, idf], [ps_qT], out=ps_qT[:, h * 128:(h + 1) * 128], in_=qd[:, h * 64:(h + 1) * 64], identity=idf.ap)
            self.op('act', 'copy', [ps_qT], [qdT], out=qdT.ap.rearrange('p a b -> p (a b)'), in_=ps_qT.ap)
            for h in range(4):
                self.op('pe', 'transpose', [ki, idf], [ps_kT], out=ps_kT[:, h * 128:(h + 1) * 128], in_=ki[:, h * 64:(h + 1) * 64], identity=idf.ap)
            self.op('dve', 'tensor_copy', [ps_kT], [kiT], out=kiT.ap.rearrange('p a b -> p (a b)'), in_=ps_kT.ap)
            for h in range(4):
                self.op('pe', 'matmul', [kiT, qdT], [ps_A], ps_A[:, h * 128:(h + 1) * 128], lhsT=kiT[:, h, :], rhs=qdT[:, h, :], start=True, stop=True)
            self.op('dve', 'tensor_tensor', [ps_A, self.tri], [AT], out=AT.ap, in0=ps_A.ap.rearrange('p (a b) -> p a b', a=4),
                    in1=self.tri.ap.unsqueeze(1).to_broadcast([128, 4, 128]), op=ALU.mult)
            for c in (range(2) if sb is None else [0]):
                Sc, Sn = S[cur], S[1 - cur]
                rows = slice(c * 64, (c + 1) * 64)
                for h in range(4):
                    hs = slice(h * 128, (h + 1) * 128)
                    self.op('pe', 'matmul', [AT, gin], [ps_o], ps_o[:, hs], lhsT=AT[:, h, :], rhs=v[:, hs], start=True, stop=False)
                    self.op('pe', 'matmul', [qdT, Sc], [ps_o], ps_o[:, hs], lhsT=qdT[:, h, :], rhs=Sc[:, hs], start=False, stop=True)
                self.op('act', 'copy', [ps_o], [o], out=o[rows, :], in_=ps_o[rows, :])
                for h in range(4):
                    hs = slice(h * 128, (h + 1) * 128)
                    self.op('pe', 'matmul', [kt, gin], [ps_s], ps_s[:, hs], lhsT=kt[rows, h * 64:(h + 1) * 64], rhs=v[rows, hs], start=True, stop=True)
                for h in range(4):
                    hs = slice(h * 128, (h + 1) * 128)
                    self.op('dve', 'scalar_tensor_tensor', [Sc, eglT, ps_s], [Sn], out=Sn[:, hs], in0=Sc[:, hs],
                            scalar=eglT[:, 2 * h + c:2 * h + c + 1], in1=ps_s[:, hs], op0=ALU.mult, op1=ALU.add)
                cur = 1 - cur
            self.gla_finish(o, gout, gin, nw, sq, ms, rstd, sg, yb, ps_y, t, sb)
            if sb is not None:
                self.P.dma('pool', self.O['gla_s'][l, sb].rearrange('h d v -> d h v'), S[cur].ap.rearrange('p (h v) -> p h v', h=4), reads=_bufs([S[cur]]))

    def gla_finish(self, o, gout, gin, nw, sq, ms, rstd, sg, yb, ps_y, t, sb=None):
        for h in range(4):
            hs = slice(h * 128, (h + 1) * 128)
            self.op('act', 'activation', [o], [sq, ms], out=sq.ap, in_=o[:, hs], func=AF.Square, accum_out=ms[:, h:h + 1])
        self.rsqrt(rstd, ms, 1.0 / 128.0, 1e-5)
        self.op('act', 'activation', [gin], [sg], out=sg.ap, in_=gout, func=AF.Silu)
        self.op('pool', 'tensor_tensor', [sg, nw], [sg], out=sg.ap.rearrange('p (a b) -> p a b', a=4),
                in0=sg.ap.rearrange('p (a b) -> p a b', a=4), in1=nw.ap.unsqueeze(1).to_broadcast([128, 4, 128]), op=ALU.mult)
        for h in range(4):
            hs = slice(h * 128, (h + 1) * 128)
            self.op('dve', 'scalar_tensor_tensor', [o, rstd, sg], [yb], out=yb[:, hs], in0=o[:, hs], scalar=rstd[:, h:h + 1],
                    in1=sg[:, hs], op0=ALU.mult, op1=ALU.mult)
        for h in range(4):
            self.op('pe', 'transpose', [yb, self.identb], [ps_y], out=ps_y[:, h, :], in_=yb[:, h * 128:(h + 1) * 128], identity=self.identb.ap)
        if sb is None:
            self.op('act', 'copy', [ps_y], [self.yT.bufs[t]], out=self.yT[:, 0:4, t * 128:(t + 1) * 128], in_=ps_y.ap)
        else:
            self.op('act', 'copy', [ps_y], [self.yT.bufs[t]], out=self.yT[:, 0:4, 2048 + sb:2049 + sb], in_=ps_y[:, :, 0:1])

    def stage_rwkv(self, l):
        self.stage_begin(over_xT=True)
        I = self.I
        A = self.alloc
        op = self.op
        bc = lambda nm, n: I[nm][l].partition_broadcast(128)
        mu = A([128, RWKV_PROJ], F32, 'mu'); self.load(mu, bc('rwkv_mu', RWKV_PROJ))
        cb = {}
        for nm in ['rwkv_w0', 'rwkv_a0', 'rwkv_k_k', 'rwkv_k_a', 'rwkv_ln_w', 'rwkv_ln_b']:
            cb[nm] = A([128, 512], F32, nm); self.load(cb[nm], bc(nm, 512))
        cb['rk'] = A([128, 512], F32, 'rk'); self.load(cb['rk'], I['rwkv_r_k'][l].rearrange('h d -> (h d)').partition_broadcast(128))
        waup = A([128, 512], F32, 'waup')
        self.P.dma('sp', waup[0:64, :], I['rwkv_w_up'][l], writes=_bufs([waup]))
        self.P.dma('sp', waup[64:128, :], I['rwkv_a_up'][l], writes=_bufs([waup]))
        gup = A([128, 512], F32, 'gup'); self.load(gup, I['rwkv_g_up'][l])
        H = [A([64, 512], F32, f'H{i}') for i in range(2)]
        op('dve', 'memset', [], [H[0]], H[0].ap, 0.0)
        prs = [A([128, RWKV_PROJ], F32, f'pr{i}') for i in range(1)]
        pv = A([128, RWKV_PROJ], F32, 'pv')
        t256 = A([128, 256], F32, 't256'); tT = A([128, 256], F32, 'tT')
        T5 = lambda n: A([128, 512], F32, n)
        logw, a_, g7, kk, kmod, b_ = T5('logw'), T5('a'), T5('g7'), T5('kk'), T5('kmod'), T5('b')
        eg, egn, eex, etl, gsb = T5('eg'), T5('egn'), T5('eex'), T5('etl'), T5('gsb')
        at, rt, bt, kt, btail, ktail = T5('at'), T5('rt'), T5('bt'), T5('kt'), T5('btail'), T5('ktail')
        aT, bT, rT, kT = [A([64, 8, 128], F32, n) for n in ('aT', 'bT', 'rT', 'kT')]
        M = [A([128, 8, 128], F32, 'M0')] * 2
        N = [A([128, 8, 128], F32, 'N0')] * 2
        X = A([128, 8, 128], F32, 'X')
        AkT, ArbT, ArkT = [A([128, 8, 128], F32, n) for n in ('AkT', 'ArbT', 'ArkT')]
        AkV, Usb, ysb = T5('AkV'), T5('Usb'), T5('ysb')
        op('dve', 'memset', [], [Usb], Usb.ap, 0.0)
        P1T = A([64, 8, 128], F32, 'P1T')
        ss = A([128, 8], F32, 'ss'); rs = A([128, 8], F32, 'rs'); s3 = A([128, 8], F32, 's3')
        eglT = A([64, 16], F32, 'eglT')
        yb = A([128, 512], BF16, 'ywb')
        HT = A([64, 512], F32, 'HT')
        pk = [self.pbank(i, 1, F32, name=f'pk{i}') for i in range(8)]
        p2 = [self.pbank(2 * i, 2, F32, name=f'p2{i}') for i in range(4)]
        ps_y = self.pbank(7, 1, BF16, [128, 4, 128])
        idf = self.ident
        v3 = lambda tl: tl.ap.rearrange('p (a b) -> p a b', a=8)
        b3 = lambda tl: tl.ap.unsqueeze(2).to_broadcast([128, 8, 64])
        mask3 = lambda tl: tl.ap.unsqueeze(1).to_broadcast([128, 8, 128])
        mask4 = lambda tl: tl.ap.unsqueeze(1).to_broadcast([128, 4, 128])
        h3 = lambda tl, hf: tl[:, 512 * hf:512 * hf + 512].rearrange('p (a b) -> p a b', a=4)
        cur = 0
        import os
        Sin = HT
        for t, sb in [(t_, None) for t_ in range(16)] + [(16, b_) for b_ in range(16)]:
            pr = prs[0]
            r0 = 1 + t * 128
            if sb is not None:
                if sb == 0:
                    self.rwkv_store_state(H[cur], HT, pk, self.O['wkv_p'][l])
                op('pool', 'memset', [], [pr], pr.ap, 0.0)
                op('pool', 'memset', [], [pv], pv.ap, 0.0)
                rs_ = 1 + 2048 + sb
                self.P.dma('sp', pr[0:1, :], self.Pd[rs_:rs_ + 1, C_PR:C_PR + RWKV_PROJ], reads=self.pcols(16, C_PR, C_PR + RWKV_PROJ), writes=_bufs([pr]))
                self.P.dma('sp', pv[0:1, :], I['state_shift'][l, sb:sb + 1, :], writes=_bufs([pv]))
                self.P.dma('pool', self.O['shift'][l, 1 + sb:2 + sb, :], pr[0:1, :], reads=_bufs([pr]))
                self.P.dma('sp', Sin.ap.rearrange('p (h j) -> p h j', h=8), I['state_wkv'][l, sb].rearrange('h i j -> i h j'), writes=_bufs([Sin]))
                for h in range(8):
                    op('pe', 'transpose', [Sin, idf], [pk[0]], out=pk[0][0:64, h * 64:(h + 1) * 64], in_=Sin[:, h * 64:(h + 1) * 64], identity=idf[0:64, 0:64])
                op('act', 'copy', [pk[0]], [H[cur]], out=H[cur].ap, in_=pk[0][0:64, :])
            for q_ in (range(4) if sb is None else []):
                cs = slice(q_ * 448, (q_ + 1) * 448)
                self.P.dma('sp', pr[:, cs], self.Pd[r0:r0 + 128, C_PR + q_ * 448:C_PR + (q_ + 1) * 448], reads=self.pcols(t, C_PR, C_PR + RWKV_PROJ), writes=_bufs([pr]))
                self.P.dma('sp', pv[:, cs], self.Pd[r0 - 1:r0 + 127, C_PR + q_ * 448:C_PR + (q_ + 1) * 448],
                           reads=self.pcols(t, C_PR, C_PR + RWKV_PROJ) + (self.pcols(t - 1, C_PR, C_PR + RWKV_PROJ) if t else [self.Pzero]), writes=_bufs([pv]))
            if t == 15:
                self.P.dma('pool', self.O['shift'][l, 0:1, :], pr[96:128, :][31:32, :], reads=_bufs([pr]))
            op('dve', 'tensor_tensor', [pv, pr], [pv], out=pv.ap, in0=pv.ap, in1=pr.ap, op=ALU.subtract)
            op('pool', 'tensor_tensor', [pv, mu], [pv], out=pv.ap, in0=pv.ap, in1=mu.ap, op=ALU.mult)
            op('dve', 'tensor_tensor', [pv, pr], [pv], out=pv.ap, in0=pv.ap, in1=pr.ap, op=ALU.add)
            r, k7, v7 = pv[:, 0:512], pv[:, 512:1024], pv[:, 1024:1536]
            self.rwkv_params(pv, t256, tT, waup, gup, cb, logw, a_, g7, kk, kmod, b_, ss, rs, eg, pk, idf, v3, b3)
            if sb is not None:
                op('dve', 'tensor_scalar', [logw, idf], [logw], out=logw.ap, in0=logw.ap, scalar1=idf[:, 0:1], scalar2=None, op0=ALU.mult)
            self.midn = 0
            op = self.opn
            for hf in range(2):
                cs = slice(hf * 256, (hf + 1) * 256)
                op('pe', 'matmul', [self.tri, logw], [pk[4]], pk[4][:, cs], lhsT=self.tri.ap, rhs=logw[:, cs], start=True, stop=True)
                op('pe', 'matmul', [self.bo, logw], [pk[5]], pk[5][:, cs], lhsT=self.bo.ap, rhs=logw[:, cs], start=True, stop=True)
            for h in range(8):
                op('pe', 'matmul', [logw, self.bsel], [pk[3]], pk[3][0:64, 2 * h:2 * h + 2], lhsT=logw[:, h * 64:(h + 1) * 64], rhs=self.bsel.ap, start=True, stop=True)
            op('act', 'activation', [pk[3]], [eglT], out=eglT.ap, in_=pk[3][0:64, 0:16], func=AF.Exp)
            op('act', 'activation', [pk[4]], [eg], out=eg.ap, in_=pk[4].ap, func=AF.Exp)
            op('act', 'activation', [pk[4]], [egn], out=egn.ap, in_=pk[4].ap, func=AF.Exp, scale=-1.0)
            op('dve', 'tensor_copy', [pk[4]], [gsb], out=gsb.ap, in_=pk[4].ap)
            op('dve', 'tensor_tensor', [gsb, logw], [eex], out=eex.ap, in0=gsb.ap, in1=logw.ap, op=ALU.subtract)
            op('act', 'activation', [eex], [eex], out=eex.ap, in_=eex.ap, func=AF.Exp)
            op('dve', 'tensor_tensor', [pk[5], gsb], [etl], out=etl.ap, in0=pk[5].ap, in1=gsb.ap, op=ALU.subtract)
            op('act', 'activation', [etl], [etl], out=etl.ap, in_=etl.ap, func=AF.Exp)
            op('dve', 'scalar_tensor_tensor', [kk, eex], [at], out=at.ap, in0=kk.ap, scalar=-1.0, in1=eex.ap, op0=ALU.mult, op1=ALU.mult)
            op('pool', 'tensor_tensor', [pv, eg], [rt], out=rt.ap, in0=r, in1=eg.ap, op=ALU.mult)
            op('dve', 'tensor_tensor', [b_, egn], [bt], out=bt.ap, in0=b_.ap, in1=egn.ap, op=ALU.mult)
            op('pool', 'tensor_tensor', [kmod, egn], [kt], out=kt.ap, in0=kmod.ap, in1=egn.ap, op=ALU.mult)
            op('pool', 'tensor_tensor', [b_, etl], [btail], out=btail.ap, in0=b_.ap, in1=etl.ap, op=ALU.mult)
            op('pool', 'tensor_tensor', [kmod, etl], [ktail], out=ktail.ap, in0=kmod.ap, in1=etl.ap, op=ALU.mult)
            op = self.op
            for i_, (src, dst) in enumerate(((at, aT), (bt, bT), (rt, rT), (kt, kT))):
                pp = p2[2 + (i_ % 2)]
                for h in range(8):
                    op('pe', 'transpose', [src, idf], [pp], out=pp[0:64, h * 128:(h + 1) * 128], in_=src[:, h * 64:(h + 1) * 64], identity=idf.ap)
                for hf in range(2):
                    op('act' if hf else 'dve', 'copy' if hf else 'tensor_copy', [pp], [dst], out=dst[:, 4 * hf:4 * hf + 4, :].rearrange('p a b -> p (a b)'), in_=pp[0:64, 512 * hf:512 * hf + 512])
            for h in range(8):
                op('pe', 'matmul', [bT, aT], [p2[0]], p2[0][:, h * 128:(h + 1) * 128], lhsT=bT[:, h, :], rhs=aT[:, h, :], start=True, stop=True)
            for h in range(8):
                op('pe', 'matmul', [bT, aT], [p2[1]], p2[1][:, h * 128:(h + 1) * 128], lhsT=aT[:, h, :], rhs=bT[:, h, :], start=True, stop=True)
            for hf in range(2):
                op('dve', 'tensor_tensor', [p2[0], self.tris], [M[0]], out=M[0][:, 4 * hf:4 * hf + 4, :], in0=h3(p2[0], hf), in1=mask4(self.tris), op=ALU.mult)
                op('dve', 'tensor_tensor', [p2[1], self.trist], [N[0]], out=N[0][:, 4 * hf:4 * hf + 4, :], in0=h3(p2[1], hf), in1=mask4(self.trist), op=ALU.mult)
            op('pool', 'tensor_tensor', [M[0], idf], [X], out=X.ap, in0=M[0].ap, in1=mask3(idf), op=ALU.add)
            c_ = 0
            for lvl in (range(1, 6) if sb is None else []):
                n_ = 1 - c_
                for h in range(8):
                    op('pe', 'matmul', [M[c_], N[c_]], [p2[1]], p2[1][:, h * 128:(h + 1) * 128], lhsT=M[c_][:, h, :], rhs=N[c_][:, h, :], start=True, stop=True)
                if lvl < 5:
                    for h in range(8):
                        op('pe', 'matmul', [M[c_], N[c_]], [p2[0]], p2[0][:, h * 128:(h + 1) * 128], lhsT=N[c_][:, h, :], rhs=M[c_][:, h, :], start=True, stop=True)
                for hf in range(2):
                    op('act', 'copy', [p2[1]], [N[n_]], out=N[n_][:, 4 * hf:4 * hf + 4, :], in_=h3(p2[1], hf))
                if lvl < 5:
                    for hf in range(2):
                        op('dve', 'tensor_copy', [p2[0]], [M[n_]], out=M[n_][:, 4 * hf:4 * hf + 4, :], in_=h3(p2[0], hf))
                for h in range(8):
                    op('pe', 'matmul', [N[n_], X], [p2[2]], p2[2][:, h * 128:(h + 1) * 128], lhsT=N[n_][:, h, :], rhs=X[:, h, :], start=True, stop=True)
                for hf in range(2):
                    op('dve', 'tensor_tensor', [X, p2[2]], [X], out=X[:, 4 * hf:4 * hf + 4, :], in0=X[:, 4 * hf:4 * hf + 4, :], in1=h3(p2[2], hf), op=ALU.add)
                c_ = n_
            for (lh, rh, pt_, dst, msk) in ((kT, aT, p2[0], AkT, self.tris), (bT, rT, p2[1], ArbT, self.tri), (kT, rT, p2[3], ArkT, self.tri)):
                for h in range(8):
                    op('pe', 'matmul', [lh, rh], [pt_], pt_[:, h * 128:(h + 1) * 128], lhsT=lh[:, h, :], rhs=rh[:, h, :], start=True, stop=True)
                for hf in range(2):
                    op('dve', 'tensor_tensor', [pt_, msk], [dst], out=dst[:, 4 * hf:4 * hf + 4, :], in0=h3(pt_, hf), in1=mask4(msk), op=ALU.mult)
            for h in range(8):
                op('pe', 'matmul', [AkT, pv], [pk[4]], pk[4][:, h * 64:(h + 1) * 64], lhsT=AkT[:, h, :], rhs=v7[:, h * 64:(h + 1) * 64], start=True, stop=True)
            op('act', 'copy', [pk[4]], [AkV], out=AkV.ap, in_=pk[4].ap)
            for h in range(8):
                op('pe', 'matmul', [at, X], [p2[0]], p2[0][0:64, h * 128:(h + 1) * 128], lhsT=at[:, h * 64:(h + 1) * 64], rhs=X[:, h, :], start=True, stop=True)
            for hf in range(2):
                op('act', 'copy', [p2[0]], [P1T], out=P1T[:, 4 * hf:4 * hf + 4, :].rearrange('p a b -> p (a b)'), in_=p2[0][0:64, 512 * hf:512 * hf + 512])
            for c in (range(2) if sb is None else [0]):
                Hc, Hn = H[cur], H[1 - cur]
                rows = slice(c * 64, (c + 1) * 64)
                for h in range(8):
                    hs = slice(h * 64, (h + 1) * 64)
                    op('pe', 'matmul', [P1T, Hc], [pk[5]], pk[5][:, hs], lhsT=P1T[:, h, :], rhs=Hc[:, hs], start=True, stop=False)
                    op('pe', 'matmul', [X, AkV], [pk[5]], pk[5][:, hs], lhsT=X[:, h, :], rhs=AkV[:, hs], start=False, stop=True)
                op('act', 'copy', [pk[5]], [Usb], out=Usb[rows, :], in_=pk[5][rows, :])
                for h in range(8):
                    hs = slice(h * 64, (h + 1) * 64)
                    op('pe', 'matmul', [rT, Hc], [pk[6]], pk[6][:, hs], lhsT=rT[:, h, :], rhs=Hc[:, hs], start=True, stop=False)
                    op('pe', 'matmul', [ArbT, Usb], [pk[6]], pk[6][:, hs], lhsT=ArbT[:, h, :], rhs=Usb[:, hs], start=False, stop=False)
                    op('pe', 'matmul', [ArkT, pv], [pk[6]], pk[6][:, hs], lhsT=ArkT[:, h, :], rhs=v7[:, hs], start=False, stop=True)
                op('act', 'copy', [pk[6]], [ysb], out=ysb[rows, :], in_=pk[6][rows, :])
                for h in range(8):
                    hs = slice(h * 64, (h + 1) * 64)
                    op('pe', 'matmul', [btail, Usb], [pk[7]], pk[7][0:64, hs], lhsT=btail[rows, hs], rhs=Usb[rows, hs], start=True, stop=False)
                    op('pe', 'matmul', [ktail, pv], [pk[7]], pk[7][0:64, hs], lhsT=ktail[rows, hs], rhs=v7[rows, hs], start=False, stop=True)
                egc = eglT.ap.rearrange('p (h c) -> p h c', c=2)[:, :, c:c + 1].to_broadcast([64, 8, 64])
                op('dve', 'tensor_tensor', [Hc, eglT], [Hn], out=Hn.ap.rearrange('p (a b) -> p a b', a=8), in0=Hc.ap.rearrange('p (a b) -> p a b', a=8), in1=egc, op=ALU.mult)
                op('dve', 'tensor_tensor', [Hn, pk[7]], [Hn], out=Hn.ap, in0=Hn.ap, in1=pk[7][0:64, :], op=ALU.add)
                cur = 1 - cur
            self.rwkv_finish(ysb, pv, kmod, g7, cb, eg, egn, ss, rs, s3, yb, ps_y, t, v3, b3, sb)
            if sb is not None:
                self.rwkv_store_state(H[cur], HT, pk, self.O['wkv_s'][l, sb])

    def rwkv_store_state(self, Hf, HT, pk, dst):
        op = self.op
        idf = self.ident
        for h in range(8):
            op('pe', 'transpose', [Hf, idf], [pk[0]], out=pk[0][0:64, h * 64:(h + 1) * 64], in_=Hf[:, h * 64:(h + 1) * 64], identity=idf[0:64, 0:64])
        op('act', 'copy', [pk[0]], [HT], out=HT.ap, in_=pk[0][0:64, :])
        self.P.dma('pool', dst.rearrange('h i j -> i h j'), HT.ap.rearrange('p (h j) -> p h j', h=8), reads=_bufs([HT]))

    def rwkv_params(self, pv, t256, tT, waup, gup, cb, logw, a_, g7, kk, kmod, b_, ss, rs, tmp, pk, idf, v3, b3):
        op = self.op
        k7 = pv[:, 512:1024]
        op('act', 'activation', [pv], [t256], out=t256[:, 0:64], in_=pv[:, 1536:1600], func=AF.Tanh)
        op('act', 'copy', [pv], [t256], out=t256[:, 64:128], in_=pv[:, 1600:1664])
        op('act', 'activation', [pv], [t256], out=t256[:, 128:256], in_=pv[:, 1664:1792], func=AF.Sigmoid)
        op('pe', 'transpose', [t256, idf], [pk[0]], out=pk[0][:, 0:128], in_=t256[:, 0:128], identity=idf.ap)
        op('pe', 'transpose', [t256, idf], [pk[0]], out=pk[0][:, 128:256], in_=t256[:, 128:256], identity=idf.ap)
        op('act', 'copy', [pk[0]], [tT], out=tT.ap, in_=pk[0][:, 0:256])
        op('pe', 'matmul', [tT, waup], [pk[1]], pk[1].ap, lhsT=tT[0:64, 0:128], rhs=waup[0:64, :], start=True, stop=True)
        op('pe', 'matmul', [tT, waup], [pk[2]], pk[2].ap, lhsT=tT[64:128, 0:128], rhs=waup[64:128, :], start=True, stop=True)
        op('pe', 'matmul', [tT, gup], [pk[3]], pk[3].ap, lhsT=tT[:, 128:256], rhs=gup.ap, start=True, stop=True)
        op('dve', 'tensor_tensor', [pk[1], cb['rwkv_w0']], [logw], out=logw.ap, in0=pk[1].ap, in1=cb['rwkv_w0'].ap, op=ALU.add)
        op('act', 'activation', [logw], [logw], out=logw.ap, in_=logw.ap, func=AF.Sigmoid)
        op('pool', 'tensor_scalar', [logw], [logw], out=logw.ap, in0=logw.ap, scalar1=-0.606531, scalar2=None, op0=ALU.mult)
        op('dve', 'tensor_tensor', [pk[2], cb['rwkv_a0']], [a_], out=a_.ap, in0=pk[2].ap, in1=cb['rwkv_a0'].ap, op=ALU.add)
        op('act', 'activation', [a_], [a_], out=a_.ap, in_=a_.ap, func=AF.Sigmoid)
        op('act', 'copy', [pk[3]], [g7], out=g7.ap, in_=pk[3].ap)
        op('dve', 'tensor_tensor', [pv, cb['rwkv_k_k']], [kk], out=kk.ap, in0=k7, in1=cb['rwkv_k_k'].ap, op=ALU.mult)
        op('dve', 'tensor_tensor', [kk], [tmp], out=tmp.ap, in0=kk.ap, in1=kk.ap, op=ALU.mult)
        op('dve', 'tensor_reduce', [tmp], [ss], out=ss.ap, in_=v3(tmp), axis=AX.X, op=ALU.add)
        self.rsqrt(rs, ss, 1.0, 1e-12)
        op('dve', 'tensor_tensor', [kk, rs], [kk], out=v3(kk), in0=v3(kk), in1=b3(rs), op=ALU.mult)
        op('dve', 'scalar_tensor_tensor', [a_, cb['rwkv_k_a']], [tmp], out=tmp.ap, in0=a_.ap, scalar=-1.0, in1=cb['rwkv_k_a'].ap, op0=ALU.add, op1=ALU.mult)
        op('dve', 'scalar_tensor_tensor', [tmp, pv], [kmod], out=kmod.ap, in0=tmp.ap, scalar=1.0, in1=k7, op0=ALU.add, op1=ALU.mult)
        op('pool', 'tensor_tensor', [kk, a_], [b_], out=b_.ap, in0=kk.ap, in1=a_.ap, op=ALU.mult)

    def rwkv_finish(self, ysb, pv, kmod, g7, cb, yc, sq, ss, rs, s3, yb, ps_y, t, v3, b3, sb=None):
        op = self.op
        r, v7 = pv[:, 0:512], pv[:, 1024:1536]
        op('dve', 'tensor_reduce', [ysb], [ss], out=ss.ap, in_=v3(ysb), axis=AX.X, op=ALU.add)
        op('dve', 'tensor_scalar', [ss], [ss], out=ss.ap, in0=ss.ap, scalar1=-1.0 / 64.0, scalar2=None, op0=ALU.mult)
        op('dve', 'tensor_tensor', [ysb, ss], [yc], out=v3(yc), in0=v3(ysb), in1=b3(ss), op=ALU.add)
        op('pool', 'tensor_tensor', [yc], [sq], out=sq.ap, in0=yc.ap, in1=yc.ap, op=ALU.mult)
        op('dve', 'tensor_reduce', [sq], [rs], out=rs.ap, in_=v3(sq), axis=AX.X, op=ALU.add)
        self.rsqrt(rs, rs, 1.0 / 64.0, 64e-5)
        op('dve', 'tensor_tensor', [yc, rs], [yc], out=v3(yc), in0=v3(yc), in1=b3(rs), op=ALU.mult)
        op('pool', 'tensor_tensor', [yc, cb['rwkv_ln_w']], [yc], out=yc.ap, in0=yc.ap, in1=cb['rwkv_ln_w'].ap, op=ALU.mult)
        op('pool', 'tensor_tensor', [yc, cb['rwkv_ln_b']], [yc], out=yc.ap, in0=yc.ap, in1=cb['rwkv_ln_b'].ap, op=ALU.add)
        op('dve', 'tensor_tensor', [pv, kmod], [sq], out=sq.ap, in0=r, in1=kmod.ap, op=ALU.mult)
        op('dve', 'tensor_tensor', [sq, cb['rk']], [sq], out=sq.ap, in0=sq.ap, in1=cb['rk'].ap, op=ALU.mult)
        op('dve', 'tensor_reduce', [sq], [s3], out=s3.ap, in_=v3(sq), axis=AX.X, op=ALU.add)
        op('dve', 'tensor_tensor', [pv, s3], [sq], out=v3(sq), in0=pv[:, 1024:1536].rearrange('p (a b) -> p a b', a=8), in1=b3(s3), op=ALU.mult)
        op('pool', 'tensor_tensor', [yc, sq], [yc], out=yc.ap, in0=yc.ap, in1=sq.ap, op=ALU.add)
        op('dve', 'tensor_tensor', [yc, g7], [yb], out=yb.ap, in0=yc.ap, in1=g7.ap, op=ALU.mult)
        for h in range(4):
            op('pe', 'transpose', [yb, self.identb], [ps_y], out=ps_y[:, h, :], in_=yb[:, h * 128:(h + 1) * 128], identity=self.identb.ap)
        if sb is None:
            op('act', 'copy', [ps_y], [self.yT.bufs[t]], out=self.yT[:, 8:12, t * 128:(t + 1) * 128], in_=ps_y.ap)
        else:
            op('act', 'copy', [ps_y], [self.yT.bufs[t]], out=self.yT[:, 8:12, 2048 + sb:2049 + sb], in_=ps_y[:, :, 0:1])

    def layer_norm(self, z, g_bc, b_bc, out_f32, stat):
        op = self.op
        op('dve', 'tensor_reduce', [z], [stat], out=stat[:, 0:1], in_=z.ap, axis=AX.X, op=ALU.add)
        op('dve', 'tensor_scalar', [stat], [stat], out=stat[:, 1:2], in0=stat[:, 0:1], scalar1=-1.0 / D, scalar2=None, op0=ALU.mult)
        op('dve', 'tensor_scalar', [z, stat], [z], out=z.ap, in0=z.ap, scalar1=stat[:, 1:2], scalar2=None, op0=ALU.add)
        op('act', 'activation', [z], [out_f32, stat], out=out_f32.ap, in_=z.ap, func=AF.Square, accum_out=stat[:, 2:3])
        self.rsqrt(stat, stat, 1.0 / D, 1e-5, src_ap=stat[:, 2:3], dst_ap=stat[:, 3:4])
        op('dve', 'scalar_tensor_tensor', [z, stat, g_bc], [out_f32], out=out_f32.ap, in0=z.ap, scalar=stat[:, 3:4], in1=g_bc.ap, op0=ALU.mult, op1=ALU.mult)
        op('pool', 'tensor_tensor', [out_f32, b_bc], [out_f32], out=out_f32.ap, in0=out_f32.ap, in1=b_bc.ap, op=ALU.add)

    def to_xT(self, xf, xb, pt, t):
        op = self.op
        op('pool', 'tensor_copy', [xf], [xb], out=xb.ap, in_=xf.ap)
        for k in range(8):
            op('pe', 'transpose', [xb, self.identb], [pt], out=pt[:, k, :], in_=xb[:, k * 128:(k + 1) * 128], identity=self.identb.ap)
        op('act', 'copy', [pt], [self.xT.bufs[t]], out=self.xT[:, :, t * 128:(t + 1) * 128], in_=pt.ap)

    def stage_merge(self, l, xsrc, xbufs):
        self.stage_begin()
        I = self.I
        A = self.alloc
        op = self.op
        wbr = A([128, 12, 1024], BF16, 'wbr', nbufs=6)
        wout = A([128, 8, 1024], BF16, 'wout', nbufs=4)
        stg = [A([128, 2, 1024], F32, f'stg{i}') for i in range(1)]
        wb_src = I['w_branch'][l].rearrange('n (k p) d -> p (n k) d', p=128)
        wo_src = I['w_out'][l].rearrange('(k p) d -> p k d', p=128)
        for i in range(6):
            st_ = stg[0]
            self.P.dma('sp', st_.ap, wb_src[:, 2 * i:2 * i + 2, :], writes=_bufs([st_]))
            op('dve' if i % 2 else 'pool', 'tensor_copy', [st_], [wbr.bufs[i]], out=wbr[:, 2 * i:2 * i + 2, :], in_=st_.ap)
        for i in range(4):
            st_ = stg[0]
            self.P.dma('sp', st_.ap, wo_src[:, 2 * i:2 * i + 2, :], writes=_bufs([st_]))
            op('dve' if i % 2 else 'pool', 'tensor_copy', [st_], [wout.bufs[i]], out=wout[:, 2 * i:2 * i + 2, :], in_=st_.ap)
        bg = A([128, 3072], F32, 'bg'); self.load(bg, I['b_gate'][l].rearrange('n d -> (n d)').partition_broadcast(128))
        lg = A([128, D], F32, 'l1g'); self.load(lg, I['ln1_g'][l].partition_broadcast(128))
        lb = A([128, D], F32, 'l1b'); self.load(lb, I['ln1_b'][l].partition_broadcast(128))
        pg = A([128, 3072], F32, 'pg')
        mg = A([128, D], F32, 'mg'); tmp = A([128, D], F32, 'tmp')
        mgb = A([128, D], BF16, 'mgb'); mgT = A([128, 8, 128], BF16, 'mgT')
        xt_ = A([128, D], F32, 'xt'); xb = A([128, D], BF16, 'xb')
        stat = A([128, 4], F32, 'stat')
        pp = [self.pbank(2 * i, 2, F32) for i in range(3)]
        pt = self.pbank(6, 1, BF16, [128, 8, 128])
        pt2 = self.pbank(7, 1, BF16, [128, 8, 128])
        for t in range(NT):
            r0 = 1 + t * 128
            for q_ in range(3):
                self.P.dma('sp', pg[:, q_ * 1024:(q_ + 1) * 1024], self.Pd[r0:r0 + 128, C_PG + q_ * 1024:C_PG + (q_ + 1) * 1024],
                           reads=self.pcols(t, C_PG, IN_PROJ), writes=_bufs([pg]))
            self.load(xt_, xsrc[t * 128:(t + 1) * 128, :], reads=[xbufs[t]] if xbufs else [])
            op('dve', 'tensor_tensor', [pg, bg], [pg], out=pg.ap, in0=pg.ap, in1=bg.ap, op=ALU.add)
            op('act', 'activation', [pg], [pg], out=pg.ap, in_=pg.ap, func=AF.Sigmoid)
            for n in range(3):
                ps = pp[n]
                for hf in range(2):
                    for k in range(4):
                        op('pe', 'matmul', [self.yT.bufs[t], wbr], [ps], ps[:, hf * 512:(hf + 1) * 512], lhsT=self.yT[:, 4 * n + k, t * 128:(t + 1) * 128],
                           rhs=wbr[:, 4 * n + k, hf * 512:(hf + 1) * 512], start=(k == 0), stop=(k == 3))
                for hf in range(2):
                    cs = slice(hf * 512, (hf + 1) * 512)
                    gs_ = pg[:, n * 1024 + hf * 512:n * 1024 + (hf + 1) * 512]
                    if n == 0:
                        op('dve', 'tensor_tensor', [ps, pg], [mg], out=mg[:, cs], in0=ps[:, cs], in1=gs_, op=ALU.mult)
                    else:
                        op('dve', 'tensor_tensor', [ps, pg], [tmp], out=tmp[:, cs], in0=ps[:, cs], in1=gs_, op=ALU.mult)
                if n:
                    op('pool', 'tensor_tensor', [mg, tmp], [mg], out=mg.ap, in0=mg.ap, in1=tmp.ap, op=ALU.add)
            op('act', 'copy', [mg], [mgb], out=mgb.ap, in_=mg.ap)
            for k in range(8):
                op('pe', 'transpose', [mgb, self.identb], [pt], out=pt[:, k, :], in_=mgb[:, k * 128:(k + 1) * 128], identity=self.identb.ap)
            op('act', 'copy', [pt], [mgT], out=mgT.ap, in_=pt.ap)
            ps = pp[0]
            for hf in range(2):
                for k in range(8):
                    op('pe', 'matmul', [mgT, wout], [ps], ps[:, hf * 512:(hf + 1) * 512], lhsT=mgT[:, k, :], rhs=wout[:, k, hf * 512:(hf + 1) * 512], start=(k == 0), stop=(k == 7))
            for hf in range(2):
                cs = slice(hf * 512, (hf + 1) * 512)
                op('dve', 'scalar_tensor_tensor', [xt_, ps], [tmp], out=tmp[:, cs], in0=xt_[:, cs], scalar=ALPHA, in1=ps[:, cs], op0=ALU.mult, op1=ALU.add)
            self.layer_norm(tmp, lg, lb, xt_, stat)
            self.P.dma('pool', self.X1d[t * 128:(t + 1) * 128, :], xt_.ap, reads=_bufs([xt_]), writes=[self.X1b[t]])
            self.to_xT(xt_, xb, pt2, t)

    def stage_mlp(self, l, dst, dst_bufs):
        self.stage_begin()
        I = self.I
        A = self.alloc
        op = self.op
        lg = A([128, D], F32, 'l2g'); self.load(lg, I['ln2_g'][l].partition_broadcast(128))
        lb = A([128, D], F32, 'l2b'); self.load(lb, I['ln2_b'][l].partition_broadcast(128))
        hT = A([128, 32, 384], BF16, 'hT')
        stg = A([128, 4096], F32, 'mstg', nbufs=2)
        wup = [A([128, 8, 512], BF16, f'wup{i}', nbufs=2) for i in range(2)]
        wdn = [A([128, 4, 1024], BF16, f'wdn{i}', nbufs=2) for i in range(2)]
        rl = A([128, 384], F32, 'rl')
        xt_ = A([128, D], F32, 'x1t'); z = A([128, D], F32, 'z2'); xb = A([128, D], BF16, 'x2b')
        stat = A([128, 4], F32, 'stat2')
        acc = [self.pbank(i, 1, F32) for i in range(6)]
        pu = [self.pbank(6, 1, F32), self.pbank(7, 1, F32)]
        ptb = self.pbank(7, 1, BF16, [128, 8, 128])
        w_up = I['w_up'][l]
        w_dn = I['w_down'][l]
        for cg in range(8):
            wu = wup[cg % 2]
            src = w_up[:, cg * 512:(cg + 1) * 512].rearrange('(k p) n -> p k n', p=128)
            sv = stg.ap.rearrange('p (k n) -> p k n', k=8)
            for hf in range(2):
                self.P.dma('sp', sv[:, 4 * hf:4 * hf + 4, :], src[:, 4 * hf:4 * hf + 4, :], writes=[stg.bufs[hf]])
                op('pool' if hf else 'dve', 'tensor_copy', [stg.bufs[hf]], [wu.bufs[hf]], out=wu[:, 4 * hf:4 * hf + 4, :], in_=sv[:, 4 * hf:4 * hf + 4, :])
            self.P.dma('pool', self.WUd[cg], wu.ap.rearrange('p k n -> p (k n)'), reads=_bufs([wu]), writes=[self.WUb[cg]])
        for cg in range(8):
            wd = wdn[cg % 2]
            src = w_dn[cg * 512:(cg + 1) * 512, :].rearrange('(j p) d -> p j d', p=128)
            sv = stg.ap.rearrange('p (j d) -> p j d', j=4)
            for hf in range(2):
                self.P.dma('sp', sv[:, 2 * hf:2 * hf + 2, :], src[:, 2 * hf:2 * hf + 2, :], writes=[stg.bufs[hf]])
                op('pool' if hf else 'dve', 'tensor_copy', [stg.bufs[hf]], [wd.bufs[hf]], out=wd[:, 2 * hf:2 * hf + 2, :], in_=sv[:, 2 * hf:2 * hf + 2, :])
            self.P.dma('pool', self.WDd[cg], wd.ap.rearrange('p j d -> p (j d)'), reads=_bufs([wd]), writes=[self.WDb[cg]])
        groups = [list(range(g, min(g + 3, NT))) for g in range(0, NT, 3)]
        for grp in groups:
            t0 = grp[0]
            ntok = len(grp) * 128
            tokc = slice(t0 * 128, t0 * 128 + ntok)
            for cg in range(8):
                wu = wup[cg % 2]
                self.P.dma('sp', wu.ap.rearrange('p k n -> p (k n)'), self.WUd[cg], reads=[self.WUb[cg]], writes=_bufs([wu]))
                for j in range(4):
                    fc = cg * 4 + j
                    ps = pu[fc % 2]
                    for k in range(8):
                        op('pe', 'matmul', [wu] + [self.xT.bufs[t] for t in grp], [ps], ps[:, 0:ntok], lhsT=wu[:, k, j * 128:(j + 1) * 128],
                           rhs=self.xT[:, k, tokc], start=(k == 0), stop=(k == 7))
                    op('act', 'activation', [ps], [rl], out=rl[:, 0:ntok], in_=ps[:, 0:ntok], func=AF.Relu)
                    op('dve', 'tensor_tensor', [rl], [hT], out=hT[:, fc, 0:ntok], in0=rl[:, 0:ntok], in1=rl[:, 0:ntok], op=ALU.mult)
            for cg in range(8):
                wd = wdn[cg % 2]
                self.P.dma('sp', wd.ap.rearrange('p j d -> p (j d)'), self.WDd[cg], reads=[self.WDb[cg]], writes=_bufs([wd]))
                for j in range(4):
                    fc = cg * 4 + j
                    for ti, t in enumerate(grp):
                        for hf in range(2):
                            a_ = acc[ti * 2 + hf]
                            op('pe', 'matmul', [hT, wd], [a_], a_.ap, lhsT=hT[:, fc, ti * 128:(ti + 1) * 128], rhs=wd[:, j, hf * 512:(hf + 1) * 512],
                               start=(fc == 0), stop=(fc == 31))
            for ti, t in enumerate(grp):
                self.load(xt_, self.X1d[t * 128:(t + 1) * 128, :], reads=[self.X1b[t]])
                for hf in range(2):
                    cs = slice(hf * 512, (hf + 1) * 512)
                    op('dve', 'scalar_tensor_tensor', [xt_, acc[ti * 2 + hf]], [z], out=z[:, cs], in0=xt_[:, cs], scalar=ALPHA, in1=acc[ti * 2 + hf].ap, op0=ALU.mult, op1=ALU.add)
                self.layer_norm(z, lg, lb, xt_, stat)
                self.P.dma('pool', dst[t * 128:(t + 1) * 128, :], xt_.ap, reads=_bufs([xt_]), writes=[dst_bufs[t]] if dst_bufs else [])
                if l + 1 < DEPTH:
                    self.to_xT(xt_, xb, ptb, t)

    def stage_moba(self, l):
        self.stage_begin(over_xT=True)
        I = self.I
        A = self.alloc
        op = self.op
        KT = A([128, 4, 2048], BF16, 'KT')
        KT.bufs = [Buf(f'KT{t}') for t in range(16)]
        Va = A([128, 16, 8, 65], BF16, 'Va')
        Va.bufs = [Buf(f'Va{t}') for t in range(16)]
        KM = A([128, 4, 8], F32, 'KM')
        KMb = A([128, 4, 8], BF16, 'KMb')
        min_ = [A([128, 1536], F32, f'min{i}') for i in range(2)]
        rp = A([128, 64], F32, 'rp')
        qr = A([128, 512], F32, 'qr'); kr = A([128, 512], F32, 'kr'); t1 = A([128, 512], F32, 't1')
        qb = A([128, 512], BF16, 'qb'); kb = A([128, 512], BF16, 'kb')
        QT = A([128, 4, 128], BF16, 'QT')
        cm = A([128, 128], F32, 'cmf'); self.load(cm, I['c_cmask'])
        cmb = A([128, 128], BF16, 'cmb'); op('dve', 'tensor_copy', [cm], [cmb], out=cmb.ap, in_=cm.ap)
        ef = A([8, 1024], F32, 'ef'); self.load(ef, I['c_eblk'])
        eb = A([8, 8, 128], BF16, 'eb'); op('dve', 'tensor_copy', [ef], [eb], out=eb.ap.rearrange('p a b -> p (a b)'), in_=ef.ap)
        pastneg = A([128, 64], F32, 'pastneg'); self.load(pastneg, I['c_pastneg'])
        gate = A([128, 8, 8], F32, 'gate'); top8 = A([128, 8, 8], F32, 'top8'); sel = A([128, 8, 8], F32, 'sel'); sel2 = A([128, 8, 8], F32, 'sel2')
        nsT = A([8, 8, 128], BF16, 'nsT')
        PT = [A([128, 4, 128], BF16, f'PT{i}') for i in range(2)]
        osb = A([128, 8, 65], F32, 'osb'); rden = A([128, 8], F32, 'rden'); ymb = A([128, 512], BF16, 'ymb')
        for t in range(16):
            op('pool', 'memset', [], [Va.bufs[t]], Va[:, t, :, 64:65], 1.0)
        ps_tr = self.pbank(0, 1, BF16, [128, 8, 128])
        ps_g = self.pbank(1, 1, F32)
        ps_ns = self.pbank(6, 2, F32)
        ps_st = [self.pbank(2, 1, F32), self.pbank(3, 1, F32)]
        ps_o = [self.pbank(4, 1, F32), self.pbank(5, 1, F32)]
        ps_y = self.pbank(6, 1, BF16, [128, 4, 128])
        idb = self.identb
        for t in range(NT):
            mi = min_[t % 2]
            r0 = 1 + t * 128
            self.load(mi, self.Pd[r0:r0 + 128, C_MQ:C_MQ + 1536], reads=self.pcols(t, C_MQ, C_MQ + 1536))
            self.load(rp, I['c_rope'][t * 128:(t + 1) * 128, :])
            for (c0, dst, scl) in ((0, qr, 0.125), (512, kr, 1.0)):
                x1 = mi[:, c0:c0 + 512].rearrange('p (h two d) -> p h two d', h=8, two=2)[:, :, 0, :]
                x2 = mi[:, c0:c0 + 512].rearrange('p (h two d) -> p h two d', h=8, two=2)[:, :, 1, :]
                d1 = dst.ap.rearrange('p (h two d) -> p h two d', h=8, two=2)[:, :, 0, :]
                d2 = dst.ap.rearrange('p (h two d) -> p h two d', h=8, two=2)[:, :, 1, :]
                tv1 = t1.ap.rearrange('p (h two d) -> p h two d', h=8, two=2)[:, :, 0, :]
                tv2 = t1.ap.rearrange('p (h two d) -> p h two d', h=8, two=2)[:, :, 1, :]
                cosb = rp[:, 0:32].unsqueeze(1).to_broadcast([128, 8, 32])
                sinb = rp[:, 32:64].unsqueeze(1).to_broadcast([128, 8, 32])
                e1, e2 = ('dve', 'pool')
                op(e1, 'tensor_tensor', [mi, rp], [dst], out=d1, in0=x1, in1=cosb, op=ALU.mult)
                op(e2, 'tensor_tensor', [mi, rp], [t1], out=tv1, in0=x2, in1=sinb, op=ALU.mult)
                op(e1, 'tensor_tensor', [mi, rp], [dst], out=d2, in0=x2, in1=cosb, op=ALU.mult)
                op(e2, 'tensor_tensor', [mi, rp], [t1], out=tv2, in0=x1, in1=sinb, op=ALU.mult)
                op(e1, 'tensor_tensor', [dst, t1], [dst], out=d1, in0=d1, in1=tv1, op=ALU.subtract)
                op(e1, 'tensor_tensor', [dst, t1], [dst], out=d2, in0=d2, in1=tv2, op=ALU.add)
            self.P.dma('pool', self.O['k_out'][l, t * 128:(t + 1) * 128, :], kr.ap, reads=_bufs([kr]))
            self.P.dma('pool', self.O['v_out'][l, t * 128:(t + 1) * 128, :], mi[:, 1024:1536], reads=_bufs([mi]))
            if t == 16:
                if 'msamp' not in SKIP:
                    self.moba_sample(l, locals())
                continue
            op('act', 'activation', [qr], [qb], out=qb.ap, in_=qr.ap, func=AF.Copy, scale=0.125)
            op('act', 'copy', [kr], [kb], out=kb.ap, in_=kr.ap)
            op('pool', 'tensor_copy', [mi], [Va.bufs[t]], out=Va[:, t, :, 0:64], in_=mi[:, 1024:1536].rearrange('p (h d) -> p h d', h=8))
            for p_ in range(4):
                op('pe', 'transpose', [qb, idb], [ps_tr], out=ps_tr[:, p_, :], in_=qb[:, p_ * 128:(p_ + 1) * 128], identity=idb.ap)
                op('pe', 'transpose', [kb, idb], [ps_tr], out=ps_tr[:, 4 + p_, :], in_=kb[:, p_ * 128:(p_ + 1) * 128], identity=idb.ap)
            op('act', 'copy', [ps_tr], [QT], out=QT.ap, in_=ps_tr[:, 0:4, :])
            op('dve', 'tensor_copy', [ps_tr], [KT.bufs[t]], out=KT[:, :, t * 128:(t + 1) * 128], in_=ps_tr[:, 4:8, :])
            b = t // 2
            if t % 2 == 1:
                op('dve', 'tensor_reduce', [KT.bufs[t - 1], KT.bufs[t]], [KM], out=KM[:, :, b:b + 1], in_=KT[:, :, b * 256:(b + 1) * 256], axis=AX.X, op=ALU.add)
                op('dve', 'tensor_copy', [KM], [KMb], out=KMb[:, :, b:b + 1], in_=KM[:, :, b:b + 1])
            hp = lambda tl, h: tl[(h % 2) * 64:(h % 2) * 64 + 64, h // 2]
            if b > 0:
                for h in range(8):
                    op('pe', 'matmul', [QT, KMb], [ps_g], ps_g[:, h * 8:h * 8 + b], lhsT=hp(QT, h)[:, :], rhs=hp(KMb, h)[:, 0:b], start=True, stop=True)
                op('dve', 'memset', [], [gate], gate.ap, -1e30)
                op('dve', 'tensor_copy', [ps_g], [gate], out=gate[:, :, 0:b], in_=ps_g[:, 0:64].rearrange('p (h n) -> p h n', h=8)[:, :, 0:b])
                for h in range(8):
                    op('dve', 'max', [gate], [top8], out=top8[:, h, :], in_=gate[:, h, :])
                op('dve', 'tensor_tensor', [gate, top8], [sel], out=sel.ap, in0=gate.ap, in1=top8[:, :, 2:3].to_broadcast([128, 8, 8]), op=ALU.is_ge)
                op('dve', 'tensor_scalar', [gate], [sel2], out=sel2.ap, in0=gate.ap, scalar1=-1e29, scalar2=None, op0=ALU.is_gt)
                op('dve', 'tensor_tensor', [sel, sel2], [sel], out=sel.ap, in0=sel.ap, in1=sel2.ap, op=ALU.mult)
                op('dve', 'tensor_scalar', [sel], [sel], out=sel.ap, in0=sel.ap, scalar1=-1.0, scalar2=-NEG, op0=ALU.add, op1=ALU.mult)
                for h in range(8):
                    op('pe', 'transpose', [sel, self.ident], [ps_ns], out=ps_ns[0:8, h * 128:(h + 1) * 128], in_=sel[:, h, :], identity=self.ident.ap)
                for hf in range(2):
                    op('act', 'copy', [ps_ns], [nsT], out=nsT[:, 4 * hf:4 * hf + 4, :].rearrange('p a b -> p (a b)'), in_=ps_ns[0:8, 512 * hf:512 * hf + 512])
            nkc = t + 1
            for h in range(8):
                po = ps_o[h // 4]
                ogrp = po[:, (h % 4) * 65:(h % 4) * 65 + 65]
                for g0 in range(0, nkc, 4):
                    g1 = min(g0 + 4, nkc)
                    pst = ps_st[(h * 5 + g0 // 4) % 2]
                    ptile = PT[(h * 5 + g0 // 4) % 2]
                    for kc in range(g0, g1):
                        dst_ = pst[:, (kc - g0) * 128:(kc - g0 + 1) * 128]
                        n = kc // 2
                        extra = (kc == t) or (n < b)
                        op('pe', 'matmul', [KT.bufs[kc], QT], [pst], dst_, lhsT=hp(KT, h)[:, kc * 128:(kc + 1) * 128], rhs=hp(QT, h)[:, :], start=True, stop=not extra)
                        if kc == t:
                            op('pe', 'matmul', [idb, cmb], [pst], dst_, lhsT=idb.ap, rhs=cmb.ap, start=False, stop=True)
                        elif n < b:
                            op('pe', 'matmul', [eb, nsT], [pst], dst_, lhsT=eb[:, n, :], rhs=nsT[:, h, :], start=False, stop=True)
                    w_ = (g1 - g0) * 128
                    op('act', 'activation', [pst], [ptile], out=ptile.ap.rearrange('p a b -> p (a b)')[:, 0:w_], in_=pst[:, 0:w_], func=AF.Exp)
                    for kc in range(g0, g1):
                        op('pe', 'matmul', [ptile, Va.bufs[kc]], [po], ogrp, lhsT=ptile[:, kc - g0, :], rhs=Va[:, kc, h, :], start=(kc == 0), stop=(kc == nkc - 1))
            for hf in range(2):
                op('act', 'copy', [ps_o[hf]], [osb], out=osb[:, 4 * hf:4 * hf + 4, :].rearrange('p a b -> p (a b)'), in_=ps_o[hf][:, 0:260])
            op('dve', 'reciprocal', [osb], [rden], out=rden.ap, in_=osb[:, :, 64])
            op('dve', 'tensor_tensor', [osb, rden], [ymb], out=ymb.ap.rearrange('p (h d) -> p h d', h=8), in0=osb[:, :, 0:64], in1=rden.ap.unsqueeze(2).to_broadcast([128, 8, 64]), op=ALU.mult)
            for p_ in range(4):
                op('pe', 'transpose', [ymb, idb], [ps_y], out=ps_y[:, p_, :], in_=ymb[:, p_ * 128:(p_ + 1) * 128], identity=idb.ap)
            op('act', 'copy', [ps_y], [self.yT.bufs[t]], out=self.yT[:, 4:8, t * 128:(t + 1) * 128], in_=ps_y.ap)

    def moba_sample(self, l, L):
        I = self.I
        A = self.alloc
        op = self.op
        qr, kr, mi, qb, kb, QT, KT, Va, KM, KMb = (L[k] for k in ('qr', 'kr', 'mi', 'qb', 'kb', 'QT', 'KT', 'Va', 'KM', 'KMb'))
        eb, gate, top8, sel, osb, rden, ymb = (L[k] for k in ('eb', 'gate', 'top8', 'sel', 'osb', 'rden', 'ymb'))
        ps_tr, ps_g, ps_ns, ps_st, ps_o, ps_y, idb = (L[k] for k in ('ps_tr', 'ps_g', 'ps_ns', 'ps_st', 'ps_o', 'ps_y', 'idb'))
        hp = lambda tl, h: tl[(h % 2) * 64:(h % 2) * 64 + 64, h // 2]
        KTs = A([128, 4, 128], BF16, 'KTs'); Vs = A([128, 8, 65], BF16, 'Vs')
        dmf = A([128, 128], F32, 'dmf'); self.load(dmf, I['c_dmask'])
        dmb = A([128, 128], BF16, 'dmb'); op('dve', 'tensor_copy', [dmf], [dmb], out=dmb.ap, in_=dmf.ap)
        iota = A([128, 16], F32, 'iota'); self.load(iota, I['c_iota'])
        idxf = A([128, 16], F32, 'idxf')
        cmk = A([8, 256], F32, 'cmk'); self.load(cmk, I['c_cmk'])
        ptb = A([128, 16], I32, 'ptb'); idx = A([128, 16], I32, 'idx')
        kpg = A([128, 8, 512], F32, 'kpg', nbufs=8); vpg = A([128, 8, 512], F32, 'vpg', nbufs=8)
        kpb = A([128, 8, 512], BF16, 'kpb')
        nsTb = A([8, 8, 16], BF16, 'nsTb')
        PTs = [A([128, 512], BF16, f'PTs{i}') for i in range(2)]
        ck = I['cache_k'].rearrange('l n r h d -> (l n r) (h d)')
        cv = I['cache_v'].rearrange('l n r h d -> (l n r) (h d)')
        nrows = ck.shape[0]
        lbase = l * (nrows // DEPTH)
        op('act', 'activation', [qr], [qb], out=qb.ap, in_=qr.ap, func=AF.Copy, scale=0.125)
        op('act', 'copy', [kr], [kb], out=kb.ap, in_=kr.ap)
        op('pool', 'memset', [], [Vs], Vs[:, :, 64:65], 1.0)
        op('pool', 'tensor_copy', [mi], [Vs], out=Vs[:, :, 0:64], in_=mi[:, 1024:1536].rearrange('p (h d) -> p h d', h=8))
        for p_ in range(4):
            op('pe', 'transpose', [qb, idb], [ps_tr], out=ps_tr[:, p_, :], in_=qb[:, p_ * 128:(p_ + 1) * 128], identity=idb.ap)
            op('pe', 'transpose', [kb, idb], [ps_tr], out=ps_tr[:, 4 + p_, :], in_=kb[:, p_ * 128:(p_ + 1) * 128], identity=idb.ap)
        op('act', 'copy', [ps_tr], [QT], out=QT.ap, in_=ps_tr[:, 0:4, :])
        op('dve', 'tensor_copy', [ps_tr], [KTs], out=KTs.ap, in_=ps_tr[:, 4:8, :])
        zb = A([128, 16], BF16, 'zb'); op('dve', 'memset', [], [zb], zb.ap, 0.0)
        for hf in range(2):
            op('pe', 'matmul', [zb, Vs], [ps_o[hf]], ps_o[hf][0:16, 0:260], lhsT=zb.ap, rhs=Vs[:, 0:4, :].rearrange('p a b -> p (a b)'), start=True, stop=False)
        for hq in range(2):
            pst, ptile = ps_st[hq], PTs[hq]
            for h4 in range(4):
                h = hq * 4 + h4
                dst_ = pst[:, h4 * 16:(h4 + 1) * 16]
                op('pe', 'matmul', [KTs, QT], [pst], dst_, lhsT=hp(KTs, h)[:, :], rhs=hp(QT, h)[:, 0:16], start=True, stop=False)
                op('pe', 'matmul', [idb, dmb], [pst], dst_, lhsT=idb.ap, rhs=dmb[:, 0:16], start=False, stop=True)
            op('act', 'activation', [pst], [ptile], out=ptile[:, 0:64], in_=pst[:, 0:64], func=AF.Exp)
            for h4 in range(4):
                h = hq * 4 + h4
                po = ps_o[h // 4]
                op('pe', 'matmul', [ptile, Vs], [po], po[0:16, (h % 4) * 65:(h % 4) * 65 + 65], lhsT=ptile[:, h4 * 16:(h4 + 1) * 16], rhs=Vs[:, h, :], start=False, stop=False)
        for sb in range(16):
            self.load(ptb, I['page_table'][sb].partition_broadcast(128))
            op('dve', 'tensor_copy', [ptb], [idxf], out=idxf.ap, in_=ptb.ap)
            op('dve', 'tensor_scalar', [idxf], [idxf], out=idxf.ap, in0=idxf.ap, scalar1=128.0, scalar2=float(lbase), op0=ALU.mult, op1=ALU.add)
            op('dve', 'tensor_tensor', [idxf, iota], [idxf], out=idxf.ap, in0=idxf.ap, in1=iota.ap, op=ALU.add)
            op('dve', 'tensor_copy', [idxf], [idx], out=idx.ap, in_=idxf.ap)
            for half in range(2):
                for j in range(8):
                    pg_ = half * 8 + j
                    for (src, dstt) in ((ck, kpg), (cv, vpg)):
                        o_ = dstt[:, j, :]
                        off = idx[:, pg_:pg_ + 1]
                        self.P.dma('pool', None, None, reads=_bufs([idx]), writes=[dstt.bufs[j]],
                                   _fn=lambda e, o_=o_, src=src, off=off: e.indirect_dma_start(out=o_, out_offset=None, in_=src, in_offset=bass.IndirectOffsetOnAxis(ap=off, axis=0)))
                op('dve', 'tensor_copy', [kpg.bufs[j] for j in range(4)], [kpb], out=kpb[:, 0:4, :], in_=kpg[:, 0:4, :])
                op('act', 'copy', [kpg.bufs[j] for j in range(4, 8)], [kpb], out=kpb[:, 4:8, :], in_=kpg[:, 4:8, :])
                for j in range(8):
                    pg_ = half * 8 + j
                    op('pool', 'tensor_copy', [vpg.bufs[j]], [Va.bufs[pg_]], out=Va[:, pg_, :, 0:64], in_=vpg[:, j, :].rearrange('p (h d) -> p h d', h=8))
                for j2 in range(4):
                    for g in range(2):
                        j = 2 * j2 + g
                        for p_ in range(4):
                            op('pe', 'transpose', [kpb, idb], [ps_tr], out=ps_tr[:, 4 * g + p_, :], in_=kpb[:, j, p_ * 128:(p_ + 1) * 128], identity=idb.ap)
                    pg0 = half * 8 + 2 * j2
                    op('dve', 'tensor_copy', [ps_tr], [KT.bufs[pg0], KT.bufs[pg0 + 1]],
                       out=KT[:, :, pg0 * 128:(pg0 + 2) * 128].rearrange('p a (g c) -> p a g c', g=2),
                       in_=ps_tr.ap.rearrange('p (g a) c -> p a g c', g=2))
            op('dve', 'tensor_reduce', [KT], [KM], out=KM.ap, in_=KT.ap.rearrange('p a (n c) -> p a n c', n=8), axis=AX.X, op=ALU.add)
            op('dve', 'tensor_copy', [KM], [KMb], out=KMb.ap, in_=KM.ap)
            for h in range(8):
                op('pe', 'matmul', [QT, KMb], [ps_g], ps_g[0:16, h * 8:h * 8 + 8], lhsT=hp(QT, h)[:, 0:16], rhs=hp(KMb, h)[:, :], start=True, stop=True)
            op('dve', 'tensor_copy', [ps_g], [gate], out=gate[0:16].rearrange('p a b -> p (a b)'), in_=ps_g[0:16, 0:64])
            for h in range(8):
                op('dve', 'max', [gate], [top8], out=top8[0:16, h, :], in_=gate[0:16, h, :])
            op('dve', 'tensor_tensor', [gate, top8], [sel], out=sel[0:16], in0=gate[0:16], in1=top8[0:16, :, 2:3].to_broadcast([16, 8, 8]), op=ALU.is_ge)
            op('dve', 'tensor_scalar', [sel], [sel], out=sel[0:16], in0=sel[0:16], scalar1=-1.0, scalar2=-NEG, op0=ALU.add, op1=ALU.mult)
            for h in range(8):
                op('pe', 'transpose', [sel, self.ident], [ps_ns], out=ps_ns[0:8, h * 16:(h + 1) * 16], in_=sel[0:16, h, :], identity=self.ident[0:16, 0:16])
            op('dve', 'tensor_tensor', [ps_ns, cmk], [nsTb], out=nsTb.ap, in0=ps_ns[0:8, 0:128].rearrange('p (h q) -> p h q', h=8),
               in1=cmk[:, sb * 16:(sb + 1) * 16].unsqueeze(1).to_broadcast([8, 8, 16]), op=ALU.add)
            for hq in range(4):
                pst, ptile = ps_st[hq % 2], PTs[hq % 2]
                for h2 in range(2):
                    h = hq * 2 + h2
                    for kc in range(16):
                        dst_ = pst[:, h2 * 256 + kc * 16:h2 * 256 + (kc + 1) * 16]
                        op('pe', 'matmul', [KT.bufs[kc], QT], [pst], dst_, lhsT=hp(KT, h)[:, kc * 128:(kc + 1) * 128], rhs=hp(QT, h)[:, 0:16], start=True, stop=False)
                        op('pe', 'matmul', [eb, nsTb], [pst], dst_, lhsT=eb[:, kc // 2, :], rhs=nsTb[:, h, :], start=False, stop=True)
                op('act', 'activation', [pst], [ptile], out=ptile.ap, in_=pst.ap, func=AF.Exp)
                for h2 in range(2):
                    h = hq * 2 + h2
                    po = ps_o[h // 4]
                    for kc in range(16):
                        op('pe', 'matmul', [ptile, Va.bufs[kc]], [po], po[0:16, (h % 4) * 65:(h % 4) * 65 + 65], lhsT=ptile[:, h2 * 256 + kc * 16:h2 * 256 + (kc + 1) * 16],
                           rhs=Va[:, kc, h, :], start=False, stop=(sb == 15 and kc == 15 and h % 4 == 3))
        for hf in range(2):
            op('act', 'copy', [ps_o[hf]], [osb], out=osb[0:16, 4 * hf:4 * hf + 4, :].rearrange('p a b -> p (a b)'), in_=ps_o[hf][0:16, 0:260])
        op('dve', 'reciprocal', [osb], [rden], out=rden[0:16], in_=osb[0:16, :, 64])
        op('dve', 'tensor_tensor', [osb, rden], [ymb], out=ymb[0:16].rearrange('p (h d) -> p h d', h=8), in0=osb[0:16, :, 0:64], in1=rden[0:16].unsqueeze(2).to_broadcast([16, 8, 64]), op=ALU.mult)
        for p_ in range(4):
            op('pe', 'transpose', [ymb, idb], [ps_y], out=ps_y[:, p_, 0:16], in_=ymb[0:16, p_ * 128:(p_ + 1) * 128], identity=idb[0:16, 0:16])
        op('act', 'copy', [ps_y], [self.yT.bufs[16]], out=self.yT[:, 4:8, 2048:2064], in_=ps_y[:, :, 0:16])

    def dump_dbg(self):
        self.P.nocut = True
        self.P.barrier()
        self.P.dma('pool', self.O['dbg_yT'], self.yT.ap.rearrange('p a b -> p (a b)'), reads=_bufs([self.yT]))

    def build(self):
        st = self.stages
        self.setup()
        self.stage_xT(self.I['xin'], None)
        for l in range(DEPTH):
            if 'proj' in st or 'all' in st:
                self.stage_proj(l)
            if 'gla' in st or 'all' in st:
                self.stage_gla(l)
            if 'rwkv' in st or 'all' in st:
                self.stage_rwkv(l)
            if 'moba' in st or 'all' in st:
                self.stage_moba(l)
            if 'fin' in st or 'all' in st:
                self.stage_merge(l, self.I['xin'] if l == 0 else self.X2d, None if l == 0 else self.X2b)
                last = (l == DEPTH - 1) or ('l0' in st)
                self.stage_mlp(l, self.O['y'] if last else self.X2d, None if last else self.X2b)
            if 'l0' in st:
                break
        if 'dbg' in st:
            self.dump_dbg()
        self.P.emit()
        return self.nc


_CACHE = {}


def kernel(**inputs):
    n_cores = 8
    n_pool = int(inputs['cache_k'].shape[1])
    key = ('all', n_pool)
    if key not in _CACHE:
        _CACHE[key] = Builder(stages=('all',), n_pool=n_pool).build()
    nc = _CACHE[key]
    consts = make_consts()
    f32 = lambda a: np.ascontiguousarray(np.asarray(a, dtype=np.float32))
    ck = f32(inputs['cache_k'])
    cv = f32(inputs['cache_v'])
    in_maps = []
    for c in range(n_cores):
        xin = np.zeros((NTOK, D), np.float32)
        xin[:2048] = inputs['x_prompt'][c]
        xin[2048:2064] = inputs['x_sample'][16 * c:16 * c + 16, 0]
        m = {'xin': xin}
        for n, s in WEIGHTS:
            m[n] = f32(inputs[n])
        m['state_gla'] = f32(inputs['state_gla'][:, 16 * c:16 * c + 16])
        m['state_wkv'] = f32(inputs['state_wkv'][:, 16 * c:16 * c + 16])
        m['state_shift'] = f32(inputs['state_shift'][:, 16 * c:16 * c + 16])
        m['cache_k'] = ck
        m['cache_v'] = cv
        m['page_table'] = np.ascontiguousarray(np.asarray(inputs['page_table'][16 * c:16 * c + 16], dtype=np.int32))
        for n, v in consts.items():
            m['c_' + n] = v
        in_maps.append(m)
    res = run_bass_kernel_spmd(nc, in_maps, core_ids=list(range(n_cores)))
    R = res.results
    y_p = np.stack([R[c]['y'][:2048] for c in range(n_cores)])
    y_s = np.concatenate([R[c]['y'][2048:2064] for c in range(n_cores)])[:, None, :]
    k_p = np.stack([R[c]['k_out'][:, :2048].reshape(DEPTH, 2048, 8, 64) for c in range(n_cores)], axis=1)
    v_p = np.stack([R[c]['v_out'][:, :2048].reshape(DEPTH, 2048, 8, 64) for c in range(n_cores)], axis=1)
    k_s = np.concatenate([R[c]['k_out'][:, 2048:2064].reshape(DEPTH, 16, 1, 8, 64) for c in range(n_cores)], axis=1)
    v_s = np.concatenate([R[c]['v_out'][:, 2048:2064].reshape(DEPTH, 16, 1, 8, 64) for c in range(n_cores)], axis=1)
    gla_p = np.stack([R[c]['gla_p'] for c in range(n_cores)], axis=1)
    gla_s = np.concatenate([R[c]['gla_s'] for c in range(n_cores)], axis=1)
    wkv_p = np.stack([R[c]['wkv_p'] for c in range(n_cores)], axis=1)
    wkv_s = np.concatenate([R[c]['wkv_s'] for c in range(n_cores)], axis=1)
    sh_p = np.stack([R[c]['shift'][:, 0] for c in range(n_cores)], axis=1)
    sh_s = np.concatenate([R[c]['shift'][:, 1:17] for c in range(n_cores)], axis=1)
    outs = (y_p, y_s, k_p, v_p, k_s, v_s, gla_p, gla_s, wkv_p, wkv_s, sh_p, sh_s)
    return tuple(np.ascontiguousarray(o, dtype=np.float32) for o in outs)
```

```python
import numpy as np
from contextlib import ExitStack
import concourse.bass as bass
import concourse.mybir as mybir
from concourse.bass_utils import run_bass_kernel_spmd

F32 = mybir.dt.float32
BF16 = mybir.dt.bfloat16
I32 = mybir.dt.int32
ALU = mybir.AluOpType
AF = mybir.ActivationFunctionType
AX = mybir.AxisListType

ENGS = ['pe', 'act', 'dve', 'pool', 'sp']
import os as _os
SKIP = _os.environ.get('SKIP', '').split(',')

D = 1024
NT = 17
NTOK = NT * 128
DEPTH = 2
W = 512
IN_PROJ = 7952
C_GQ, C_GK, C_GV, C_GLOW, C_GOUT = 0, 256, 512, 1024, 1040
C_MQ, C_MK, C_MV = 1552, 2064, 2576
C_PR = 3088
RWKV_PROJ = 1792
C_PG = 4880
ALPHA = (2 * DEPTH) ** 0.25
NEG = -30000.0


class Buf:
    __slots__ = ('name', 'w', 'r', 'excl')

    def __init__(self, name='', excl=False):
        self.name = name
        self.w = None
        self.r = {}
        self.excl = excl


class Op:
    __slots__ = ('stream', 'idx', 'fn', 'waits', 'marked', 'val', 'issuer')

    def __init__(self, stream, idx, fn, issuer):
        self.stream = stream
        self.idx = idx
        self.fn = fn
        self.waits = []
        self.marked = False
        self.val = 0
        self.issuer = issuer


class Tile:
    __slots__ = ('ap', 'bufs')

    def __init__(self, ap, bufs):
        self.ap = ap
        self.bufs = bufs if isinstance(bufs, list) else [bufs]

    def __getitem__(self, key):
        return self.ap[key]


def _bufs(items):
    out = []
    for it in items:
        if it is None:
            continue
        if isinstance(it, Buf):
            out.append(it)
        elif isinstance(it, Tile):
            out.extend(it.bufs)
        else:
            out.extend(_bufs(it))
    return out


class Prog:
    def __init__(self, nc, nslots=18):
        self.nc = nc
        self.es = ExitStack()
        self.nslots = nslots
        self.slotpool = {'sp': list(range(0, 7)), 'pool': list(range(7, 14)), 'act': list(range(14, 18))}
        self.slotrr = {'sp': 0, 'pool': 0, 'act': 0}
        self.streams = {e: [] for e in ENGS}
        for k in range(nslots):
            self.streams[('slot', k)] = []
        self.issue = {e: [] for e in ENGS}
        self.seen = {e: {} for e in ENGS}
        self.pending = {e: [] for e in ENGS}
        self.nextslot = 0

    def _deps(self, issuer, reads, writes):
        deps = {}

        def add(o):
            if o is None:
                return
            cur = deps.get(o.stream)
            if cur is None or cur.idx < o.idx:
                deps[o.stream] = o

        for b in reads:
            add(b.w)
        for b in writes:
            add(b.w)
            for o in b.r.values():
                add(o)
        if self.pending[issuer]:
            for o in self.pending[issuer]:
                add(o)
            self.pending[issuer] = []
        waits = []
        seen = self.seen[issuer]
        for st, o in deps.items():
            if st == 'pe' and issuer == 'pe':
                continue
            if seen.get(st, 0) >= o.idx:
                continue
            seen[st] = o.idx
            waits.append(o)
        return waits

    def _record(self, op, reads, writes):
        for b in reads:
            b.r[op.stream] = op
        for b in writes:
            b.w = op
            b.r = {}

    def _cut(self):
        import os
        cut = int(os.environ.get('CUT', '0'))
        self.nops = getattr(self, 'nops', 0) + 1
        return bool(cut) and self.nops > cut and not getattr(self, 'nocut', False)

    def op(self, eng, fn, reads=(), writes=()):
        if self._cut():
            return None
        reads = _bufs(reads)
        writes = _bufs(writes)
        writes = writes + [b for b in reads if b.excl]
        reads = [b for b in reads if not b.excl]
        waits = self._deps(eng, reads, writes)
        st = self.streams[eng]
        o = Op(eng, len(st) + 1, fn, eng)
        o.waits = waits
        st.append(o)
        self.issue[eng].append(o)
        self._record(o, reads, writes)
        return o

    def dma(self, issuer, out, in_, reads=(), writes=(), **kw):
        if self._cut():
            return None
        reads = _bufs(reads)
        writes = _bufs(writes)
        pool_ = self.slotpool[issuer]
        k = pool_[self.slotrr[issuer] % len(pool_)]
        self.slotrr[issuer] += 1
        stn = ('slot', k)
        st = self.streams[stn]
        waits = self._deps(issuer, reads, writes)
        if st:
            prev = st[-1]
            if self.seen[issuer].get(stn, 0) < prev.idx:
                self.seen[issuer][stn] = prev.idx
                waits.append(prev)
        fn_ = kw.pop('_fn', None)
        o = Op(stn, len(st) + 1, fn_ if fn_ is not None else (lambda e: e.dma_start(out=out, in_=in_, **kw)), issuer)
        o.waits = waits
        st.append(o)
        self.issue[issuer].append(o)
        self._record(o, reads, writes)
        return o

    def barrier(self):
        lasts = [st[-1] for st in self.streams.values() if st]
        for e in ENGS:
            self.pending[e] = list(lasts)

    def emit(self):
        nc = self.nc
        for e in ENGS:
            for o in self.issue[e]:
                for w in o.waits:
                    w.marked = True
        finals = []
        for stn, st in self.streams.items():
            if st:
                st[-1].marked = True
                finals.append(st[-1])
        for stn, st in self.streams.items():
            c = 0
            isslot = not isinstance(stn, str)
            for o in st:
                if isslot:
                    o.marked = True
                if o.marked:
                    c += 1
                o.val = c * (16 if isslot else 1)
        sems = {}
        for stn in self.streams:
            nm = stn if isinstance(stn, str) else f'slot{stn[1]}'
            sems[stn] = self.es.enter_context(nc.semaphore('sem_' + nm))
        prog = self
        self.stats = {str(k): len(v) for k, v in self.streams.items()}

        def run(ename, e):
            for o in prog.issue[ename]:
                for w in o.waits:
                    e.wait_ge(sems[w.stream], w.val)
                ins = o.fn(e)
                if o.marked:
                    ins.then_inc(sems[o.stream], 16 if not isinstance(o.stream, str) else 1)
            if ename == 'sp':
                for f in finals:
                    e.wait_ge(sems[f.stream], f.val)

        with nc.Block() as block:
            @block.tensor
            def _(e):
                run('pe', e)

            @block.scalar
            def _(e):
                run('act', e)

            @block.vector
            def _(e):
                run('dve', e)

            @block.gpsimd
            def _(e):
                run('pool', e)

            @block.sync
            def _(e):
                run('sp', e)
        self.es.close()


def make_consts():
    c = {}
    c['ident'] = np.eye(128, dtype=np.float32)
    j = np.arange(128)[:, None]
    i = np.arange(128)[None, :]
    same = (j // 64) == (i // 64)
    c['tri'] = (same & (j <= i)).astype(np.float32)
    c['tris'] = (same & (j < i)).astype(np.float32)
    c['trist'] = (same & (j > i)).astype(np.float32)
    c['bo'] = same.astype(np.float32)
    bs = np.zeros((128, 2), np.float32)
    bs[:64, 0] = 1
    bs[64:, 1] = 1
    c['bsel'] = bs
    pos = np.concatenate([np.arange(2048), np.full(128, 2048)]).astype(np.float32)
    inv = (10000.0 ** (-np.arange(32, dtype=np.float32) / 32)).astype(np.float32)
    ang = pos[:, None] * inv[None, :]
    c['rope'] = np.concatenate([np.cos(ang), np.sin(ang)], 1).astype(np.float32)
    c['cmask'] = np.where(j > i, NEG, 0.0).astype(np.float32)
    E = np.zeros((8, 8, 128), np.float32)
    for n in range(8):
        E[n, n, :] = 1
    c['eblk'] = E.reshape(8, 1024)
    pm = np.where(np.arange(8)[None, :] < np.arange(8)[:, None], 0.0, -1e30).astype(np.float32)
    c['pastneg'] = np.ascontiguousarray(np.broadcast_to(pm.reshape(1, 64), (128, 64)))
    i16 = np.eye(16, dtype=np.float32)
    c['i16bc'] = np.ascontiguousarray(np.broadcast_to(i16.reshape(1, 256), (128, 256)))
    c['iota'] = np.ascontiguousarray(np.broadcast_to(np.arange(128, dtype=np.float32)[:, None], (128, 16)))
    cmk = np.where(np.eye(16, dtype=bool), 0.0, NEG).astype(np.float32)
    c['cmk'] = np.ascontiguousarray(np.broadcast_to(cmk.reshape(1, 256), (8, 256)))
    c['dmask'] = np.where(np.eye(128, dtype=bool), 0.0, NEG).astype(np.float32)
    return c


CONST_SHAPES = {k: (v.shape, v.dtype) for k, v in make_consts().items()}

WEIGHTS = [('w_in', [DEPTH, D, IN_PROJ]), ('b_gate', [DEPTH, 3, D]), ('gla_gk_up', [DEPTH, 16, 256]),
           ('gla_gk_bias', [DEPTH, 256]), ('gla_norm_w', [DEPTH, 128]), ('rwkv_mu', [DEPTH, RWKV_PROJ]),
           ('rwkv_w0', [DEPTH, W]), ('rwkv_w_up', [DEPTH, 64, W]), ('rwkv_a0', [DEPTH, W]),
           ('rwkv_a_up', [DEPTH, 64, W]), ('rwkv_g_up', [DEPTH, 128, W]), ('rwkv_k_k', [DEPTH, W]),
           ('rwkv_k_a', [DEPTH, W]), ('rwkv_r_k', [DEPTH, 8, 64]), ('rwkv_ln_w', [DEPTH, W]),
           ('rwkv_ln_b', [DEPTH, W]), ('w_branch', [DEPTH, 3, W, D]), ('w_out', [DEPTH, D, D]),
           ('ln1_g', [DEPTH, D]), ('ln1_b', [DEPTH, D]), ('w_up', [DEPTH, D, 4 * D]),
           ('w_down', [DEPTH, 4 * D, D]), ('ln2_g', [DEPTH, D]), ('ln2_b', [DEPTH, D])]


class Builder:
    def __init__(self, stages=('all',), n_pool=2560):
        self.stages = stages
        nc = bass.Bass("TRN2", target_bir_lowering=False)
        self.nc = nc
        self.P = Prog(nc)
        P = self.P
        din = lambda n, s, dt=F32: nc.dram_tensor(n, list(s), dt, kind="ExternalInput").ap()
        dout = lambda n, s, dt=F32: nc.dram_tensor(n, list(s), dt, kind="ExternalOutput").ap()
        dint = lambda n, s, dt=F32: nc.dram_tensor(n, list(s), dt, kind="Internal").ap()
        self.I = {}
        self.I['xin'] = din('xin', [NTOK, D])
        for n, s in WEIGHTS:
            self.I[n] = din(n, s)
        self.I['state_gla'] = din('state_gla', [DEPTH, 16, 4, 64, 128])
        self.I['state_wkv'] = din('state_wkv', [DEPTH, 16, 8, 64, 64])
        self.I['state_shift'] = din('state_shift', [DEPTH, 16, RWKV_PROJ])
        self.I['cache_k'] = din('cache_k', [DEPTH, n_pool, 128, 8, 64])
        self.I['cache_v'] = din('cache_v', [DEPTH, n_pool, 128, 8, 64])
        self.I['page_table'] = din('page_table', [16, 16], I32)
        for n, (s, dt_) in CONST_SHAPES.items():
            self.I['c_' + n] = din('c_' + n, s, I32 if dt_ == np.int32 else F32)
        self.O = {}
        self.O['y'] = dout('y', [NTOK, D])
        self.O['k_out'] = dout('k_out', [DEPTH, NTOK, W])
        self.O['v_out'] = dout('v_out', [DEPTH, NTOK, W])
        self.O['gla_p'] = dout('gla_p', [DEPTH, 4, 64, 128])
        self.O['gla_s'] = dout('gla_s', [DEPTH, 16, 4, 64, 128])
        self.O['wkv_p'] = dout('wkv_p', [DEPTH, 8, 64, 64])
        self.O['wkv_s'] = dout('wkv_s', [DEPTH, 16, 8, 64, 64])
        self.O['shift'] = dout('shift', [DEPTH, 17, RWKV_PROJ])
        self.Pd = dint('Pscr', [1 + NTOK, IN_PROJ])
        self.Pb = [[Buf(f'P{t}_{g}') for g in range(16)] for t in range(NT)]
        self.Pzero = Buf('Pzero')
        self.X1d = dint('X1scr', [NTOK, D])
        self.X1b = [Buf(f'X1_{t}') for t in range(NT)]
        self.WUd = dint('WUscr', [8, 128, 4096], BF16)
        self.WUb = [Buf(f'WU{i}') for i in range(8)]
        self.WDd = dint('WDscr', [8, 128, 4096], BF16)
        self.WDb = [Buf(f'WD{i}') for i in range(8)]
        self.X2d = dint('X2scr', [NTOK, D])
        self.X2b = [Buf(f'X2_{t}') for t in range(NT)]
        if 'dbg' in stages:
            self.O['dbg_yT'] = dout('dbg_yT', [128, 12 * NTOK], BF16)
        self.ARENA_WORDS = 48000
        self.arena = P.es.enter_context(nc.sbuf_tensor('arena', [128, self.ARENA_WORDS], F32))
        self.psum = P.es.enter_context(nc.psum_tensor('psum', [128, 4096], F32))
        self.aptr = 0
        self.rr = 0
        self.pbufs = [Buf(f'bank{i}', excl=True) for i in range(8)]

    def alloc(self, shape, dtype=F32, name='', nbufs=1):
        parts = shape[0]
        free = int(np.prod(shape[1:]))
        words = free if dtype in (F32, I32) else (free + 1) // 2
        a = self.aptr
        self.aptr += words
        assert self.aptr <= self.ARENA_WORDS, f'arena overflow {self.aptr} ({name})'
        ap = self.arena[0:parts, a:a + words]
        if dtype == BF16:
            ap = ap.bitcast(BF16)
            if free % 2:
                ap = ap[:, 0:free]
        elif dtype == I32:
            ap = ap.bitcast(I32)
        ap = self._shape(ap, shape)
        return Tile(ap, [Buf(name) for _ in range(nbufs)])

    @staticmethod
    def _shape(ap, shape):
        if len(shape) == 2:
            return ap
        if len(shape) == 3:
            return ap.rearrange('p (a b) -> p a b', a=shape[1], b=shape[2])
        if len(shape) == 4:
            return ap.rearrange('p (a b c) -> p a b c', a=shape[1], b=shape[2], c=shape[3])
        raise ValueError

    def pbank(self, bank, nbanks=1, dtype=F32, shape=None, parts=128, name=''):
        ap = self.psum[0:parts, bank * 512:(bank + nbanks) * 512]
        if dtype == BF16:
            ap = ap.bitcast(BF16)
        if shape is not None:
            free = int(np.prod(shape[1:]))
            ap = ap[:, 0:free]
            ap = self._shape(ap, shape)
        return Tile(ap, [self.pbufs[bank + i] for i in range(nbanks)])

    def op(self, eng, method, reads, writes, *a, **kw):
        import os
        if eng == 'pool' and os.environ.get('NOPOOL'):
            eng = 'dve'
        return self.P.op(eng, lambda e: getattr(e, method)(*a, **kw), reads, writes)

    def opn(self, *a, **kw):
        import os
        self.midn += 1
        if self.midn > int(os.environ.get('MIDN', '1000')):
            return None
        return self.op(*a, **kw)

    def load(self, dst, src, reads=(), eng='sp'):
        return self.P.dma(eng, dst.ap if isinstance(dst, Tile) else dst, src, reads=reads, writes=[dst])

    def store(self, dst, src_tile, src_ap=None, writes=(), eng='pool'):
        return self.P.dma(eng, dst, src_ap if src_ap is not None else src_tile.ap, reads=[src_tile], writes=writes)

    def pcols(self, t, c0, c1):
        return [self.Pb[t][g] for g in range(c0 // 512, (c1 - 1) // 512 + 1)]

    def bc_row(self, dram_row_ap, n):
        return dram_row_ap.partition_broadcast(128)

    def rsqrt(self, dst, src, scale, eps, src_ap=None, dst_ap=None):
        da = dst_ap if dst_ap is not None else dst.ap
        sa = src_ap if src_ap is not None else src.ap
        self.op('act', 'activation', [src, self.epsb], [dst], out=da, in_=sa, func=AF.Ln, bias=self.epsb[:, self.epsidx[eps]:self.epsidx[eps] + 1], scale=scale)
        self.op('act', 'activation', [dst], [dst], out=da, in_=da, func=AF.Exp, scale=-0.5)

    def setup(self):
        I = self.I
        self.ident = self.alloc([128, 128], F32, 'ident')
        self.identb = self.alloc([128, 128], BF16, 'identb')
        self.tri = self.alloc([128, 128], F32, 'tri')
        self.tris = self.alloc([128, 128], F32, 'tris')
        self.trist = self.alloc([128, 128], F32, 'trist')
        self.bo = self.alloc([128, 128], F32, 'bo')
        self.bsel = self.alloc([128, 2], F32, 'bsel')
        for nm in ['ident', 'tri', 'tris', 'trist', 'bo', 'bsel']:
            self.load(getattr(self, nm), I['c_' + nm])
        self.op('dve', 'tensor_copy', [self.ident], [self.identb], out=self.identb.ap, in_=self.ident.ap)
        self.epsb = self.alloc([128, 4], F32, 'epsb')
        self.epsidx = {1e-5: 0, 64e-5: 1, 1e-12: 2, 1.0: 3}
        for e_, i_ in self.epsidx.items():
            self.op('dve', 'memset', [], [self.epsb], self.epsb[:, i_:i_ + 1], float(e_))
        self.yT = self.alloc([128, 12, NTOK], BF16, 'yT')
        self.yT.bufs = [Buf(f'yT{t}') for t in range(NT)]
        self.xT_start = self.aptr
        self.xT = self.alloc([128, 8, NTOK], BF16, 'xT')
        self.xT.bufs = [Buf(f'xT{t}') for t in range(NT)]
        if 'dbg' in self.stages:
            self.op('pool', 'memset', [], [self.yT], self.yT.ap, 0.0)
        else:
            self.op('pool', 'memset', [], [self.yT.bufs[16]], self.yT[:, :, 2048:NTOK], 0.0)
        self.pers_end = self.aptr
        z = self.alloc([1, IN_PROJ], F32, 'zrow')
        self.op('dve', 'memset', [], [z], z.ap, 0.0)
        self.P.dma('pool', self.Pd[0:1, :], z.ap, reads=[z], writes=[self.Pzero])
        self.aptr = self.pers_end

    def stage_begin(self, over_xT=False):
        self.P.barrier()
        self.aptr = self.xT_start if over_xT else self.pers_end

    def stage_xT(self, xsrc, xbufs):
        self.stage_begin()
        xs = [self.alloc([128, D], F32, f'xs{i}') for i in range(2)]
        xb = [self.alloc([128, D], BF16, f'xb{i}') for i in range(2)]
        pts = [self.pbank(i, 1, BF16, [128, 8, 128], name=f'ptx{i}') for i in range(2)]
        for t in range(NT):
            s, b, pt = xs[t % 2], xb[t % 2], pts[t % 2]
            self.load(s, xsrc[t * 128:(t + 1) * 128, :], reads=[xbufs[t]] if xbufs else [])
            self.op('dve' if t % 2 else 'pool', 'tensor_copy', [s], [b], out=b.ap, in_=s.ap)
            for k in range(8):
                self.op('pe', 'transpose', [b, self.identb], [pt], out=pt[:, k, :], in_=b[:, k * 128:(k + 1) * 128],
                        identity=self.identb.ap)
            self.op('act', 'copy', [pt], [self.xT.bufs[t]], out=self.xT[:, :, t * 128:(t + 1) * 128], in_=pt.ap)

    def stage_proj(self, l):
        self.stage_begin()
        w_in = self.I['w_in']
        wst = [self.alloc([128, 8, 512], F32, f'wst{i}', nbufs=2) for i in range(2)]
        wbf = [self.alloc([128, 8, 512], BF16, f'wbf{i}', nbufs=2) for i in range(2)]
        ost = [self.alloc([128, 512], F32, f'ost{i}') for i in range(4)]
        pss = [self.pbank(i, 1, F32, name=f'psA{i}') for i in range(4)]
        n = 0
        for g in range(16):
            c0 = g * 512
            cw = min(512, IN_PROJ - c0)
            ws, wb = wst[g % 2], wbf[g % 2]
            src = w_in[l, :, c0:c0 + cw].rearrange('(k p) n -> p k n', p=128)
            self.P.dma('sp', ws[:, 0:4, 0:cw], src[:, 0:4, :], writes=[ws.bufs[0]])
            self.P.dma('sp', ws[:, 4:8, 0:cw], src[:, 4:8, :], writes=[ws.bufs[1]])
            self.op('pool', 'tensor_copy', [ws.bufs[0]], [wb.bufs[0]], out=wb[:, 0:4, 0:cw], in_=ws[:, 0:4, 0:cw])
            self.op('dve', 'tensor_copy', [ws.bufs[1]], [wb.bufs[1]], out=wb[:, 4:8, 0:cw], in_=ws[:, 4:8, 0:cw])
            for t in range(NT):
                ps, os_ = pss[n % 4], ost[n % 4]
                for k in range(8):
                    self.op('pe', 'matmul', [self.xT.bufs[t], wb], [ps], ps[:, 0:cw],
                            lhsT=self.xT[:, k, t * 128:(t + 1) * 128], rhs=wb[:, k, 0:cw], start=(k == 0), stop=(k == 7))
                if n % 2:
                    self.op('act', 'copy', [ps], [os_], out=os_[:, 0:cw], in_=ps[:, 0:cw])
                else:
                    self.op('dve', 'tensor_copy', [ps], [os_], out=os_[:, 0:cw], in_=ps[:, 0:cw])
                self.P.dma('pool', self.Pd[1 + t * 128:1 + (t + 1) * 128, c0:c0 + cw], os_[:, 0:cw],
                           reads=_bufs([os_]), writes=[self.Pb[t][g]])
                n += 1

    def stage_gla(self, l):
        self.stage_begin(over_xT=True)
        I = self.I
        A = self.alloc
        gkup = A([16, 256], F32, 'gkup')
        self.load(gkup, I['gla_gk_up'][l])
        gbias = A([128, 256], F32, 'gbias')
        self.load(gbias, I['gla_gk_bias'][l].partition_broadcast(128))
        nw = A([128, 128], F32, 'gnw')
        self.load(nw, I['gla_norm_w'][l].partition_broadcast(128))
        S = [A([64, 512], F32, f'glaS{i}') for i in range(2)]
        self.op('dve', 'memset', [], [S[0]], S[0].ap, 0.0)
        gins = [A([128, 1552], F32, f'gin{i}') for i in range(2)]
        glT = A([16, 128], F32, 'glT')
        z = A([128, 256], F32, 'z')
        la = A([128, 256], F32, 'la')
        eg = A([128, 256], F32, 'eg')
        egn = A([128, 256], F32, 'egn')
        et = A([128, 256], F32, 'et')
        gs = A([128, 256], F32, 'gs')
        eglT = A([64, 8], F32, 'eglT')
        qd = A([128, 256], F32, 'qd')
        ki = A([128, 256], F32, 'ki')
        kt = A([128, 256], F32, 'kt')
        qdT = A([64, 4, 128], F32, 'qdT')
        kiT = A([64, 4, 128], F32, 'kiT')
        AT = A([128, 4, 128], F32, 'AT')
        o = A([128, 512], F32, 'o')
        sq = A([128, 128], F32, 'sq')
        ms = A([128, 4], F32, 'ms')
        rstd = A([128, 4], F32, 'rstd')
        sg = A([128, 512], F32, 'sg')
        yb = A([128, 512], BF16, 'yb')
        ps_t = self.pbank(0, 1, F32, name='ps_t')
        ps_g = self.pbank(1, 1, F32, name='ps_g')
        ps_e = self.pbank(2, 1, F32, parts=64, name='ps_e')
        ps_qT = self.pbank(3, 1, F32, parts=64, name='ps_qT')
        ps_kT = self.pbank(4, 1, F32, parts=64, name='ps_kT')
        ps_A = self.pbank(5, 1, F32, name='ps_A')
        ps_o = self.pbank(6, 1, F32, name='ps_o')
        ps_s = self.pbank(7, 1, F32, parts=64, name='ps_s')
        ps_y = self.pbank(0, 1, BF16, [128, 4, 128], name='ps_t')
        ps_y.bufs = ps_t.bufs
        idf = self.ident
        cur = 0
        rowm = self.ident[:, 0:1]
        for t, sb in [(t_, None) for t_ in range(16)] + [(16, b_) for b_ in range(16)]:
            gin = gins[(t + (sb or 0)) % 2]
            if sb is None:
                self.load(gin, self.Pd[1 + t * 128:1 + (t + 1) * 128, 0:1552], reads=self.pcols(t, 0, 1552))
            else:
                if sb == 0:
                    self.P.dma('pool', self.O['gla_p'][l].rearrange('h d v -> d h v'), S[cur].ap.rearrange('p (h v) -> p h v', h=4), reads=_bufs([S[cur]]))
                self.op('pool', 'memset', [], [gin], gin.ap, 0.0)
                self.P.dma('sp', gin[0:1, :], self.Pd[1 + 2048 + sb:2 + 2048 + sb, 0:1552], reads=self.pcols(16, 0, 1552), writes=_bufs([gin]))
                self.P.dma('sp', S[cur].ap.rearrange('p (h v) -> p h v', h=4), I['state_gla'][l, sb].rearrange('h d v -> d h v'), writes=_bufs([S[cur]]))
            q, k, v = gin[:, 0:256], gin[:, 256:512], gin[:, 512:1024]
            glow, gout = gin[:, 1024:1040], gin[:, 1040:1552]
            self.op('pe', 'transpose', [gin, idf], [ps_t], out=ps_t[0:16, 0:128], in_=glow, identity=idf.ap)
            self.op('act', 'copy', [ps_t], [glT], out=glT.ap, in_=ps_t[0:16, 0:128])
            self.op('pe', 'matmul', [glT, gkup], [ps_t], ps_t[:, 128:384], lhsT=glT.ap, rhs=gkup.ap, start=True, stop=True)
            self.op('dve', 'tensor_tensor', [ps_t, gbias], [z], out=z.ap, in0=ps_t[:, 128:384], in1=gbias.ap, op=ALU.add)
            self.op('act', 'activation', [z], [z], out=z.ap, in_=z.ap, func=AF.Exp, scale=-1.0)
            self.op('act', 'activation', [z, self.epsb], [la], out=la.ap, in_=z.ap, func=AF.Ln, bias=self.epsb[:, 3:4])
            self.op('dve', 'tensor_scalar', [la], [la], out=la.ap, in0=la.ap, scalar1=-1.0 / 16.0, scalar2=None, op0=ALU.mult)
            if sb is not None:
                self.op('dve', 'tensor_scalar', [la, self.ident], [la], out=la.ap, in0=la.ap, scalar1=rowm, scalar2=None, op0=ALU.mult)
            self.op('pe', 'matmul', [self.tri, la], [ps_g], ps_g[:, 0:256], lhsT=self.tri.ap, rhs=la.ap, start=True, stop=True)
            self.op('pe', 'matmul', [self.bo, la], [ps_g], ps_g[:, 256:512], lhsT=self.bo.ap, rhs=la.ap, start=True, stop=True)
            for h in range(4):
                self.op('pe', 'matmul', [la, self.bsel], [ps_e], ps_e[:, 2 * h:2 * h + 2], lhsT=la[:, h * 64:(h + 1) * 64],
                        rhs=self.bsel.ap, start=True, stop=True)
            self.op('act', 'activation', [ps_e], [eglT], out=eglT.ap, in_=ps_e[:, 0:8], func=AF.Exp)
            self.op('act', 'activation', [ps_g], [eg], out=eg.ap, in_=ps_g[:, 0:256], func=AF.Exp)
            self.op('act', 'activation', [ps_g], [egn], out=egn.ap, in_=ps_g[:, 0:256], func=AF.Exp, scale=-1.0)
            self.op('dve', 'tensor_copy', [ps_g], [gs], out=gs.ap, in_=ps_g[:, 0:256])
            self.op('dve', 'tensor_tensor', [ps_g, gs], [gs], out=gs.ap, in0=ps_g[:, 256:512], in1=gs.ap, op=ALU.subtract)
            self.op('act', 'activation', [gs], [et], out=et.ap, in_=gs.ap, func=AF.Exp)
            self.op('dve', 'scalar_tensor_tensor', [gin, eg], [qd], out=qd.ap, in0=q, scalar=0.125, in1=eg.ap, op0=ALU.mult, op1=ALU.mult)
            self.op('dve', 'tensor_tensor', [gin, egn], [ki], out=ki.ap, in0=k, in1=egn.ap, op=ALU.mult)
            self.op('pool', 'tensor_tensor', [gin, et], [kt], out=kt.ap, in0=k, in1=et.ap, op=ALU.mult)
            for h in range(4):
                self.op('pe', 'transpose', [qd, idf], [ps_qT], out=ps_qT[:, h * 128:(h + 1) * 128], in_=qd[:, h * 64:(h + 1) * 64], identity=idf.ap)
            self.op('act', 'copy', [ps_qT], [qdT], out=qdT.ap.rearrange('p a b -> p (a b)'), in_=ps_qT.ap)
            for h in range(4):
                self.op('pe', 'transpose', [ki, idf], [ps_kT], out=ps_kT[:, h * 128:(h + 1) * 128], in_=ki[:, h * 64:(h + 1) * 64], identity=idf.ap)
            self.op('dve', 'tensor_copy', [ps_kT], [kiT], out=kiT.ap.rearrange('p a b -> p (a b)'), in_=ps_kT.ap)
            for h in range(4):
                self.op('pe', 'matmul', [kiT, qdT], [ps_A], ps_A[:, h * 128:(h + 1) * 128], lhsT=kiT[:, h, :], rhs=qdT[:, h, :], start=True, stop=True)
            self.op('dve', 'tensor_tensor', [ps_A, self.tri], [AT], out=AT.ap, in0=ps_A.ap.rearrange('p (a b) -> p a b', a=4),
                    in1=self.tri.ap.unsqueeze(1).to_broadcast([128, 4, 128]), op=ALU.mult)
            for c in (range(2) if sb is None else [0]):
                Sc, Sn = S[cur], S[1 - cur]
                rows = slice(c * 64, (c + 1) * 64)
                for h in range(4):
                    hs = slice(h * 128, (h + 1) * 128)
                    self.op('pe', 'matmul', [AT, gin], [ps_o], ps_o[:, hs], lhsT=AT[:, h, :], rhs=v[:, hs], start=True, stop=False)
                    self.op('pe', 'matmul', [qdT, Sc], [ps_o], ps_o[:, hs], lhsT=qdT[:, h, :], rhs=Sc[:, hs], start=False, stop=True)
                self.op('act', 'copy', [ps_o], [o], out=o[rows, :], in_=ps_o[rows, :])
                for h in range(4):
                    hs = slice(h * 128, (h + 1) * 128)
                    self.op('pe', 'matmul', [kt, gin], [ps_s], ps_s[:, hs], lhsT=kt[rows, h * 64:(h + 1) * 64], rhs=v[rows, hs], start=True, stop=True)
                for h in range(4):
                    hs = slice(h * 128, (h + 1) * 128)
                    self.op('dve', 'scalar_tensor_tensor', [Sc, eglT, ps_s], [Sn], out=Sn[:, hs], in0=Sc[:, hs],
                            scalar=eglT[:, 2 * h + c:2 * h + c + 1], in1=ps_s[:, hs], op0=ALU.mult, op1=ALU.add)
                cur = 1 - cur
            self.gla_finish(o, gout, gin, nw, sq, ms, rstd, sg, yb, ps_y, t, sb)
            if sb is not None:
                self.P.dma('pool', self.O['gla_s'][l, sb].rearrange('h d v -> d h v'), S[cur].ap.rearrange('p (h v) -> p h v', h=4), reads=_bufs([S[cur]]))

    def gla_finish(self, o, gout, gin, nw, sq, ms, rstd, sg, yb, ps_y, t, sb=None):
        for h in range(4):
            hs = slice(h * 128, (h + 1) * 128)
            self.op('act', 'activation', [o], [sq, ms], out=sq.ap, in_=o[:, hs], func=AF.Square, accum_out=ms[:, h:h + 1])
        self.rsqrt(rstd, ms, 1.0 / 128.0, 1e-5)
        self.op('act', 'activation', [gin], [sg], out=sg.ap, in_=gout, func=AF.Silu)
        self.op('pool', 'tensor_tensor', [sg, nw], [sg], out=sg.ap.rearrange('p (a b) -> p a b', a=4),
                in0=sg.ap.rearrange('p (a b) -> p a b', a=4), in1=nw.ap.unsqueeze(1).to_broadcast([128, 4, 128]), op=ALU.mult)
        for h in range(4):
            hs = slice(h * 128, (h + 1) * 128)
            self.op('dve', 'scalar_tensor_tensor', [o, rstd, sg], [yb], out=yb[:, hs], in0=o[:, hs], scalar=rstd[:, h:h + 1],
                    in1=sg[:, hs], op0=ALU.mult, op1=ALU.mult)
        for h in range(4):
            self.op('pe', 'transpose', [yb, self.identb], [ps_y], out=ps_y[:, h, :], in_=yb[:, h * 128:(h + 1) * 128], identity=self.identb.ap)
        if sb is None:
            self.op('act', 'copy', [ps_y], [self.yT.bufs[t]], out=self.yT[:, 0:4, t * 128:(t + 1) * 128], in_=ps_y.ap)
        else:
            self.op('act', 'copy', [ps_y], [self.yT.bufs[t]], out=self.yT[:, 0:4, 2048 + sb:2049 + sb], in_=ps_y[:, :, 0:1])

    def stage_rwkv(self, l):
        self.stage_begin(over_xT=True)
        I = self.I
        A = self.alloc
        op = self.op
        bc = lambda nm, n: I[nm][l].partition_broadcast(128)
        mu = A([128, RWKV_PROJ], F32, 'mu'); self.load(mu, bc('rwkv_mu', RWKV_PROJ))
        cb = {}
        for nm in ['rwkv_w0', 'rwkv_a0', 'rwkv_k_k', 'rwkv_k_a', 'rwkv_ln_w', 'rwkv_ln_b']:
            cb[nm] = A([128, 512], F32, nm); self.load(cb[nm], bc(nm, 512))
        cb['rk'] = A([128, 512], F32, 'rk'); self.load(cb['rk'], I['rwkv_r_k'][l].rearrange('h d -> (h d)').partition_broadcast(128))
        waup = A([128, 512], F32, 'waup')
        self.P.dma('sp', waup[0:64, :], I['rwkv_w_up'][l], writes=_bufs([waup]))
        self.P.dma('sp', waup[64:128, :], I['rwkv_a_up'][l], writes=_bufs([waup]))
        gup = A([128, 512], F32, 'gup'); self.load(gup, I['rwkv_g_up'][l])
        H = [A([64, 512], F32, f'H{i}') for i in range(2)]
        op('dve', 'memset', [], [H[0]], H[0].ap, 0.0)
        prs = [A([128, RWKV_PROJ], F32, f'pr{i}') for i in range(1)]
        pv = A([128, RWKV_PROJ], F32, 'pv')
        t256 = A([128, 256], F32, 't256'); tT = A([128, 256], F32, 'tT')
        T5 = lambda n: A([128, 512], F32, n)
        logw, a_, g7, kk, kmod, b_ = T5('logw'), T5('a'), T5('g7'), T5('kk'), T5('kmod'), T5('b')
        eg, egn, eex, etl, gsb = T5('eg'), T5('egn'), T5('eex'), T5('etl'), T5('gsb')
        at, rt, bt, kt, btail, ktail = T5('at'), T5('rt'), T5('bt'), T5('kt'), T5('btail'), T5('ktail')
        aT, bT, rT, kT = [A([64, 8, 128], F32, n) for n in ('aT', 'bT', 'rT', 'kT')]
        M = [A([128, 8, 128], F32, 'M0')] * 2
        N = [A([128, 8, 128], F32, 'N0')] * 2
        X = A([128, 8, 128], F32, 'X')
        AkT, ArbT, ArkT = [A([128, 8, 128], F32, n) for n in ('AkT', 'ArbT', 'ArkT')]
        AkV, Usb, ysb = T5('AkV'), T5('Usb'), T5('ysb')
        op('dve', 'memset', [], [Usb], Usb.ap, 0.0)
        P1T = A([64, 8, 128], F32, 'P1T')
        ss = A([128, 8], F32, 'ss'); rs = A([128, 8], F32, 'rs'); s3 = A([128, 8], F32, 's3')
        eglT = A([64, 16], F32, 'eglT')
        yb = A([128, 512], BF16, 'ywb')
        HT = A([64, 512], F32, 'HT')
        pk = [self.pbank(i, 1, F32, name=f'pk{i}') for i in range(8)]
        p2 = [self.pbank(2 * i, 2, F32, name=f'p2{i}') for i in range(4)]
        ps_y = self.pbank(7, 1, BF16, [128, 4, 128])
        idf = self.ident
        v3 = lambda tl: tl.ap.rearrange('p (a b) -> p a b', a=8)
        b3 = lambda tl: tl.ap.unsqueeze(2).to_broadcast([128, 8, 64])
        mask3 = lambda tl: tl.ap.unsqueeze(1).to_broadcast([128, 8, 128])
        mask4 = lambda tl: tl.ap.unsqueeze(1).to_broadcast([128, 4, 128])
        h3 = lambda tl, hf: tl[:, 512 * hf:512 * hf + 512].rearrange('p (a b) -> p a b', a=4)
        cur = 0
        import os
        Sin = HT
        for t, sb in [(t_, None) for t_ in range(16)] + [(16, b_) for b_ in range(16)]:
            pr = prs[0]
            r0 = 1 + t * 128
            if sb is not None:
                if sb == 0:
                    self.rwkv_store_state(H[cur], HT, pk, self.O['wkv_p'][l])
                op('pool', 'memset', [], [pr], pr.ap, 0.0)
                op('pool', 'memset', [], [pv], pv.ap, 0.0)
                rs_ = 1 + 2048 + sb
                self.P.dma('sp', pr[0:1, :], self.Pd[rs_:rs_ + 1, C_PR:C_PR + RWKV_PROJ], reads=self.pcols(16, C_PR, C_PR + RWKV_PROJ), writes=_bufs([pr]))
                self.P.dma('sp', pv[0:1, :], I['state_shift'][l, sb:sb + 1, :], writes=_bufs([pv]))
                self.P.dma('pool', self.O['shift'][l, 1 + sb:2 + sb, :], pr[0:1, :], reads=_bufs([pr]))
                self.P.dma('sp', Sin.ap.rearrange('p (h j) -> p h j', h=8), I['state_wkv'][l, sb].rearrange('h i j -> i h j'), writes=_bufs([Sin]))
                for h in range(8):
                    op('pe', 'transpose', [Sin, idf], [pk[0]], out=pk[0][0:64, h * 64:(h + 1) * 64], in_=Sin[:, h * 64:(h + 1) * 64], identity=idf[0:64, 0:64])
                op('act', 'copy', [pk[0]], [H[cur]], out=H[cur].ap, in_=pk[0][0:64, :])
            for q_ in (range(4) if sb is None else []):
                cs = slice(q_ * 448, (q_ + 1) * 448)
                self.P.dma('sp', pr[:, cs], self.Pd[r0:r0 + 128, C_PR + q_ * 448:C_PR + (q_ + 1) * 448], reads=self.pcols(t, C_PR, C_PR + RWKV_PROJ), writes=_bufs([pr]))
                self.P.dma('sp', pv[:, cs], self.Pd[r0 - 1:r0 + 127, C_PR + q_ * 448:C_PR + (q_ + 1) * 448],
                           reads=self.pcols(t, C_PR, C_PR + RWKV_PROJ) + (self.pcols(t - 1, C_PR, C_PR + RWKV_PROJ) if t else [self.Pzero]), writes=_bufs([pv]))
            if t == 15:
                self.P.dma('pool', self.O['shift'][l, 0:1, :], pr[96:128, :][31:32, :], reads=_bufs([pr]))
            op('dve', 'tensor_tensor', [pv, pr], [pv], out=pv.ap, in0=pv.ap, in1=pr.ap, op=ALU.subtract)
            op('pool', 'tensor_tensor', [pv, mu], [pv], out=pv.ap, in0=pv.ap, in1=mu.ap, op=ALU.mult)
            op('dve', 'tensor_tensor', [pv, pr], [pv], out=pv.ap, in0=pv.ap, in1=pr.ap, op=ALU.add)
            r, k7, v7 = pv[:, 0:512], pv[:, 512:1024], pv[:, 1024:1536]
            self.rwkv_params(pv, t256, tT, waup, gup, cb, logw, a_, g7, kk, kmod, b_, ss, rs, eg, pk, idf, v3, b3)
            if sb is not None:
                op('dve', 'tensor_scalar', [logw, idf], [logw], out=logw.ap, in0=logw.ap, scalar1=idf[:, 0:1], scalar2=None, op0=ALU.mult)
            self.midn = 0
            op = self.opn
            for hf in range(2):
                cs = slice(hf * 256, (hf + 1) * 256)
                op('pe', 'matmul', [self.tri, logw], [pk[4]], pk[4][:, cs], lhsT=self.tri.ap, rhs=logw[:, cs], start=True, stop=True)
                op('pe', 'matmul', [self.bo, logw], [pk[5]], pk[5][:, cs], lhsT=self.bo.ap, rhs=logw[:, cs], start=True, stop=True)
            for h in range(8):
                op('pe', 'matmul', [logw, self.bsel], [pk[3]], pk[3][0:64, 2 * h:2 * h + 2], lhsT=logw[:, h * 64:(h + 1) * 64], rhs=self.bsel.ap, start=True, stop=True)
            op('act', 'activation', [pk[3]], [eglT], out=eglT.ap, in_=pk[3][0:64, 0:16], func=AF.Exp)
            op('act', 'activation', [pk[4]], [eg], out=eg.ap, in_=pk[4].ap, func=AF.Exp)
            op('act', 'activation', [pk[4]], [egn], out=egn.ap, in_=pk[4].ap, func=AF.Exp, scale=-1.0)
            op('dve', 'tensor_copy', [pk[4]], [gsb], out=gsb.ap, in_=pk[4].ap)
            op('dve', 'tensor_tensor', [gsb, logw], [eex], out=eex.ap, in0=gsb.ap, in1=logw.ap, op=ALU.subtract)
            op('act', 'activation', [eex], [eex], out=eex.ap, in_=eex.ap, func=AF.Exp)
            op('dve', 'tensor_tensor', [pk[5], gsb], [etl], out=etl.ap, in0=pk[5].ap, in1=gsb.ap, op=ALU.subtract)
            op('act', 'activation', [etl], [etl], out=etl.ap, in_=etl.ap, func=AF.Exp)
            op('dve', 'scalar_tensor_tensor', [kk, eex], [at], out=at.ap, in0=kk.ap, scalar=-1.0, in1=eex.ap, op0=ALU.mult, op1=ALU.mult)
            op('pool', 'tensor_tensor', [pv, eg], [rt], out=rt.ap, in0=r, in1=eg.ap, op=ALU.mult)
            op('dve', 'tensor_tensor', [b_, egn], [bt], out=bt.ap, in0=b_.ap, in1=egn.ap, op=ALU.mult)
            op('pool', 'tensor_tensor', [kmod, egn], [kt], out=kt.ap, in0=kmod.ap, in1=egn.ap, op=ALU.mult)
            op('pool', 'tensor_tensor', [b_, etl], [btail], out=btail.ap, in0=b_.ap, in1=etl.ap, op=ALU.mult)
            op('pool', 'tensor_tensor', [kmod, etl], [ktail], out=ktail.ap, in0=kmod.ap, in1=etl.ap, op=ALU.mult)
            op = self.op
            for i_, (src, dst) in enumerate(((at, aT), (bt, bT), (rt, rT), (kt, kT))):
                pp = p2[2 + (i_ % 2)]
                for h in range(8):
                    op('pe', 'transpose', [src, idf], [pp], out=pp[0:64, h * 128:(h + 1) * 128], in_=src[:, h * 64:(h + 1) * 64], identity=idf.ap)
                for hf in range(2):
                    op('act' if hf else 'dve', 'copy' if hf else 'tensor_copy', [pp], [dst], out=dst[:, 4 * hf:4 * hf + 4, :].rearrange('p a b -> p (a b)'), in_=pp[0:64, 512 * hf:512 * hf + 512])
            for h in range(8):
                op('pe', 'matmul', [bT, aT], [p2[0]], p2[0][:, h * 128:(h + 1) * 128], lhsT=bT[:, h, :], rhs=aT[:, h, :], start=True, stop=True)
            for h in range(8):
                op('pe', 'matmul', [bT, aT], [p2[1]], p2[1][:, h * 128:(h + 1) * 128], lhsT=aT[:, h, :], rhs=bT[:, h, :], start=True, stop=True)
            for hf in range(2):
                op('dve', 'tensor_tensor', [p2[0], self.tris], [M[0]], out=M[0][:, 4 * hf:4 * hf + 4, :], in0=h3(p2[0], hf), in1=mask4(self.tris), op=ALU.mult)
                op('dve', 'tensor_tensor', [p2[1], self.trist], [N[0]], out=N[0][:, 4 * hf:4 * hf + 4, :], in0=h3(p2[1], hf), in1=mask4(self.trist), op=ALU.mult)
            op('pool', 'tensor_tensor', [M[0], idf], [X], out=X.ap, in0=M[0].ap, in1=mask3(idf), op=ALU.add)
            c_ = 0
            for lvl in (range(1, 6) if sb is None else []):
                n_ = 1 - c_
                for h in range(8):
                    op('pe', 'matmul', [M[c_], N[c_]], [p2[1]], p2[1][:, h * 128:(h + 1) * 128], lhsT=M[c_][:, h, :], rhs=N[c_][:, h, :], start=True, stop=True)
                if lvl < 5:
                    for h in range(8):
                        op('pe', 'matmul', [M[c_], N[c_]], [p2[0]], p2[0][:, h * 128:(h + 1) * 128], lhsT=N[c_][:, h, :], rhs=M[c_][:, h, :], start=True, stop=True)
                for hf in range(2):
                    op('act', 'copy', [p2[1]], [N[n_]], out=N[n_][:, 4 * hf:4 * hf + 4, :], in_=h3(p2[1], hf))
                if lvl < 5:
                    for hf in range(2):
                        op('dve', 'tensor_copy', [p2[0]], [M[n_]], out=M[n_][:, 4 * hf:4 * hf + 4, :], in_=h3(p2[0], hf))
                for h in range(8):
                    op('pe', 'matmul', [N[n_], X], [p2[2]], p2[2][:, h * 128:(h + 1) * 128], lhsT=N[n_][:, h, :], rhs=X[:, h, :], start=True, stop=True)
                for hf in range(2):
                    op('dve', 'tensor_tensor', [X, p2[2]], [X], out=X[:, 4 * hf:4 * hf + 4, :], in0=X[:, 4 * hf:4 * hf + 4, :], in1=h3(p2[2], hf), op=ALU.add)
                c_ = n_
            for (lh, rh, pt_, dst, msk) in ((kT, aT, p2[0], AkT, self.tris), (bT, rT, p2[1], ArbT, self.tri), (kT, rT, p2[3], ArkT, self.tri)):
                for h in range(8):
                    op('pe', 'matmul', [lh, rh], [pt_], pt_[:, h * 128:(h + 1) * 128], lhsT=lh[:, h, :], rhs=rh[:, h, :], start=True, stop=True)
                for hf in range(2):
                    op('dve', 'tensor_tensor', [pt_, msk], [dst], out=dst[:, 4 * hf:4 * hf + 4, :], in0=h3(pt_, hf), in1=mask4(msk), op=ALU.mult)
            for h in range(8):
                op('pe', 'matmul', [AkT, pv], [pk[4]], pk[4][:, h * 64:(h + 1) * 64], lhsT=AkT[:, h, :], rhs=v7[:, h * 64:(h + 1) * 64], start=True, stop=True)
            op('act', 'copy', [pk[4]], [AkV], out=AkV.ap, in_=pk[4].ap)
            for h in range(8):
                op('pe', 'matmul', [at, X], [p2[0]], p2[0][0:64, h * 128:(h + 1) * 128], lhsT=at[:, h * 64:(h + 1) * 64], rhs=X[:, h, :], start=True, stop=True)
            for hf in range(2):
                op('act', 'copy', [p2[0]], [P1T], out=P1T[:, 4 * hf:4 * hf + 4, :].rearrange('p a b -> p (a b)'), in_=p2[0][0:64, 512 * hf:512 * hf + 512])
            for c in (range(2) if sb is None else [0]):
                Hc, Hn = H[cur], H[1 - cur]
                rows = slice(c * 64, (c + 1) * 64)
                for h in range(8):
                    hs = slice(h * 64, (h + 1) * 64)
                    op('pe', 'matmul', [P1T, Hc], [pk[5]], pk[5][:, hs], lhsT=P1T[:, h, :], rhs=Hc[:, hs], start=True, stop=False)
                    op('pe', 'matmul', [X, AkV], [pk[5]], pk[5][:, hs], lhsT=X[:, h, :], rhs=AkV[:, hs], start=False, stop=True)
                op('act', 'copy', [pk[5]], [Usb], out=Usb[rows, :], in_=pk[5][rows, :])
                for h in range(8):
                    hs = slice(h * 64, (h + 1) * 64)
                    op('pe', 'matmul', [rT, Hc], [pk[6]], pk[6][:, hs], lhsT=rT[:, h, :], rhs=Hc[:, hs], start=True, stop=False)
                    op('pe', 'matmul', [ArbT, Usb], [pk[6]], pk[6][:, hs], lhsT=ArbT[:, h, :], rhs=Usb[:, hs], start=False, stop=False)
                    op('pe', 'matmul', [ArkT, pv], [pk[6]], pk[6][:, hs], lhsT=ArkT[:, h, :], rhs=v7[:, hs], start=False, stop=True)
                op('act', 'copy', [pk[6]], [ysb], out=ysb[rows, :], in_=pk[6][rows, :])
                for h in range(8):
                    hs = slice(h * 64, (h + 1) * 64)
                    op('pe', 'matmul', [btail, Usb], [pk[7]], pk[7][0:64, hs], lhsT=btail[rows, hs], rhs=Usb[rows, hs], start=True, stop=False)
                    op('pe', 'matmul', [ktail, pv], [pk[7]], pk[7][0:64, hs], lhsT=ktail[rows, hs], rhs=v7[rows, hs], start=False, stop=True)
                egc = eglT.ap.rearrange('p (h c) -> p h c', c=2)[:, :, c:c + 1].to_broadcast([64, 8, 64])
                op('dve', 'tensor_tensor', [Hc, eglT], [Hn], out=Hn.ap.rearrange('p (a b) -> p a b', a=8), in0=Hc.ap.rearrange('p (a b) -> p a b', a=8), in1=egc, op=ALU.mult)
                op('dve', 'tensor_tensor', [Hn, pk[7]], [Hn], out=Hn.ap, in0=Hn.ap, in1=pk[7][0:64, :], op=ALU.add)
                cur = 1 - cur
            self.rwkv_finish(ysb, pv, kmod, g7, cb, eg, egn, ss, rs, s3, yb, ps_y, t, v3, b3, sb)
            if sb is not None:
                self.rwkv_store_state(H[cur], HT, pk, self.O['wkv_s'][l, sb])

    def rwkv_store_state(self, Hf, HT, pk, dst):
        op = self.op
        idf = self.ident
        for h in range(8):
            op('pe', 'transpose', [Hf, idf], [pk[0]], out=pk[0][0:64, h * 64:(h + 1) * 64], in_=Hf[:, h * 64:(h + 1) * 64], identity=idf[0:64, 0:64])
        op('act', 'copy', [pk[0]], [HT], out=HT.ap, in_=pk[0][0:64, :])
        self.P.dma('pool', dst.rearrange('h i j -> i h j'), HT.ap.rearrange('p (h j) -> p h j', h=8), reads=_bufs([HT]))

    def rwkv_params(self, pv, t256, tT, waup, gup, cb, logw, a_, g7, kk, kmod, b_, ss, rs, tmp, pk, idf, v3, b3):
        op = self.op
        k7 = pv[:, 512:1024]
        op('act', 'activation', [pv], [t256], out=t256[:, 0:64], in_=pv[:, 1536:1600], func=AF.Tanh)
        op('act', 'copy', [pv], [t256], out=t256[:, 64:128], in_=pv[:, 1600:1664])
        op('act', 'activation', [pv], [t256], out=t256[:, 128:256], in_=pv[:, 1664:1792], func=AF.Sigmoid)
        op('pe', 'transpose', [t256, idf], [pk[0]], out=pk[0][:, 0:128], in_=t256[:, 0:128], identity=idf.ap)
        op('pe', 'transpose', [t256, idf], [pk[0]], out=pk[0][:, 128:256], in_=t256[:, 128:256], identity=idf.ap)
        op('act', 'copy', [pk[0]], [tT], out=tT.ap, in_=pk[0][:, 0:256])
        op('pe', 'matmul', [tT, waup], [pk[1]], pk[1].ap, lhsT=tT[0:64, 0:128], rhs=waup[0:64, :], start=True, stop=True)
        op('pe', 'matmul', [tT, waup], [pk[2]], pk[2].ap, lhsT=tT[64:128, 0:128], rhs=waup[64:128, :], start=True, stop=True)
        op('pe', 'matmul', [tT, gup], [pk[3]], pk[3].ap, lhsT=tT[:, 128:256], rhs=gup.ap, start=True, stop=True)
        op('dve', 'tensor_tensor', [pk[1], cb['rwkv_w0']], [logw], out=logw.ap, in0=pk[1].ap, in1=cb['rwkv_w0'].ap, op=ALU.add)
        op('act', 'activation', [logw], [logw], out=logw.ap, in_=logw.ap, func=AF.Sigmoid)
        op('pool', 'tensor_scalar', [logw], [logw], out=logw.ap, in0=logw.ap, scalar1=-0.606531, scalar2=None, op0=ALU.mult)
        op('dve', 'tensor_tensor', [pk[2], cb['rwkv_a0']], [a_], out=a_.ap, in0=pk[2].ap, in1=cb['rwkv_a0'].ap, op=ALU.add)
        op('act', 'activation', [a_], [a_], out=a_.ap, in_=a_.ap, func=AF.Sigmoid)
        op('act', 'copy', [pk[3]], [g7], out=g7.ap, in_=pk[3].ap)
        op('dve', 'tensor_tensor', [pv, cb['rwkv_k_k']], [kk], out=kk.ap, in0=k7, in1=cb['rwkv_k_k'].ap, op=ALU.mult)
        op('dve', 'tensor_tensor', [kk], [tmp], out=tmp.ap, in0=kk.ap, in1=kk.ap, op=ALU.mult)
        op('dve', 'tensor_reduce', [tmp], [ss], out=ss.ap, in_=v3(tmp), axis=AX.X, op=ALU.add)
        self.rsqrt(rs, ss, 1.0, 1e-12)
        op('dve', 'tensor_tensor', [kk, rs], [kk], out=v3(kk), in0=v3(kk), in1=b3(rs), op=ALU.mult)
        op('dve', 'scalar_tensor_tensor', [a_, cb['rwkv_k_a']], [tmp], out=tmp.ap, in0=a_.ap, scalar=-1.0, in1=cb['rwkv_k_a'].ap, op0=ALU.add, op1=ALU.mult)
        op('dve', 'scalar_tensor_tensor', [tmp, pv], [kmod], out=kmod.ap, in0=tmp.ap, scalar=1.0, in1=k7, op0=ALU.add, op1=ALU.mult)
        op('pool', 'tensor_tensor', [kk, a_], [b_], out=b_.ap, in0=kk.ap, in1=a_.ap, op=ALU.mult)

    def rwkv_finish(self, ysb, pv, kmod, g7, cb, yc, sq, ss, rs, s3, yb, ps_y, t, v3, b3, sb=None):
        op = self.op
        r, v7 = pv[:, 0:512], pv[:, 1024:1536]
        op('dve', 'tensor_reduce', [ysb], [ss], out=ss.ap, in_=v3(ysb), axis=AX.X, op=ALU.add)
        op('dve', 'tensor_scalar', [ss], [ss], out=ss.ap, in0=ss.ap, scalar1=-1.0 / 64.0, scalar2=None, op0=ALU.mult)
        op('dve', 'tensor_tensor', [ysb, ss], [yc], out=v3(yc), in0=v3(ysb), in1=b3(ss), op=ALU.add)
        op('pool', 'tensor_tensor', [yc], [sq], out=sq.ap, in0=yc.ap, in1=yc.ap, op=ALU.mult)
        op('dve', 'tensor_reduce', [sq], [rs], out=rs.ap, in_=v3(sq), axis=AX.X, op=ALU.add)
        self.rsqrt(rs, rs, 1.0 / 64.0, 64e-5)
        op('dve', 'tensor_tensor', [yc, rs], [yc], out=v3(yc), in0=v3(yc), in1=b3(rs), op=ALU.mult)
        op('pool', 'tensor_tensor', [yc, cb['rwkv_ln_w']], [yc], out=yc.ap, in0=yc.ap, in1=cb['rwkv_ln_w'].ap, op=ALU.mult)
        op('pool', 'tensor_tensor', [yc, cb['rwkv_ln_b']], [yc], out=yc.ap, in0=yc.ap, in1=cb['rwkv_ln_b'].ap, op=ALU.add)
        op('dve', 'tensor_tensor', [pv, kmod], [sq], out=sq.ap, in0=r, in1=kmod.ap, op=ALU.mult)
        op('dve', 'tensor_tensor', [sq, cb['rk']], [sq], out=sq.ap, in0=sq.ap, in1=cb['rk'].ap, op=ALU.mult)
        op('dve', 'tensor_reduce', [sq], [s3], out=s3.ap, in_=v3(sq), axis=AX.X, op=ALU.add)
        op('dve', 'tensor_tensor', [pv, s3], [sq], out=v3(sq), in0=pv[:, 1024:1536].rearrange('p (a b) -> p a b', a=8), in1=b3(s3), op=ALU.mult)
        op('pool', 'tensor_tensor', [yc, sq], [yc], out=yc.ap, in0=yc.ap, in1=sq.ap, op=ALU.add)
        op('dve', 'tensor_tensor', [yc, g7], [yb], out=yb.ap, in0=yc.ap, in1=g7.ap, op=ALU.mult)
        for h in range(4):
            op('pe', 'transpose', [yb, self.identb], [ps_y], out=ps_y[:, h, :], in_=yb[:, h * 128:(h + 1) * 128], identity=self.identb.ap)
        if sb is None:
            op('act', 'copy', [ps_y], [self.yT.bufs[t]], out=self.yT[:, 8:12, t * 128:(t + 1) * 128], in_=ps_y.ap)
        else:
            op('act', 'copy', [ps_y], [self.yT.bufs[t]], out=self.yT[:, 8:12, 2048 + sb:2049 + sb], in_=ps_y[:, :, 0:1])

    def layer_norm(self, z, g_bc, b_bc, out_f32, stat):
        op = self.op
        op('dve', 'tensor_reduce', [z], [stat], out=stat[:, 0:1], in_=z.ap, axis=AX.X, op=ALU.add)
        op('dve', 'tensor_scalar', [stat], [stat], out=stat[:, 1:2], in0=stat[:, 0:1], scalar1=-1.0 / D, scalar2=None, op0=ALU.mult)
        op('dve', 'tensor_scalar', [z, stat], [z], out=z.ap, in0=z.ap, scalar1=stat[:, 1:2], scalar2=None, op0=ALU.add)
        op('act', 'activation', [z], [out_f32, stat], out=out_f32.ap, in_=z.ap, func=AF.Square, accum_out=stat[:, 2:3])
        self.rsqrt(stat, stat, 1.0 / D, 1e-5, src_ap=stat[:, 2:3], dst_ap=stat[:, 3:4])
        op('dve', 'scalar_tensor_tensor', [z, stat, g_bc], [out_f32], out=out_f32.ap, in0=z.ap, scalar=stat[:, 3:4], in1=g_bc.ap, op0=ALU.mult, op1=ALU.mult)
        op('pool', 'tensor_tensor', [out_f32, b_bc], [out_f32], out=out_f32.ap, in0=out_f32.ap, in1=b_bc.ap, op=ALU.add)

    def to_xT(self, xf, xb, pt, t):
        op = self.op
        op('pool', 'tensor_copy', [xf], [xb], out=xb.ap, in_=xf.ap)
        for k in range(8):
            op('pe', 'transpose', [xb, self.identb], [pt], out=pt[:, k, :], in_=xb[:, k * 128:(k + 1) * 128], identity=self.identb.ap)
        op('act', 'copy', [pt], [self.xT.bufs[t]], out=self.xT[:, :, t * 128:(t + 1) * 128], in_=pt.ap)

    def stage_merge(self, l, xsrc, xbufs):
        self.stage_begin()
        I = self.I
        A = self.alloc
        op = self.op
        wbr = A([128, 12, 1024], BF16, 'wbr', nbufs=6)
        wout = A([128, 8, 1024], BF16, 'wout', nbufs=4)
        stg = [A([128, 2, 1024], F32, f'stg{i}') for i in range(1)]
        wb_src = I['w_branch'][l].rearrange('n (k p) d -> p (n k) d', p=128)
        wo_src = I['w_out'][l].rearrange('(k p) d -> p k d', p=128)
        for i in range(6):
            st_ = stg[0]
            self.P.dma('sp', st_.ap, wb_src[:, 2 * i:2 * i + 2, :], writes=_bufs([st_]))
            op('dve' if i % 2 else 'pool', 'tensor_copy', [st_], [wbr.bufs[i]], out=wbr[:, 2 * i:2 * i + 2, :], in_=st_.ap)
        for i in range(4):
            st_ = stg[0]
            self.P.dma('sp', st_.ap, wo_src[:, 2 * i:2 * i + 2, :], writes=_bufs([st_]))
            op('dve' if i % 2 else 'pool', 'tensor_copy', [st_], [wout.bufs[i]], out=wout[:, 2 * i:2 * i + 2, :], in_=st_.ap)
        bg = A([128, 3072], F32, 'bg'); self.load(bg, I['b_gate'][l].rearrange('n d -> (n d)').partition_broadcast(128))
        lg = A([128, D], F32, 'l1g'); self.load(lg, I['ln1_g'][l].partition_broadcast(128))
        lb = A([128, D], F32, 'l1b'); self.load(lb, I['ln1_b'][l].partition_broadcast(128))
        pg = A([128, 3072], F32, 'pg')
        mg = A([128, D], F32, 'mg'); tmp = A([128, D], F32, 'tmp')
        mgb = A([128, D], BF16, 'mgb'); mgT = A([128, 8, 128], BF16, 'mgT')
        xt_ = A([128, D], F32, 'xt'); xb = A([128, D], BF16, 'xb')
        stat = A([128, 4], F32, 'stat')
        pp = [self.pbank(2 * i, 2, F32) for i in range(3)]
        pt = self.pbank(6, 1, BF16, [128, 8, 128])
        pt2 = self.pbank(7, 1, BF16, [128, 8, 128])
        for t in range(NT):
            r0 = 1 + t * 128
            for q_ in range(3):
                self.P.dma('sp', pg[:, q_ * 1024:(q_ + 1) * 1024], self.Pd[r0:r0 + 128, C_PG + q_ * 1024:C_PG + (q_ + 1) * 1024],
                           reads=self.pcols(t, C_PG, IN_PROJ), writes=_bufs([pg]))
            self.load(xt_, xsrc[t * 128:(t + 1) * 128, :], reads=[xbufs[t]] if xbufs else [])
            op('dve', 'tensor_tensor', [pg, bg], [pg], out=pg.ap, in0=pg.ap, in1=bg.ap, op=ALU.add)
            op('act', 'activation', [pg], [pg], out=pg.ap, in_=pg.ap, func=AF.Sigmoid)
            for n in range(3):
                ps = pp[n]
                for hf in range(2):
                    for k in range(4):
                        op('pe', 'matmul', [self.yT.bufs[t], wbr], [ps], ps[:, hf * 512:(hf + 1) * 512], lhsT=self.yT[:, 4 * n + k, t * 128:(t + 1) * 128],
                           rhs=wbr[:, 4 * n + k, hf * 512:(hf + 1) * 512], start=(k == 0), stop=(k == 3))
                for hf in range(2):
                    cs = slice(hf * 512, (hf + 1) * 512)
                    gs_ = pg[:, n * 1024 + hf * 512:n * 1024 + (hf + 1) * 512]
                    if n == 0:
                        op('dve', 'tensor_tensor', [ps, pg], [mg], out=mg[:, cs], in0=ps[:, cs], in1=gs_, op=ALU.mult)
                    else:
                        op('dve', 'tensor_tensor', [ps, pg], [tmp], out=tmp[:, cs], in0=ps[:, cs], in1=gs_, op=ALU.mult)
                if n:
                    op('pool', 'tensor_tensor', [mg, tmp], [mg], out=mg.ap, in0=mg.ap, in1=tmp.ap, op=ALU.add)
            op('act', 'copy', [mg], [mgb], out=mgb.ap, in_=mg.ap)
            for k in range(8):
                op('pe', 'transpose', [mgb, self.identb], [pt], out=pt[:, k, :], in_=mgb[:, k * 128:(k + 1) * 128], identity=self.identb.ap)
            op('act', 'copy', [pt], [mgT], out=mgT.ap, in_=pt.ap)
            ps = pp[0]
            for hf in range(2):
                for k in range(8):
                    op('pe', 'matmul', [mgT, wout], [ps], ps[:, hf * 512:(hf + 1) * 512], lhsT=mgT[:, k, :], rhs=wout[:, k, hf * 512:(hf + 1) * 512], start=(k == 0), stop=(k == 7))
            for hf in range(2):
                cs = slice(hf * 512, (hf + 1) * 512)
                op('dve', 'scalar_tensor_tensor', [xt_, ps], [tmp], out=tmp[:, cs], in0=xt_[:, cs], scalar=ALPHA, in1=ps[:, cs], op0=ALU.mult, op1=ALU.add)
            self.layer_norm(tmp, lg, lb, xt_, stat)
            self.P.dma('pool', self.X1d[t * 128:(t + 1) * 128, :], xt_.ap, reads=_bufs([xt_]), writes=[self.X1b[t]])
            self.to_xT(xt_, xb, pt2, t)

    def stage_mlp(self, l, dst, dst_bufs):
        self.stage_begin()
        I = self.I
        A = self.alloc
        op = self.op
        lg = A([128, D], F32, 'l2g'); self.load(lg, I['ln2_g'][l].partition_broadcast(128))
        lb = A([128, D], F32, 'l2b'); self.load(lb, I['ln2_b'][l].partition_broadcast(128))
        hT = A([128, 32, 384], BF16, 'hT')
        stg = A([128, 4096], F32, 'mstg', nbufs=2)
        wup = [A([128, 8, 512], BF16, f'wup{i}', nbufs=2) for i in range(2)]
        wdn = [A([128, 4, 1024], BF16, f'wdn{i}', nbufs=2) for i in range(2)]
        rl = A([128, 384], F32, 'rl')
        xt_ = A([128, D], F32, 'x1t'); z = A([128, D], F32, 'z2'); xb = A([128, D], BF16, 'x2b')
        stat = A([128, 4], F32, 'stat2')
        acc = [self.pbank(i, 1, F32) for i in range(6)]
        pu = [self.pbank(6, 1, F32), self.pbank(7, 1, F32)]
        ptb = self.pbank(7, 1, BF16, [128, 8, 128])
        w_up = I['w_up'][l]
        w_dn = I['w_down'][l]
        for cg in range(8):
            wu = wup[cg % 2]
            src = w_up[:, cg * 512:(cg + 1) * 512].rearrange('(k p) n -> p k n', p=128)
            sv = stg.ap.rearrange('p (k n) -> p k n', k=8)
            for hf in range(2):
                self.P.dma('sp', sv[:, 4 * hf:4 * hf + 4, :], src[:, 4 * hf:4 * hf + 4, :], writes=[stg.bufs[hf]])
                op('pool' if hf else 'dve', 'tensor_copy', [stg.bufs[hf]], [wu.bufs[hf]], out=wu[:, 4 * hf:4 * hf + 4, :], in_=sv[:, 4 * hf:4 * hf + 4, :])
            self.P.dma('pool', self.WUd[cg], wu.ap.rearrange('p k n -> p (k n)'), reads=_bufs([wu]), writes=[self.WUb[cg]])
        for cg in range(8):
            wd = wdn[cg % 2]
            src = w_dn[cg * 512:(cg + 1) * 512, :].rearrange('(j p) d -> p j d', p=128)
            sv = stg.ap.rearrange('p (j d) -> p j d', j=4)
            for hf in range(2):
                self.P.dma('sp', sv[:, 2 * hf:2 * hf + 2, :], src[:, 2 * hf:2 * hf + 2, :], writes=[stg.bufs[hf]])
                op('pool' if hf else 'dve', 'tensor_copy', [stg.bufs[hf]], [wd.bufs[hf]], out=wd[:, 2 * hf:2 * hf + 2, :], in_=sv[:, 2 * hf:2 * hf + 2, :])
            self.P.dma('pool', self.WDd[cg], wd.ap.rearrange('p j d -> p (j d)'), reads=_bufs([wd]), writes=[self.WDb[cg]])
        groups = [list(range(g, min(g + 3, NT))) for g in range(0, NT, 3)]
        for grp in groups:
            t0 = grp[0]
            ntok = len(grp) * 128
            tokc = slice(t0 * 128, t0 * 128 + ntok)
            for cg in range(8):
                wu = wup[cg % 2]
                self.P.dma('sp', wu.ap.rearrange('p k n -> p (k n)'), self.WUd[cg], reads=[self.WUb[cg]], writes=_bufs([wu]))
                for j in range(4):
                    fc = cg * 4 + j
                    ps = pu[fc % 2]
                    for k in range(8):
                        op('pe', 'matmul', [wu] + [self.xT.bufs[t] for t in grp], [ps], ps[:, 0:ntok], lhsT=wu[:, k, j * 128:(j + 1) * 128],
                           rhs=self.xT[:, k, tokc], start=(k == 0), stop=(k == 7))
                    op('act', 'activation', [ps], [rl], out=rl[:, 0:ntok], in_=ps[:, 0:ntok], func=AF.Relu)
                    op('dve', 'tensor_tensor', [rl], [hT], out=hT[:, fc, 0:ntok], in0=rl[:, 0:ntok], in1=rl[:, 0:ntok], op=ALU.mult)
            for cg in range(8):
                wd = wdn[cg % 2]
                self.P.dma('sp', wd.ap.rearrange('p j d -> p (j d)'), self.WDd[cg], reads=[self.WDb[cg]], writes=_bufs([wd]))
                for j in range(4):
                    fc = cg * 4 + j
                    for ti, t in enumerate(grp):
                        for hf in range(2):
                            a_ = acc[ti * 2 + hf]
                            op('pe', 'matmul', [hT, wd], [a_], a_.ap, lhsT=hT[:, fc, ti * 128:(ti + 1) * 128], rhs=wd[:, j, hf * 512:(hf + 1) * 512],
                               start=(fc == 0), stop=(fc == 31))
            for ti, t in enumerate(grp):
                self.load(xt_, self.X1d[t * 128:(t + 1) * 128, :], reads=[self.X1b[t]])
                for hf in range(2):
                    cs = slice(hf * 512, (hf + 1) * 512)
                    op('dve', 'scalar_tensor_tensor', [xt_, acc[ti * 2 + hf]], [z], out=z[:, cs], in0=xt_[:, cs], scalar=ALPHA, in1=acc[ti * 2 + hf].ap, op0=ALU.mult, op1=ALU.add)
                self.layer_norm(z, lg, lb, xt_, stat)
                self.P.dma('pool', dst[t * 128:(t + 1) * 128, :], xt_.ap, reads=_bufs([xt_]), writes=[dst_bufs[t]] if dst_bufs else [])
                if l + 1 < DEPTH:
                    self.to_xT(xt_, xb, ptb, t)

    def stage_moba(self, l):
        self.stage_begin(over_xT=True)
        I = self.I
        A = self.alloc
        op = self.op
        KT = A([128, 4, 2048], BF16, 'KT')
        KT.bufs = [Buf(f'KT{t}') for t in range(16)]
        Va = A([128, 16, 8, 65], BF16, 'Va')
        Va.bufs = [Buf(f'Va{t}') for t in range(16)]
        KM = A([128, 4, 8], F32, 'KM')
        KMb = A([128, 4, 8], BF16, 'KMb')
        min_ = [A([128, 1536], F32, f'min{i}') for i in range(2)]
        rp = A([128, 64], F32, 'rp')
        qr = A([128, 512], F32, 'qr'); kr = A([128, 512], F32, 'kr'); t1 = A([128, 512], F32, 't1')
        qb = A([128, 512], BF16, 'qb'); kb = A([128, 512], BF16, 'kb')
        QT = A([128, 4, 128], BF16, 'QT')
        cm = A([128, 128], F32, 'cmf'); self.load(cm, I['c_cmask'])
        cmb = A([128, 128], BF16, 'cmb'); op('dve', 'tensor_copy', [cm], [cmb], out=cmb.ap, in_=cm.ap)
        ef = A([8, 1024], F32, 'ef'); self.load(ef, I['c_eblk'])
        eb = A([8, 8, 128], BF16, 'eb'); op('dve', 'tensor_copy', [ef], [eb], out=eb.ap.rearrange('p a b -> p (a b)'), in_=ef.ap)
        pastneg = A([128, 64], F32, 'pastneg'); self.load(pastneg, I['c_pastneg'])
        gate = A([128, 8, 8], F32, 'gate'); top8 = A([128, 8, 8], F32, 'top8'); sel = A([128, 8, 8], F32, 'sel'); sel2 = A([128, 8, 8], F32, 'sel2')
        nsT = A([8, 8, 128], BF16, 'nsT')
        PT = [A([128, 4, 128], BF16, f'PT{i}') for i in range(2)]
        osb = A([128, 8, 65], F32, 'osb'); rden = A([128, 8], F32, 'rden'); ymb = A([128, 512], BF16, 'ymb')
        for t in range(16):
            op('pool', 'memset', [], [Va.bufs[t]], Va[:, t, :, 64:65], 1.0)
        ps_tr = self.pbank(0, 1, BF16, [128, 8, 128])
        ps_g = self.pbank(1, 1, F32)
        ps_ns = self.pbank(6, 2, F32)
        ps_st = [self.pbank(2, 1, F32), self.pbank(3, 1, F32)]
        ps_o = [self.pbank(4, 1, F32), self.pbank(5, 1, F32)]
        ps_y = self.pbank(6, 1, BF16, [128, 4, 128])
        idb = self.identb
        for t in range(NT):
            mi = min_[t % 2]
            r0 = 1 + t * 128
            self.load(mi, self.Pd[r0:r0 + 128, C_MQ:C_MQ + 1536], reads=self.pcols(t, C_MQ, C_MQ + 1536))
            self.load(rp, I['c_rope'][t * 128:(t + 1) * 128, :])
            for (c0, dst, scl) in ((0, qr, 0.125), (512, kr, 1.0)):
                x1 = mi[:, c0:c0 + 512].rearrange('p (h two d) -> p h two d', h=8, two=2)[:, :, 0, :]
                x2 = mi[:, c0:c0 + 512].rearrange('p (h two d) -> p h two d', h=8, two=2)[:, :, 1, :]
                d1 = dst.ap.rearrange('p (h two d) -> p h two d', h=8, two=2)[:, :, 0, :]
                d2 = dst.ap.rearrange('p (h two d) -> p h two d', h=8, two=2)[:, :, 1, :]
                tv1 = t1.ap.rearrange('p (h two d) -> p h two d', h=8, two=2)[:, :, 0, :]
                tv2 = t1.ap.rearrange('p (h two d) -> p h two d', h=8, two=2)[:, :, 1, :]
                cosb = rp[:, 0:32].unsqueeze(1).to_broadcast([128, 8, 32])
                sinb = rp[:, 32:64].unsqueeze(1).to_broadcast([128, 8, 32])
                e1, e2 = ('dve', 'pool')
                op(e1, 'tensor_tensor', [mi, rp], [dst], out=d1, in0=x1, in1=cosb, op=ALU.mult)
                op(e2, 'tensor_tensor', [mi, rp], [t1], out=tv1, in0=x2, in1=sinb, op=ALU.mult)
                op(e1, 'tensor_tensor', [mi, rp], [dst], out=d2, in0=x2, in1=cosb, op=ALU.mult)
                op(e2, 'tensor_tensor', [mi, rp], [t1], out=tv2, in0=x1, in1=sinb, op=ALU.mult)
                op(e1, 'tensor_tensor', [dst, t1], [dst], out=d1, in0=d1, in1=tv1, op=ALU.subtract)
                op(e1, 'tensor_tensor', [dst, t1], [dst], out=d2, in0=d2, in1=tv2, op=ALU.add)
            self.P.dma('pool', self.O['k_out'][l, t * 128:(t + 1) * 128, :], kr.ap, reads=_bufs([kr]))
            self.P.dma('pool', self.O['v_out'][l, t * 128:(t + 1) * 128, :], mi[:, 1024:1536], reads=_bufs([mi]))
            if t == 16:
                if 'msamp' not in SKIP:
                    self.moba_sample(l, locals())
                continue
            op('act', 'activation', [qr], [qb], out=qb.ap, in_=qr.ap, func=AF.Copy, scale=0.125)
            op('act', 'copy', [kr], [kb], out=kb.ap, in_=kr.ap)
            op('pool', 'tensor_copy', [mi], [Va.bufs[t]], out=Va[:, t, :, 0:64], in_=mi[:, 1024:1536].rearrange('p (h d) -> p h d', h=8))
            for p_ in range(4):
                op('pe', 'transpose', [qb, idb], [ps_tr], out=ps_tr[:, p_, :], in_=qb[:, p_ * 128:(p_ + 1) * 128], identity=idb.ap)
                op('pe', 'transpose', [kb, idb], [ps_tr], out=ps_tr[:, 4 + p_, :], in_=kb[:, p_ * 128:(p_ + 1) * 128], identity=idb.ap)
            op('act', 'copy', [ps_tr], [QT], out=QT.ap, in_=ps_tr[:, 0:4, :])
            op('dve', 'tensor_copy', [ps_tr], [KT.bufs[t]], out=KT[:, :, t * 128:(t + 1) * 128], in_=ps_tr[:, 4:8, :])
            b = t // 2
            if t % 2 == 1:
                op('dve', 'tensor_reduce', [KT.bufs[t - 1], KT.bufs[t]], [KM], out=KM[:, :, b:b + 1], in_=KT[:, :, b * 256:(b + 1) * 256], axis=AX.X, op=ALU.add)
                op('dve', 'tensor_copy', [KM], [KMb], out=KMb[:, :, b:b + 1], in_=KM[:, :, b:b + 1])
            hp = lambda tl, h: tl[(h % 2) * 64:(h % 2) * 64 + 64, h // 2]
            if b > 0:
                for h in range(8):
                    op('pe', 'matmul', [QT, KMb], [ps_g], ps_g[:, h * 8:h * 8 + b], lhsT=hp(QT, h)[:, :], rhs=hp(KMb, h)[:, 0:b], start=True, stop=True)
                op('dve', 'memset', [], [gate], gate.ap, -1e30)
                op('dve', 'tensor_copy', [ps_g], [gate], out=gate[:, :, 0:b], in_=ps_g[:, 0:64].rearrange('p (h n) -> p h n', h=8)[:, :, 0:b])
                for h in range(8):
                    op('dve', 'max', [gate], [top8], out=top8[:, h, :], in_=gate[:, h, :])
                op('dve', 'tensor_tensor', [gate, top8], [sel], out=sel.ap, in0=gate.ap, in1=top8[:, :, 2:3].to_broadcast([128, 8, 8]), op=ALU.is_ge)
                op('dve', 'tensor_scalar', [gate], [sel2], out=sel2.ap, in0=gate.ap, scalar1=-1e29, scalar2=None, op0=ALU.is_gt)
                op('dve', 'tensor_tensor', [sel, sel2], [sel], out=sel.ap, in0=sel.ap, in1=sel2.ap, op=ALU.mult)
                op('dve', 'tensor_scalar', [sel], [sel], out=sel.ap, in0=sel.ap, scalar1=-1.0, scalar2=-NEG, op0=ALU.add, op1=ALU.mult)
                for h in range(8):
                    op('pe', 'transpose', [sel, self.ident], [ps_ns], out=ps_ns[0:8, h * 128:(h + 1) * 128], in_=sel[:, h, :], identity=self.ident.ap)
                for hf in range(2):
                    op('act', 'copy', [ps_ns], [nsT], out=nsT[:, 4 * hf:4 * hf + 4, :].rearrange('p a b -> p (a b)'), in_=ps_ns[0:8, 512 * hf:512 * hf + 512])
            nkc = t + 1
            for h in range(8):
                po = ps_o[h // 4]
                ogrp = po[:, (h % 4) * 65:(h % 4) * 65 + 65]
                for g0 in range(0, nkc, 4):
                    g1 = min(g0 + 4, nkc)
                    pst = ps_st[(h * 5 + g0 // 4) % 2]
                    ptile = PT[(h * 5 + g0 // 4) % 2]
                    for kc in range(g0, g1):
                        dst_ = pst[:, (kc - g0) * 128:(kc - g0 + 1) * 128]
                        n = kc // 2
                        extra = (kc == t) or (n < b)
                        op('pe', 'matmul', [KT.bufs[kc], QT], [pst], dst_, lhsT=hp(KT, h)[:, kc * 128:(kc + 1) * 128], rhs=hp(QT, h)[:, :], start=True, stop=not extra)
                        if kc == t:
                            op('pe', 'matmul', [idb, cmb], [pst], dst_, lhsT=idb.ap, rhs=cmb.ap, start=False, stop=True)
                        elif n < b:
                            op('pe', 'matmul', [eb, nsT], [pst], dst_, lhsT=eb[:, n, :], rhs=nsT[:, h, :], start=False, stop=True)
                    w_ = (g1 - g0) * 128
                    op('act', 'activation', [pst], [ptile], out=ptile.ap.rearrange('p a b -> p (a b)')[:, 0:w_], in_=pst[:, 0:w_], func=AF.Exp)
                    for kc in range(g0, g1):
                        op('pe', 'matmul', [ptile, Va.bufs[kc]], [po], ogrp, lhsT=ptile[:, kc - g0, :], rhs=Va[:, kc, h, :], start=(kc == 0), stop=(kc == nkc - 1))
            for hf in range(2):
                op('act', 'copy', [ps_o[hf]], [osb], out=osb[:, 4 * hf:4 * hf + 4, :].rearrange('p a b -> p (a b)'), in_=ps_o[hf][:, 0:260])
            op('dve', 'reciprocal', [osb], [rden], out=rden.ap, in_=osb[:, :, 64])
            op('dve', 'tensor_tensor', [osb, rden], [ymb], out=ymb.ap.rearrange('p (h d) -> p h d', h=8), in0=osb[:, :, 0:64], in1=rden.ap.unsqueeze(2).to_broadcast([128, 8, 64]), op=ALU.mult)
            for p_ in range(4):
                op('pe', 'transpose', [ymb, idb], [ps_y], out=ps_y[:, p_, :], in_=ymb[:, p_ * 128:(p_ + 1) * 128], identity=idb.ap)
            op('act', 'copy', [ps_y], [self.yT.bufs[t]], out=self.yT[:, 4:8, t * 128:(t + 1) * 128], in_=ps_y.ap)

    def moba_sample(self, l, L):
        I = self.I
        A = self.alloc
        op = self.op
        qr, kr, mi, qb, kb, QT, KT, Va, KM, KMb = (L[k] for k in ('qr', 'kr', 'mi', 'qb', 'kb', 'QT', 'KT', 'Va', 'KM', 'KMb'))
        eb, gate, top8, sel, osb, rden, ymb = (L[k] for k in ('eb', 'gate', 'top8', 'sel', 'osb', 'rden', 'ymb'))
        ps_tr, ps_g, ps_ns, ps_st, ps_o, ps_y, idb = (L[k] for k in ('ps_tr', 'ps_g', 'ps_ns', 'ps_st', 'ps_o', 'ps_y', 'idb'))
        hp = lambda tl, h: tl[(h % 2) * 64:(h % 2) * 64 + 64, h // 2]
        KTs = A([128, 4, 128], BF16, 'KTs'); Vs = A([128, 8, 65], BF16, 'Vs')
        dmf = A([128, 128], F32, 'dmf'); self.load(dmf, I['c_dmask'])
        dmb = A([128, 128], BF16, 'dmb'); op('dve', 'tensor_copy', [dmf], [dmb], out=dmb.ap, in_=dmf.ap)
        iota = A([128, 16], F32, 'iota'); self.load(iota, I['c_iota'])
        idxf = A([128, 16], F32, 'idxf')
        cmk = A([8, 256], F32, 'cmk'); self.load(cmk, I['c_cmk'])
        ptb = A([128, 16], I32, 'ptb'); idx = A([128, 16], I32, 'idx')
        kpg = A([128, 8, 512], F32, 'kpg', nbufs=8); vpg = A([128, 8, 512], F32, 'vpg', nbufs=8)
        kpb = A([128, 8, 512], BF16, 'kpb')
        nsTb = A([8, 8, 16], BF16, 'nsTb')
        PTs = [A([128, 512], BF16, f'PTs{i}') for i in range(2)]
        ck = I['cache_k'].rearrange('l n r h d -> (l n r) (h d)')
        cv = I['cache_v'].rearrange('l n r h d -> (l n r) (h d)')
        nrows = ck.shape[0]
        lbase = l * (nrows // DEPTH)
        op('act', 'activation', [qr], [qb], out=qb.ap, in_=qr.ap, func=AF.Copy, scale=0.125)
        op('act', 'copy', [kr], [kb], out=kb.ap, in_=kr.ap)
        op('pool', 'memset', [], [Vs], Vs[:, :, 64:65], 1.0)
        op('pool', 'tensor_copy', [mi], [Vs], out=Vs[:, :, 0:64], in_=mi[:, 1024:1536].rearrange('p (h d) -> p h d', h=8))
        for p_ in range(4):
            op('pe', 'transpose', [qb, idb], [ps_tr], out=ps_tr[:, p_, :], in_=qb[:, p_ * 128:(p_ + 1) * 128], identity=idb.ap)
            op('pe', 'transpose', [kb, idb], [ps_tr], out=ps_tr[:, 4 + p_, :], in_=kb[:, p_ * 128:(p_ + 1) * 128], identity=idb.ap)
        op('act', 'copy', [ps_tr], [QT], out=QT.ap, in_=ps_tr[:, 0:4, :])
        op('dve', 'tensor_copy', [ps_tr], [KTs], out=KTs.ap, in_=ps_tr[:, 4:8, :])
        zb = A([128, 16], BF16, 'zb'); op('dve', 'memset', [], [zb], zb.ap, 0.0)
        for hf in range(2):
            op('pe', 'matmul', [zb, Vs], [ps_o[hf]], ps_o[hf][0:16, 0:260], lhsT=zb.ap, rhs=Vs[:, 0:4, :].rearrange('p a b -> p (a b)'), start=True, stop=False)
        for hq in range(2):
            pst, ptile = ps_st[hq], PTs[hq]
            for h4 in range(4):
                h = hq * 4 + h4
                dst_ = pst[:, h4 * 16:(h4 + 1) * 16]
                op('pe', 'matmul', [KTs, QT], [pst], dst_, lhsT=hp(KTs, h)[:, :], rhs=hp(QT, h)[:, 0:16], start=True, stop=False)
                op('pe', 'matmul', [idb, dmb], [pst], dst_, lhsT=idb.ap, rhs=dmb[:, 0:16], start=False, stop=True)
            op('act', 'activation', [pst], [ptile], out=ptile[:, 0:64], in_=pst[:, 0:64], func=AF.Exp)
            for h4 in range(4):
                h = hq * 4 + h4
                po = ps_o[h // 4]
                op('pe', 'matmul', [ptile, Vs], [po], po[0:16, (h % 4) * 65:(h % 4) * 65 + 65], lhsT=ptile[:, h4 * 16:(h4 + 1) * 16], rhs=Vs[:, h, :], start=False, stop=False)
        for sb in range(16):
            self.load(ptb, I['page_table'][sb].partition_broadcast(128))
            op('dve', 'tensor_copy', [ptb], [idxf], out=idxf.ap, in_=ptb.ap)
            op('dve', 'tensor_scalar', [idxf], [idxf], out=idxf.ap, in0=idxf.ap, scalar1=128.0, scalar2=float(lbase), op0=ALU.mult, op1=ALU.add)
            op('dve', 'tensor_tensor', [idxf, iota], [idxf], out=idxf.ap, in0=idxf.ap, in1=iota.ap, op=ALU.add)
            op('dve', 'tensor_copy', [idxf], [idx], out=idx.ap, in_=idxf.ap)
            for half in range(2):
                for j in range(8):
                    pg_ = half * 8 + j
                    for (src, dstt) in ((ck, kpg), (cv, vpg)):
                        o_ = dstt[:, j, :]
                        off = idx[:, pg_:pg_ + 1]
                        self.P.dma('pool', None, None, reads=_bufs([idx]), writes=[dstt.bufs[j]],
                                   _fn=lambda e, o_=o_, src=src, off=off: e.indirect_dma_start(out=o_, out_offset=None, in_=src, in_offset=bass.IndirectOffsetOnAxis(ap=off, axis=0)))
                op('dve', 'tensor_copy', [kpg.bufs[j] for j in range(4)], [kpb], out=kpb[:, 0:4, :], in_=kpg[:, 0:4, :])
                op('act', 'copy', [kpg.bufs[j] for j in range(4, 8)], [kpb], out=kpb[:, 4:8, :], in_=kpg[:, 4:8, :])
                for j in range(8):
                    pg_ = half * 8 + j
                    if j % 2:
                        op('act', 'copy', [vpg.bufs[j]], [Va.bufs[pg_]], out=Va[:, pg_, :, 0:64], in_=vpg[:, j, :].rearrange('p (h d) -> p h d', h=8))
                    else:
                        op('dve', 'tensor_copy', [vpg.bufs[j]], [Va.bufs[pg_]], out=Va[:, pg_, :, 0:64], in_=vpg[:, j, :].rearrange('p (h d) -> p h d', h=8))
                for j2 in range(4):
                    for g in range(2):
                        j = 2 * j2 + g
                        for p_ in range(4):
                            op('pe', 'transpose', [kpb, idb], [ps_tr], out=ps_tr[:, 4 * g + p_, :], in_=kpb[:, j, p_ * 128:(p_ + 1) * 128], identity=idb.ap)
                    pg0 = half * 8 + 2 * j2
                    op('dve', 'tensor_copy', [ps_tr], [KT.bufs[pg0], KT.bufs[pg0 + 1]],
                       out=KT[:, :, pg0 * 128:(pg0 + 2) * 128].rearrange('p a (g c) -> p a g c', g=2),
                       in_=ps_tr.ap.rearrange('p (g a) c -> p a g c', g=2))
            op('dve', 'tensor_reduce', [KT], [KM], out=KM.ap, in_=KT.ap.rearrange('p a (n c) -> p a n c', n=8), axis=AX.X, op=ALU.add)
            op('dve', 'tensor_copy', [KM], [KMb], out=KMb.ap, in_=KM.ap)
            for h in range(8):
                op('pe', 'matmul', [QT, KMb], [ps_g], ps_g[0:16, h * 8:h * 8 + 8], lhsT=hp(QT, h)[:, 0:16], rhs=hp(KMb, h)[:, :], start=True, stop=True)
            op('dve', 'tensor_copy', [ps_g], [gate], out=gate[0:16].rearrange('p a b -> p (a b)'), in_=ps_g[0:16, 0:64])
            for h in range(8):
                op('dve', 'max', [gate], [top8], out=top8[0:16, h, :], in_=gate[0:16, h, :])
            op('dve', 'tensor_tensor', [gate, top8], [sel], out=sel[0:16], in0=gate[0:16], in1=top8[0:16, :, 2:3].to_broadcast([16, 8, 8]), op=ALU.is_ge)
            op('dve', 'tensor_scalar', [sel], [sel], out=sel[0:16], in0=sel[0:16], scalar1=-1.0, scalar2=-NEG, op0=ALU.add, op1=ALU.mult)
            for h in range(8):
                op('pe', 'transpose', [sel, self.ident], [ps_ns], out=ps_ns[0:8, h * 16:(h + 1) * 16], in_=sel[0:16, h, :], identity=self.ident[0:16, 0:16])
            op('dve', 'tensor_tensor', [ps_ns, cmk], [nsTb], out=nsTb.ap, in0=ps_ns[0:8, 0:128].rearrange('p (h q) -> p h q', h=8),
               in1=cmk[:, sb * 16:(sb + 1) * 16].unsqueeze(1).to_broadcast([8, 8, 16]), op=ALU.add)
            for hq in range(4):
                pst, ptile = ps_st[hq % 2], PTs[hq % 2]
                for h2 in range(2):
                    h = hq * 2 + h2
                    for kc in range(16):
                        dst_ = pst[:, h2 * 256 + kc * 16:h2 * 256 + (kc + 1) * 16]
                        op('pe', 'matmul', [KT.bufs[kc], QT], [pst], dst_, lhsT=hp(KT, h)[:, kc * 128:(kc + 1) * 128], rhs=hp(QT, h)[:, 0:16], start=True, stop=False)
                        op('pe', 'matmul', [eb, nsTb], [pst], dst_, lhsT=eb[:, kc // 2, :], rhs=nsTb[:, h, :], start=False, stop=True)
                op('act', 'activation', [pst], [ptile], out=ptile.ap, in_=pst.ap, func=AF.Exp)
                for h2 in range(2):
                    h = hq * 2 + h2
                    po = ps_o[h // 4]
                    for kc in range(16):
                        op('pe', 'matmul', [ptile, Va.bufs[kc]], [po], po[0:16, (h % 4) * 65:(h % 4) * 65 + 65], lhsT=ptile[:, h2 * 256 + kc * 16:h2 * 256 + (kc + 1) * 16],
                           rhs=Va[:, kc, h, :], start=False, stop=(sb == 15 and kc == 15 and h % 4 == 3))
        for hf in range(2):
            op('act', 'copy', [ps_o[hf]], [osb], out=osb[0:16, 4 * hf:4 * hf + 4, :].rearrange('p a b -> p (a b)'), in_=ps_o[hf][0:16, 0:260])
        op('dve', 'reciprocal', [osb], [rden], out=rden[0:16], in_=osb[0:16, :, 64])
        op('dve', 'tensor_tensor', [osb, rden], [ymb], out=ymb[0:16].rearrange('p (h d) -> p h d', h=8), in0=osb[0:16, :, 0:64], in1=rden[0:16].unsqueeze(2).to_broadcast([16, 8, 64]), op=ALU.mult)
        for p_ in range(4):
            op('pe', 'transpose', [ymb, idb], [ps_y], out=ps_y[:, p_, 0:16], in_=ymb[0:16, p_ * 128:(p_ + 1) * 128], identity=idb[0:16, 0:16])
        op('act', 'copy', [ps_y], [self.yT.bufs[16]], out=self.yT[:, 4:8, 2048:2064], in_=ps_y[:, :, 0:16])

    def dump_dbg(self):
        self.P.nocut = True
        self.P.barrier()
        self.P.dma('pool', self.O['dbg_yT'], self.yT.ap.rearrange('p a b -> p (a b)'), reads=_bufs([self.yT]))

    def build(self):
        st = self.stages
        self.setup()
        self.stage_xT(self.I['xin'], None)
        for l in range(DEPTH):
            if 'proj' in st or 'all' in st:
                self.stage_proj(l)
            if 'gla' in st or 'all' in st:
                self.stage_gla(l)
            if 'rwkv' in st or 'all' in st:
                self.stage_rwkv(l)
            if 'moba' in st or 'all' in st:
                self.stage_moba(l)
            if 'fin' in st or 'all' in st:
                self.stage_merge(l, self.I['xin'] if l == 0 else self.X2d, None if l == 0 else self.X2b)
                last = (l == DEPTH - 1) or ('l0' in st)
                self.stage_mlp(l, self.O['y'] if last else self.X2d, None if last else self.X2b)
            if 'l0' in st:
                break
        if 'dbg' in st:
            self.dump_dbg()
        self.P.emit()
        return self.nc


_CACHE = {}


def kernel(**inputs):
    n_cores = 8
    n_pool = int(inputs['cache_k'].shape[1])
    key = ('all', n_pool)
    if key not in _CACHE:
        _CACHE[key] = Builder(stages=('all',), n_pool=n_pool).build()
    nc = _CACHE[key]
    consts = make_consts()
    f32 = lambda a: np.ascontiguousarray(np.asarray(a, dtype=np.float32))
    ck = f32(inputs['cache_k'])
    cv = f32(inputs['cache_v'])
    in_maps = []
    for c in range(n_cores):
        xin = np.zeros((NTOK, D), np.float32)
        xin[:2048] = inputs['x_prompt'][c]
        xin[2048:2064] = inputs['x_sample'][16 * c:16 * c + 16, 0]
        m = {'xin': xin}
        for n, s in WEIGHTS:
            m[n] = f32(inputs[n])
        m['state_gla'] = f32(inputs['state_gla'][:, 16 * c:16 * c + 16])
        m['state_wkv'] = f32(inputs['state_wkv'][:, 16 * c:16 * c + 16])
        m['state_shift'] = f32(inputs['state_shift'][:, 16 * c:16 * c + 16])
        m['cache_k'] = ck
        m['cache_v'] = cv
        m['page_table'] = np.ascontiguousarray(np.asarray(inputs['page_table'][16 * c:16 * c + 16], dtype=np.int32))
        for n, v in consts.items():
            m['c_' + n] = v
        in_maps.append(m)
    res = run_bass_kernel_spmd(nc, in_maps, core_ids=list(range(n_cores)))
    R = res.results
    y_p = np.stack([R[c]['y'][:2048] for c in range(n_cores)])
    y_s = np.concatenate([R[c]['y'][2048:2064] for c in range(n_cores)])[:, None, :]
    k_p = np.stack([R[c]['k_out'][:, :2048].reshape(DEPTH, 2048, 8, 64) for c in range(n_cores)], axis=1)
    v_p = np.stack([R[c]['v_out'][:, :2048].reshape(DEPTH, 2048, 8, 64) for c in range(n_cores)], axis=1)
    k_s = np.concatenate([R[c]['k_out'][:, 2048:2064].reshape(DEPTH, 16, 1, 8, 64) for c in range(n_cores)], axis=1)
    v_s = np.concatenate([R[c]['v_out'][:, 2048:2064].reshape(DEPTH, 16, 1, 8, 64) for c in range(n_cores)], axis=1)
    gla_p = np.stack([R[c]['gla_p'] for c in range(n_cores)], axis=1)
    gla_s = np.concatenate([R[c]['gla_s'] for c in range(n_cores)], axis=1)
    wkv_p = np.stack([R[c]['wkv_p'] for c in range(n_cores)], axis=1)
    wkv_s = np.concatenate([R[c]['wkv_s'] for c in range(n_cores)], axis=1)
    sh_p = np.stack([R[c]['shift'][:, 0] for c in range(n_cores)], axis=1)
    sh_s = np.concatenate([R[c]['shift'][:, 1:17] for c in range(n_cores)], axis=1)
    outs = (y_p, y_s, k_p, v_p, k_s, v_s, gla_p, gla_s, wkv_p, wkv_s, sh_p, sh_s)
    return tuple(np.ascontiguousarray(o, dtype=np.float32) for o in outs)
```
